# Optimizing a Trainium2 kernel written in Bass

```python
import math
import jax, jax.numpy as jnp
from jax import lax
import numpy as np

D_MODEL = 4096
BATCH = 4
SEQ = 4096
DEPTH = 1

HEAD_DIM = 128
SB_HEADS = 16
SB_WIDTH = SB_HEADS * HEAD_DIM
DA_HEADS = 8
DA_WIDTH = DA_HEADS * 2 * HEAD_DIM
GATE_WIDTH = 2 * D_MODEL
IN_WIDTH = 3 * SB_WIDTH + 3 * DA_WIDTH + GATE_WIDTH
Q_BLOCK = 128
ROPE_THETA = 10000.0
PEER_HEADS = 8
PEER_N_KEYS = 128
PEER_EXPERTS = PEER_N_KEYS * PEER_N_KEYS
PEER_TOPK = 16
PEER_KEY_DIM = 256
PEER_HALF = PEER_KEY_DIM // 2
PEER_CHUNK = 128
LN_EPS = 1e-5
DEEPNORM_ALPHA = (2 * DEPTH) ** 0.25
DEEPNORM_BETA = (8 * DEPTH) ** -0.25

kernel_name = "hybrid_sb_diffattn_peer_deepnorm"


def layer_norm(x, g, b):
    xf = x.astype(jnp.float32)
    mu = jnp.mean(xf, axis=-1, keepdims=True)
    var = jnp.mean(jnp.square(xf - mu), axis=-1, keepdims=True)
    return ((xf - mu) * lax.rsqrt(var + LN_EPS) * g + b).astype(x.dtype)


def rms_norm(x, g):
    xf = x.astype(jnp.float32)
    return (xf * lax.rsqrt(jnp.mean(jnp.square(xf), axis=-1, keepdims=True) + LN_EPS) * g).astype(x.dtype)


def rope_tables(positions):
    inv_freq = ROPE_THETA ** (-jnp.arange(0, HEAD_DIM, 2, dtype=jnp.float32) / HEAD_DIM)
    ang = positions.astype(jnp.float32)[..., None] * inv_freq
    return jnp.cos(ang), jnp.sin(ang)


def apply_rope(x, cos, sin):
    x1, x2 = jnp.split(x.astype(jnp.float32), 2, axis=-1)
    c = cos[:, :, None, :]
    s = sin[:, :, None, :]
    return jnp.concatenate([x1 * c - x2 * s, x1 * s + x2 * c], axis=-1).astype(x.dtype)


def to_query_blocks(x):
    b, h, s, d = x.shape
    return x.reshape(b, h, s // Q_BLOCK, Q_BLOCK, d).transpose(2, 0, 1, 3, 4)


def from_query_blocks(o):
    nb, b, h, q, d = o.shape
    return o.transpose(1, 0, 3, 2, 4).reshape(b, nb * q, h * d)


def stick_breaking_attention(q, k, v):
    s = q.shape[2]
    scale = HEAD_DIM ** -0.5
    kpos = jnp.arange(s)
    starts = jnp.arange(s // Q_BLOCK) * Q_BLOCK

    def block(args):
        qb, start = args
        z = jnp.einsum('bhqd,bhkd->bhqk', qb, k).astype(jnp.float32) * scale
        qpos = start + jnp.arange(Q_BLOCK)
        valid = kpos[None, :] < qpos[:, None]
        log_fail = jnp.where(valid, jax.nn.log_sigmoid(-z), 0.0)
        incl = lax.cumsum(log_fail, axis=3, reverse=True)
        excl = jnp.concatenate([incl[..., 1:], jnp.zeros_like(incl[..., :1])], axis=-1)
        w = jnp.where(valid, jnp.exp(jax.nn.log_sigmoid(z) + excl), 0.0)
        return jnp.einsum('bhqk,bhkd->bhqd', w.astype(v.dtype), v)

    return from_query_blocks(lax.map(block, (to_query_blocks(q), starts)))


def differential_attention(q1, q2, k1, k2, v, lam):
    s = q1.shape[2]
    scale = HEAD_DIM ** -0.5
    kpos = jnp.arange(s)
    starts = jnp.arange(s // Q_BLOCK) * Q_BLOCK

    def block(args):
        q1b, q2b, start = args
        qpos = start + jnp.arange(Q_BLOCK)
        causal = kpos[None, :] <= qpos[:, None]

        def probs(qb, kk):
            sc = jnp.einsum('bhqd,bhkd->bhqk', qb, kk).astype(jnp.float32) * scale
            return jax.nn.softmax(jnp.where(causal, sc, -jnp.inf), axis=-1)

        p = probs(q1b, k1) - lam * probs(q2b, k2)
        return jnp.einsum('bhqk,bhkd->bhqd', p.astype(v.dtype), v)

    return from_query_blocks(lax.map(block, (to_query_blocks(q1), to_query_blocks(q2), starts)))


def peer_layer(h, w_q, sub_keys, u_tab, v_tab):
    b, s, d = h.shape
    t = h.reshape(b * s, d)
    n_tok = b * s
    q = (t @ w_q).reshape(n_tok, PEER_HEADS, 2, PEER_HALF)
    scores = jnp.einsum('thcd,hckd->thck', q, sub_keys).astype(jnp.float32)
    s_top, i_top = lax.top_k(scores, PEER_TOPK)
    cand = (s_top[:, :, 0, :, None] + s_top[:, :, 1, None, :]).reshape(n_tok, PEER_HEADS, PEER_TOPK * PEER_TOPK)
    cand_idx = (i_top[:, :, 0, :, None] * PEER_N_KEYS + i_top[:, :, 1, None, :]).reshape(n_tok, PEER_HEADS, PEER_TOPK * PEER_TOPK)
    best, pos = lax.top_k(cand, PEER_TOPK)
    idx = jnp.take_along_axis(cand_idx, pos, axis=-1)
    gate = jax.nn.softmax(best, axis=-1).astype(h.dtype)
    n_sel = PEER_HEADS * PEER_TOPK
    n_chunks = n_tok // PEER_CHUNK

    def chunk(args):
        tc, ic, gc = args
        act = jax.nn.gelu(jnp.einsum('cd,ced->ce', tc, u_tab[ic]), approximate=False)
        return jnp.einsum('ce,ced->cd', gc * act, v_tab[ic])

    out = lax.map(chunk, (t.reshape(n_chunks, PEER_CHUNK, d),
                          idx.reshape(n_chunks, PEER_CHUNK, n_sel),
                          gate.reshape(n_chunks, PEER_CHUNK, n_sel)))
    return out.reshape(b, s, d)


def setup_inputs(seed: int = 0) -> dict:
    key = jax.random.key(seed)
    ks = jax.random.split(key, 24)
    f32 = jnp.float32
    nrm = lambda k, shape, sc: jax.random.normal(k, shape, f32) * sc
    x = jax.random.normal(ks[0], (BATCH, SEQ, D_MODEL), f32)
    offsets = jax.random.randint(ks[1], (BATCH, 1), 0, 1024, dtype=jnp.int32)
    positions = offsets + jnp.arange(SEQ, dtype=jnp.int32)[None, :]
    col_scale = jnp.concatenate([
        jnp.ones((2 * SB_WIDTH,), f32), jnp.full((SB_WIDTH,), DEEPNORM_BETA, f32),
        jnp.ones((2 * DA_WIDTH,), f32), jnp.full((DA_WIDTH,), DEEPNORM_BETA, f32),
        jnp.ones((GATE_WIDTH,), f32)])
    w_in = nrm(ks[2], (DEPTH, D_MODEL, IN_WIDTH), D_MODEL ** -0.5) * col_scale
    b_gate = nrm(ks[3], (DEPTH, GATE_WIDTH), 0.02)
    lambda_q1 = nrm(ks[4], (DEPTH, HEAD_DIM), 0.1)
    lambda_k1 = nrm(ks[5], (DEPTH, HEAD_DIM), 0.1)
    lambda_q2 = nrm(ks[6], (DEPTH, HEAD_DIM), 0.1)
    lambda_k2 = nrm(ks[7], (DEPTH, HEAD_DIM), 0.1)
    subln_g = 1.0 + nrm(ks[8], (DEPTH, 2 * HEAD_DIM), 0.02)
    w_sb_branch = nrm(ks[9], (DEPTH, SB_WIDTH, D_MODEL), SB_WIDTH ** -0.5 * DEEPNORM_BETA)
    w_da_branch = nrm(ks[10], (DEPTH, DA_WIDTH, D_MODEL), DA_WIDTH ** -0.5 * DEEPNORM_BETA)
    w_out = nrm(ks[11], (DEPTH, D_MODEL, D_MODEL), D_MODEL ** -0.5 * DEEPNORM_BETA)
    ln1_g = 1.0 + nrm(ks[12], (DEPTH, D_MODEL), 0.02)
    ln1_b = nrm(ks[13], (DEPTH, D_MODEL), 0.02)
    peer_w_q = nrm(ks[14], (DEPTH, D_MODEL, PEER_HEADS * PEER_KEY_DIM), D_MODEL ** -0.5)
    peer_sub_keys = nrm(ks[15], (DEPTH, PEER_HEADS, 2, PEER_N_KEYS, PEER_HALF), PEER_HALF ** -0.5)
    peer_u = nrm(ks[16], (DEPTH, PEER_EXPERTS, D_MODEL), D_MODEL ** -0.5)
    peer_v = nrm(ks[17], (DEPTH, PEER_EXPERTS, D_MODEL), (PEER_HEADS * PEER_TOPK) ** -0.5 * DEEPNORM_BETA)
    ln2_g = 1.0 + nrm(ks[18], (DEPTH, D_MODEL), 0.02)
    ln2_b = nrm(ks[19], (DEPTH, D_MODEL), 0.02)
    return {"x": x, "positions": positions, "w_in": w_in, "b_gate": b_gate,
            "lambda_q1": lambda_q1, "lambda_k1": lambda_k1, "lambda_q2": lambda_q2, "lambda_k2": lambda_k2,
            "subln_g": subln_g, "w_sb_branch": w_sb_branch, "w_da_branch": w_da_branch, "w_out": w_out,
            "ln1_g": ln1_g, "ln1_b": ln1_b, "peer_w_q": peer_w_q, "peer_sub_keys": peer_sub_keys,
            "peer_u": peer_u, "peer_v": peer_v, "ln2_g": ln2_g, "ln2_b": ln2_b}


def reference(x, positions, w_in, b_gate, lambda_q1, lambda_k1, lambda_q2, lambda_k2, subln_g,
              w_sb_branch, w_da_branch, w_out, ln1_g, ln1_b, peer_w_q, peer_sub_keys, peer_u, peer_v,
              ln2_g, ln2_b):
    b, s, _ = x.shape
    cos, sin = rope_tables(positions)
    splits = [SB_WIDTH, 2 * SB_WIDTH, 3 * SB_WIDTH,
              3 * SB_WIDTH + DA_WIDTH, 3 * SB_WIDTH + 2 * DA_WIDTH, 3 * SB_WIDTH + 3 * DA_WIDTH]
    h = x
    for l in range(DEPTH):
        lam_init = 0.8 - 0.6 * math.exp(-0.3 * l)
        proj = h @ w_in[l]
        sb_q, sb_k, sb_v, da_q, da_k, da_v, gate_pre = jnp.split(proj, splits, axis=-1)

        heads = lambda t: t.reshape(b, s, SB_HEADS, HEAD_DIM).transpose(0, 2, 1, 3)
        o_sb = stick_breaking_attention(heads(sb_q), heads(sb_k), heads(sb_v))

        def qk_pair(t):
            t = apply_rope(t.reshape(b, s, 2 * DA_HEADS, HEAD_DIM), cos, sin)
            t = t.reshape(b, s, DA_HEADS, 2, HEAD_DIM).transpose(0, 2, 3, 1, 4)
            return t[:, :, 0], t[:, :, 1]
        q1, q2 = qk_pair(da_q)
        k1, k2 = qk_pair(da_k)
        v_da = da_v.reshape(b, s, DA_HEADS, 2 * HEAD_DIM).transpose(0, 2, 1, 3)
        lam = (jnp.exp(jnp.sum(lambda_q1[l].astype(jnp.float32) * lambda_k1[l].astype(jnp.float32)))
               - jnp.exp(jnp.sum(lambda_q2[l].astype(jnp.float32) * lambda_k2[l].astype(jnp.float32)))
               + lam_init)
        o_da = differential_attention(q1, q2, k1, k2, v_da, lam)
        o_da = (rms_norm(o_da.reshape(b, s, DA_HEADS, 2 * HEAD_DIM), subln_g[l]) * (1.0 - lam_init)).reshape(b, s, DA_WIDTH)

        g_sb, g_da = jnp.split(jax.nn.sigmoid(gate_pre + b_gate[l]), 2, axis=-1)
        merged = g_sb * (o_sb @ w_sb_branch[l]) + g_da * (o_da @ w_da_branch[l])
        mix = merged @ w_out[l]
        h = layer_norm(DEEPNORM_ALPHA * h + mix, ln1_g[l], ln1_b[l])

        ffn = peer_layer(h, peer_w_q[l], peer_sub_keys[l], peer_u[l], peer_v[l])
        h = layer_norm(DEEPNORM_ALPHA * h + ffn, ln2_g[l], ln2_b[l])
    return h
```

```python
import contextlib
import math
import numpy as np
import ml_dtypes
import concourse.bass as bass
import concourse.mybir as mybir
from concourse.bass_utils import run_bass_kernel_spmd

F32 = mybir.dt.float32
BF16 = mybir.dt.bfloat16
I32 = mybir.dt.int32
U32 = mybir.dt.uint32
AF = mybir.ActivationFunctionType
ALU = mybir.AluOpType
AX = mybir.AxisListType

SEM_ROT = 30000
LN_EPS = 1e-5
NKEYS = 128
TOPK = 16
NEXP = NKEYS * NKEYS
ROPE_THETA = 10000.0

FULL_CFG = dict(D=4096, S=4096, NB=4, SBH=16, DAH=8, PH=8, DEPTH=1)


class Buf:
    __slots__ = ("name", "w", "r", "ds")

    def __init__(self, name=""):
        self.name = name
        self.w = None
        self.r = {}
        self.ds = None


class DSem:
    def __init__(self, handle):
        self.h = handle
        self.total = 0
        self.bg = False


class Sched:
    ENGS = ("pe", "act", "dve", "pool", "sp")

    def __init__(self, nc, stack):
        self.nc = nc
        self.stack = stack
        self.streams = {e: [] for e in self.ENGS}
        self.esem = {}
        self.ecount = {}
        self.known = {e: {} for e in self.ENGS}
        self.nsem = 0
        self.dsems = []
        self.free_ds = []
        for e in ("pe", "act", "dve", "pool"):
            self._new_esem(e)

    def _alloc(self, name):
        self.nsem += 1
        return self.stack.enter_context(self.nc.semaphore(f"{name}_{self.nsem}"))

    def _new_esem(self, e):
        self.esem[e] = self._alloc(f"es_{e}")
        self.ecount[e] = 0

    def dsem(self, name="d"):
        if self.free_ds:
            return self.free_ds.pop()
        d = DSem(self._alloc(f"ds_{name}"))
        self.dsems.append(d)
        return d

    def release(self, bufs):
        for b in bufs:
            if b.ds is not None:
                self.free_ds.append(b.ds)
                b.ds = None

    def _collect(self, eng, reads, writes, dma=False):
        need = {}

        def add(tok, kind):
            if tok is None:
                return
            semh, val, teng, ds = tok
            if teng == eng and teng is not None and not dma:
                if eng == "pe":
                    return
                if kind == "war":
                    return
            if ds is not None:
                val = ds.total
            k = id(semh)
            if k not in need or need[k][1] < val:
                need[k] = (semh, val)

        for b in reads:
            add(b.w, "raw")
        for b in writes:
            add(b.w, "waw")
            for t in b.r.values():
                add(t, "war")
        return self._filter(eng, need.values())

    def _filter(self, eng, pairs):
        waits = []
        kn = self.known[eng]
        for semh, val in pairs:
            k = id(semh)
            if kn.get(k, 0) >= val:
                continue
            kn[k] = val
            waits.append((semh, val))
        return waits

    def _commit(self, tok, reads, writes):
        k = id(tok[0])
        for b in reads:
            b.r[k] = tok
        for b in writes:
            b.w = tok
            b.r = {}

    def op(self, eng, fn, reads=(), writes=()):
        waits = self._collect(eng, reads, writes)
        if self.ecount[eng] >= SEM_ROT:
            self._new_esem(eng)
        self.ecount[eng] += 1
        semh = self.esem[eng]
        tok = (semh, self.ecount[eng], eng, None)
        self.streams[eng].append((waits, fn, (semh, 1)))
        self._commit(tok, reads, writes)
        return tok

    def dma(self, q, fn, owner, reads=(), writes=()):
        if owner.ds is None:
            owner.ds = self.dsem(owner.name)
        ds = owner.ds
        waits = self._collect(q, reads, writes, dma=True)
        ds.total += 16
        tok = (ds.h, ds.total, None, ds)
        self.streams[q].append((waits, fn, (ds.h, 16)))
        self._commit(tok, reads, writes)
        return tok

    def barrier(self, include_bg=False):
        pairs = [(self.esem[e], self.ecount[e]) for e in ("pe", "act", "dve", "pool") if self.ecount[e] > 0]
        pairs += [(d.h, d.total) for d in self.dsems if d.total > 0 and (include_bg or not d.bg)]
        for e in self.ENGS:
            w = self._filter(e, pairs)
            if w:
                self.streams[e].append((w, None, None))

    def emit(self):
        nc = self.nc
        streams = self.streams

        def run(engh, items):
            for waits, fn, inc in items:
                for semh, val in waits:
                    engh.wait_ge(semh, val)
                if fn is None:
                    continue
                ins = fn(engh)
                if inc is not None:
                    ins.then_inc(inc[0], inc[1])

        with nc.Block() as block:
            @block.tensor
            def _(e):
                run(e, streams["pe"])

            @block.scalar
            def _(e):
                run(e, streams["act"])

            @block.vector
            def _(e):
                run(e, streams["dve"])

            @block.gpsimd
            def _(e):
                run(e, streams["pool"])

            @block.sync
            def _(e):
                run(e, streams["sp"])


def build(cfg):
    D, S, SBH, DAH, PH = cfg["D"], cfg["S"], cfg["SBH"], cfg["DAH"], cfg["PH"]
    DEPTH = cfg["DEPTH"]
    assert DEPTH == 1
    KC = D // 128
    SO = S // 2
    SBW = SBH * 128
    DAW = DAH * 256
    INW = 3 * SBW + 3 * DAW + 2 * D
    c_sbq, c_sbk, c_sbv = 0, SBW, 2 * SBW
    c_daq, c_dak, c_dav = 3 * SBW, 3 * SBW + DAW, 3 * SBW + 2 * DAW
    c_g = 3 * SBW + 3 * DAW
    NQ = SO // 512
    HC = 2 * PH
    NSEL = PH * TOPK
    QW = PH * 256
    alpha = (2 * DEPTH) ** 0.25
    lam_init = 0.8 - 0.6 * math.exp(-0.3 * 0)
    scale = 128 ** -0.5
    PI = math.pi

    nc = bass.Bass("TRN2", target_bir_lowering=False)

    def din(name, shape, dt=F32):
        return nc.dram_tensor(name, list(shape), dt, kind="ExternalInput").ap()

    xT_all = din("xT_all", [D, S])
    xT_own = din("xT_own", [D, SO])
    x_own = din("x_own", [SO, D])
    pos_all = din("pos_all", [1, S], I32)
    pos_own = din("pos_own", [1, SO], I32)
    w_in = din("w_in", [D, INW])
    bgT = din("bgT", [128, 2 * KC])
    lamv = din("lamv", [4, 128])
    sublnT = din("sublnT", [128, 2])
    w_sbb = din("w_sbb", [SBW, D])
    w_dab = din("w_dab", [DAW, D])
    w_out = din("w_out", [D, D])
    ln1g = din("ln1g", [1, D]); ln1b = din("ln1b", [1, D])
    ln2g = din("ln2g", [1, D]); ln2b = din("ln2b", [1, D])
    w_q = din("w_q", [D, QW])
    skT = din("skT", [HC, 128, NKEYS])
    pu = din("pu", [NEXP, D])
    pv = din("pv", [NEXP, D])
    c_ident = din("c_ident", [128, 128])
    c_tri = din("c_tri", [128, 128])
    c_ones = din("c_ones", [128, 128])
    c_perm = din("c_perm", [128, 128])
    c_rope = din("c_rope", [128, 3])
    c_maskS = din("c_maskS", [128, 8 * 512], BF16)
    c_maskD = din("c_maskD", [128, 8 * 512], BF16)
    out_own = nc.dram_tensor("out_own", [SO, D], F32, kind="ExternalOutput").ap()

    def scr(name, shape, dt):
        return nc.dram_tensor(name, list(shape), dt).ap()

    KsbT = scr("KsbT", [SBH, 128, S], BF16)
    Vsb = scr("Vsb", [S, SBW], BF16)
    KdaT = scr("KdaT", [2 * DAH, 128, S], BF16)
    Vda = scr("Vda", [S, DAW], BF16)
    QsbT = scr("QsbT", [SBH, 128, SO], BF16)
    QdaT = scr("QdaT", [2 * DAH, 128, SO], BF16)
    OsbT = scr("OsbT", [SBH, 128, SO], BF16)
    OdaT = scr("OdaT", [2 * DAH, 128, SO], BF16)
    mergedT = scr("mergedT", [KC, 128, SO], BF16)
    h1_scr = scr("h1_scr", [SO, D], F32)
    h1T_scr = scr("h1T_scr", [KC, 128, SO], BF16)
    sc_scr = scr("sc_scr", [SO, 2 * PH * NKEYS], F32)
    w_in_b = scr("w_in_b", [D, INW], BF16)
    w_sbb_b = scr("w_sbb_b", [SBW, D], BF16)
    w_dab_b = scr("w_dab_b", [DAW, D], BF16)
    w_out_b = scr("w_out_b", [D, D], BF16)
    w_q_b = scr("w_q_b", [D, QW], BF16)
    puv_b = scr("puv_b", [NEXP, 2 * D], BF16)

    with contextlib.ExitStack() as st:
        S_ = Sched(nc, st)
        op = S_.op
        dma = S_.dma

        class T:
            def __init__(self, stack, name, shape, dt, psum=False):
                if psum:
                    self.t = stack.enter_context(nc.psum_tensor(name, list(shape), dt))
                else:
                    self.t = stack.enter_context(nc.sbuf_tensor(name, list(shape), dt))
                self.b = Buf(name)

            def __getitem__(self, k):
                return self.t[k]

        PS = [T(st, f"ps{i}", [128, 512], F32, psum=True) for i in range(8)]
        ident = T(st, "ident", [128, 128], F32)
        tri = T(st, "tri", [128, 128], F32)
        ones = T(st, "ones", [128, 128], F32)
        onesb = T(st, "onesb", [128, 128], BF16)
        trib = T(st, "trib", [128, 128], BF16)
        perm = T(st, "perm", [128, 128], F32)
        rope_c = T(st, "rope_c", [128, 3], F32)
        epsc = T(st, "epsc", [128, 1], F32)
        lam = T(st, "lam", [128, 1], F32)

        dma("sp", lambda e: e.dma_start(out=ident[:], in_=c_ident), ident.b, writes=[ident.b])
        dma("sp", lambda e: e.dma_start(out=tri[:], in_=c_tri), tri.b, writes=[tri.b])
        dma("sp", lambda e: e.dma_start(out=ones[:], in_=c_ones), ones.b, writes=[ones.b])
        dma("pool", lambda e: e.dma_start(out=onesb[:], in_=c_ones), onesb.b, writes=[onesb.b])
        dma("pool", lambda e: e.dma_start(out=trib[:], in_=c_tri), trib.b, writes=[trib.b])
        dma("sp", lambda e: e.dma_start(out=perm[:], in_=c_perm), perm.b, writes=[perm.b])
        dma("sp", lambda e: e.dma_start(out=rope_c[:], in_=c_rope), rope_c.b, writes=[rope_c.b])
        op("dve", lambda e: e.memset(epsc[:], LN_EPS), writes=[epsc.b])

        with contextlib.ExitStack() as ph:
            lv = T(ph, "lv", [128, 4, 128], F32)
            lp = T(ph, "lp", [128, 2, 128], F32)
            ls = T(ph, "ls", [128, 2], F32)
            le = T(ph, "le", [128, 2], F32)
            for r in range(4):
                dma("sp", lambda e, r=r: e.dma_start(out=lv[:, r, :], in_=lamv[r:r + 1, :].partition_broadcast(128)),
                    lv.b, writes=[lv.b])
            op("dve", lambda e: e.tensor_tensor(out=lp[:, 0, :], in0=lv[:, 0, :], in1=lv[:, 1, :], op=ALU.mult),
               reads=[lv.b], writes=[lp.b])
            op("dve", lambda e: e.tensor_tensor(out=lp[:, 1, :], in0=lv[:, 2, :], in1=lv[:, 3, :], op=ALU.mult),
               reads=[lv.b], writes=[lp.b])
            op("dve", lambda e: e.tensor_reduce(out=ls[:], in_=lp[:], axis=AX.X, op=ALU.add), reads=[lp.b], writes=[ls.b])
            op("act", lambda e: e.activation(le[:], ls[:], AF.Exp), reads=[ls.b], writes=[le.b])
            op("dve", lambda e: e.tensor_tensor(out=lam[:], in0=le[:, 0:1], in1=le[:, 1:2], op=ALU.subtract),
               reads=[le.b], writes=[lam.b])
            op("dve", lambda e: e.tensor_scalar(out=lam[:], in0=lam[:], scalar1=float(lam_init), scalar2=None, op0=ALU.add),
               reads=[lam.b], writes=[lam.b])
            S_.barrier()
            S_.release([lv.b])

        CB = {}

        pending = []

        def precast(key, dst, src, r_tot, c0, c1, rblk=1024, dc0=None, defer=False):
            if key not in CB:
                CB[key] = Buf("cb_" + key)
                CB[key].ds = S_.dsem("cb_" + key)
                CB[key].ds.bg = True
            b = CB[key]
            if dc0 is None:
                dc0 = c0
            for r in range(0, r_tot, rblk):
                r1 = min(r_tot, r + rblk)
                th = (lambda r=r, r1=r1: dma("pool", lambda e: e.dma_start(out=dst[r:r1, dc0:dc0 + (c1 - c0)], in_=src[r:r1, c0:c1]), b, writes=[b]))
                th.key = key
                if defer:
                    pending.append(th)
                else:
                    th()

        def flush_pending(k=None):
            n = len(pending) if k is None else min(k, len(pending))
            for _ in range(n):
                pending.pop(0)()

        def early_precast():
            precast("sbkv", w_in_b, w_in, D, c_sbk, c_sbv + SBW)
            precast("dakv", w_in_b, w_in, D, c_dak, c_dav + DAW)
            precast("q", w_in_b, w_in, D, c_sbq, c_sbq + SBW)
            precast("q", w_in_b, w_in, D, c_daq, c_daq + DAW)
        did_early = [False]
        precast("sbb", w_sbb_b, w_sbb, SBW, 0, D, defer=True)
        precast("dab", w_dab_b, w_dab, DAW, 0, D, defer=True)
        precast("out", w_out_b, w_out, D, 0, D, defer=True)
        precast("wq", w_q_b, w_q, D, 0, QW, defer=True)
        n_w_pending = len(pending)
        precast("puv", puv_b, pu, NEXP, 0, D, dc0=0, defer=True)
        precast("puv", puv_b, pv, NEXP, 0, D, dc0=D, defer=True)
        pend_w = pending[:n_w_pending]
        pend_t = pending[n_w_pending:]
        del pending[:]

        with contextlib.ExitStack() as ph:
            xT = T(ph, "xT", [128, KC, 1024], BF16)
            posi = T(ph, "posi", [128, 1024], I32)
            ang = T(ph, "ang", [128, 1024], F32)
            m1 = T(ph, "m1", [128, 1024], F32)
            ki = T(ph, "ki", [128, 1024], I32)
            kf = T(ph, "kf", [128, 1024], F32)
            CT = T(ph, "CT", [128, 1024], F32)
            ST = T(ph, "ST", [128, 1024], F32)
            wsl = [T(ph, f"wA{i}", [128, KC, 512], BF16) for i in range(2)]
            stg = [T(ph, f"stA{i}", [128, 512], BF16) for i in range(4)]
            qf = [T(ph, f"qf{i}", [128, 512], F32) for i in range(2)]
            t1 = [T(ph, f"t1{i}", [128, 512], F32) for i in range(2)]
            t2 = [T(ph, f"t2{i}", [128, 512], F32) for i in range(2)]
            wcnt = [0]
            scnt = [0]
            pcnt = [0]
            rcnt = [0]

            def phaseA_pass(xsrc, possrc, ntiles, groups):
                for tile in range(ntiles):
                    t0 = tile * 1024
                    xv = xsrc.rearrange("(kc p) t -> p kc t", p=128)
                    for k0 in range(0, KC, 4):
                        dma("pool", lambda e, k0=k0, t0=t0, xv=xv: e.dma_start(out=xT[:, k0:k0 + 4, :], in_=xv[:, k0:k0 + 4, t0:t0 + 1024]),
                            xT.b, writes=[xT.b])
                    if not did_early[0]:
                        did_early[0] = True
                        early_precast()
                    dma("sp", lambda e, t0=t0, possrc=possrc: e.dma_start(out=posi[:], in_=possrc[0:1, t0:t0 + 1024].partition_broadcast(128)),
                        posi.b, writes=[posi.b])
                    op("dve", lambda e: e.tensor_copy(ang[:], posi[:]), reads=[posi.b], writes=[ang.b])
                    op("dve", lambda e: e.tensor_scalar(out=ang[:], in0=ang[:], scalar1=rope_c[:, 0:1], scalar2=None, op0=ALU.mult),
                       reads=[ang.b, rope_c.b], writes=[ang.b])
                    def range_red(add):
                        op("dve", lambda e: e.tensor_scalar(out=m1[:], in0=ang[:], scalar1=float(1.0 / (2 * PI)), scalar2=float(add),
                                                            op0=ALU.mult, op1=ALU.add), reads=[ang.b], writes=[m1.b])
                        op("dve", lambda e: e.tensor_copy(ki[:], m1[:]), reads=[m1.b], writes=[ki.b])
                        op("dve", lambda e: e.tensor_copy(kf[:], ki[:]), reads=[ki.b], writes=[kf.b])
                        op("dve", lambda e: e.tensor_tensor(out=m1[:], in0=m1[:], in1=kf[:], op=ALU.subtract), reads=[m1.b, kf.b], writes=[m1.b])
                        op("dve", lambda e: e.tensor_scalar(out=kf[:], in0=m1[:], scalar1=0.5, scalar2=None, op0=ALU.is_gt),
                           reads=[m1.b], writes=[kf.b])
                        op("dve", lambda e: e.tensor_tensor(out=m1[:], in0=m1[:], in1=kf[:], op=ALU.subtract), reads=[m1.b, kf.b], writes=[m1.b])
                    range_red(0.0)
                    op("act", lambda e: e.activation(ST[:], m1[:], AF.Sin, scale=rope_c[:, 1:2]),
                       reads=[m1.b, rope_c.b], writes=[ST.b])
                    range_red(0.25)
                    op("act", lambda e: e.activation(CT[:], m1[:], AF.Sin, scale=float(2 * PI)), reads=[m1.b], writes=[CT.b])

                    for (c0, kind, dst, hbase, rope, sc, ck) in groups:
                        if did_early[0] and wcnt[0] >= 30 and wcnt[0] % 2 == 0 and pend_w:
                            pend_w.pop(0)()
                        for th in [t_ for t_ in pend_w if t_.key == ck]:
                            pend_w.remove(th)
                            th()
                        w = wsl[wcnt[0] % 2]
                        wcnt[0] += 1
                        wv = w_in_b.rearrange("(kc p) c -> p kc c", p=128)
                        for k0 in range(0, KC, 8):
                            k1_ = min(KC, k0 + 8)
                            dma("sp", lambda e, w=w, k0=k0, k1_=k1_, c0=c0, wv=wv: e.dma_start(out=w[:, k0:k1_, :], in_=wv[:, k0:k1_, c0:c0 + 512]),
                                w.b, reads=[CB[ck]], writes=[w.b])
                        if kind == "fm":
                            for cb in range(4):
                                head = hbase + cb
                                for ts in range(2):
                                    p = PS[pcnt[0] % 4]
                                    pcnt[0] += 1
                                    for kc in range(KC):
                                        op("pe", lambda e, p=p, w=w, kc=kc, cb=cb, ts=ts: e.matmul(
                                            p[:], w[:, kc, cb * 128:(cb + 1) * 128], xT[:, kc, ts * 512:(ts + 1) * 512],
                                            start=(kc == 0), stop=(kc == KC - 1)),
                                           reads=[w.b, xT.b], writes=[p.b])
                                    sg = stg[scnt[0] % 4]
                                    scnt[0] += 1
                                    if not rope:
                                        op("act", lambda e, sg=sg, p=p, sc=sc: e.activation(sg[:], p[:], AF.Copy, scale=float(sc)),
                                           reads=[p.b], writes=[sg.b])
                                    else:
                                        r = rcnt[0] % 2
                                        rcnt[0] += 1
                                        p2 = PS[4 + r]
                                        op("act", lambda e, r=r, p=p, sc=sc: e.activation(qf[r][:], p[:], AF.Copy, scale=float(sc)),
                                           reads=[p.b], writes=[qf[r].b])
                                        op("pe", lambda e, r=r, p2=p2: e.matmul(p2[:], perm[:], qf[r][:], start=True, stop=True),
                                           reads=[perm.b, qf[r].b], writes=[p2.b])
                                        op("dve", lambda e, r=r, ts=ts: e.tensor_tensor(out=t1[r][:], in0=qf[r][:], in1=CT[:, ts * 512:(ts + 1) * 512], op=ALU.mult),
                                           reads=[qf[r].b, CT.b], writes=[t1[r].b])
                                        op("dve", lambda e, r=r, ts=ts, p2=p2: e.tensor_tensor(out=t2[r][:], in0=p2[:], in1=ST[:, ts * 512:(ts + 1) * 512], op=ALU.mult),
                                           reads=[p2.b, ST.b], writes=[t2[r].b])
                                        op("dve", lambda e, r=r, sg=sg: e.tensor_tensor(out=sg[:], in0=t1[r][:], in1=t2[r][:], op=ALU.add),
                                           reads=[t1[r].b, t2[r].b], writes=[sg.b])
                                    dma("pool", lambda e, sg=sg, dst=dst, head=head, t0=t0, ts=ts: e.dma_start(
                                        out=dst[head, :, t0 + ts * 512:t0 + (ts + 1) * 512], in_=sg[:]),
                                        sg.b, reads=[sg.b])
                        else:
                            for tc in range(8):
                                p = PS[pcnt[0] % 4]
                                pcnt[0] += 1
                                for kc in range(KC):
                                    op("pe", lambda e, p=p, w=w, kc=kc, tc=tc: e.matmul(
                                        p[:], xT[:, kc, tc * 128:(tc + 1) * 128], w[:, kc, :],
                                        start=(kc == 0), stop=(kc == KC - 1)),
                                       reads=[w.b, xT.b], writes=[p.b])
                                sg = stg[scnt[0] % 4]
                                scnt[0] += 1
                                if tc % 2 == 0:
                                    op("act", lambda e, sg=sg, p=p: e.copy(sg[:], p[:]), reads=[p.b], writes=[sg.b])
                                else:
                                    op("dve", lambda e, sg=sg, p=p: e.tensor_copy(sg[:], p[:]), reads=[p.b], writes=[sg.b])
                                dma("pool", lambda e, sg=sg, dst=dst, hbase=hbase, t0=t0, tc=tc: e.dma_start(
                                    out=dst[t0 + tc * 128:t0 + (tc + 1) * 128, hbase:hbase + 512], in_=sg[:]),
                                    sg.b, reads=[sg.b])

            kv_groups = []
            for g in range(SBW // 512):
                kv_groups.append((c_sbk + g * 512, "fm", KsbT, g * 4, False, 1.0, "sbkv"))
            for g in range(SBW // 512):
                kv_groups.append((c_sbv + g * 512, "tm", Vsb, g * 512, False, 1.0, "sbkv"))
            for g in range(DAW // 512):
                kv_groups.append((c_dak + g * 512, "fm", KdaT, g * 4, True, 1.0, "dakv"))
            for g in range(DAW // 512):
                kv_groups.append((c_dav + g * 512, "tm", Vda, g * 512, False, 1.0, "dakv"))
            q_groups = []
            for g in range(SBW // 512):
                q_groups.append((c_sbq + g * 512, "fm", QsbT, g * 4, False, scale, "q"))
            for g in range(DAW // 512):
                q_groups.append((c_daq + g * 512, "fm", QdaT, g * 4, True, scale, "q"))
            phaseA_pass(xT_all, pos_all, S // 1024, kv_groups)
            phaseA_pass(xT_own, pos_own, SO // 1024, q_groups)
            while pend_w:
                pend_w.pop(0)()
            S_.barrier()
            S_.release([xT.b, posi.b] + [t.b for t in wsl] + [t.b for t in stg])

        NKB = S // 128
        with contextlib.ExitStack() as ph:
            maskS = T(ph, "maskS", [128, 8 * 512], BF16)
            dma("sp", lambda e: e.dma_start(out=maskS[:], in_=c_maskS), maskS.b, writes=[maskS.b])
            kT = [T(ph, f"kT{i}", [128, S], BF16) for i in range(2)]
            vv = [T(ph, f"vv{i}", [128, NKB, 128], BF16) for i in range(2)]
            qq = [T(ph, f"qq{i}", [128, SO], BF16) for i in range(2)]
            NE, NSP, NL, NLW, NW = 3, 6, 4, 3, 4
            e_sb = [T(ph, f"e_sb{i}", [128, 512], F32) for i in range(NE)]
            sp_sb = [T(ph, f"sp_sb{i}", [128, 512], F32) for i in range(NSP)]
            L_sb = [T(ph, f"L_sb{i}", [128, 512], BF16) for i in range(NL)]
            lw_sb = [T(ph, f"lw_sb{i}", [128, 512], F32) for i in range(NLW)]
            w_sb = [T(ph, f"w_sb{i}", [128, 512], BF16) for i in range(NW)]
            Lsum = [T(ph, f"Lsum{i}", [128, 512], BF16) for i in range(2)]
            ostg = [T(ph, f"ostg{i}", [128, 512], BF16) for i in range(2)]
            chains = []
            gi = 0
            for h in range(SBH):
                for i in range(NQ):
                    nkb = 8 * (i + 1)
                    for n, kb in enumerate(range(nkb - 1, -1, -1)):
                        chains.append(dict(h=h, i=i, kb=kb, first=(n == 0), last=(n == nkb - 1), g=gi,
                                           masked=(kb >= 8 * i), r=kb - 8 * i, newhead=(i == 0 and n == 0)))
                    gi += 1
            NCH_B = len(chains)
            ZB = lambda c: PS[c % 4]
            XB = lambda c: PS[4 + c % 2]
            OB = lambda g: PS[6 + g % 2]

            def load_head(h):
                sl = h % 2
                dma("sp", lambda e: e.dma_start(out=kT[sl][:], in_=KsbT[h, :, :]), kT[sl].b, writes=[kT[sl].b])
                vsrc = Vsb.rearrange("(kb s) c -> s kb c", s=128)
                dma("sp", lambda e: e.dma_start(out=vv[sl][:], in_=vsrc[:, :, h * 128:(h + 1) * 128]), vv[sl].b, writes=[vv[sl].b])
                dma("sp", lambda e: e.dma_start(out=qq[sl][:], in_=QsbT[h, :, :]), qq[sl].b, writes=[qq[sl].b])

            def stage(st_, c):
                ch = chains[c]
                h, i, kb, sl = ch["h"], ch["i"], ch["kb"], ch["h"] % 2
                zp, xp, ob = ZB(c), XB(c), OB(ch["g"])
                E, SPt, Lt, LW, W = e_sb[c % NE], sp_sb[c % NSP], L_sb[c % NL], lw_sb[c % NLW], w_sb[c % NW]
                first, last, masked, r = ch["first"], ch["last"], ch["masked"], ch["r"]
                if st_ == 0:
                    if ch["newhead"] and h == 0:
                        load_head(0)
                    op("pe", lambda e: e.matmul(zp[:], kT[sl][:, kb * 128:(kb + 1) * 128], qq[sl][:, i * 512:(i + 1) * 512], start=True, stop=True),
                       reads=[kT[sl].b, qq[sl].b], writes=[zp.b])
                elif st_ == 1:
                    op("act", lambda e: e.activation(E[:], zp[:], AF.Exp, scale=-1.0), reads=[zp.b], writes=[E.b])
                elif st_ == 2:
                    op("act", lambda e: e.activation(SPt[:], E[:], AF.Ln, bias=1.0), reads=[E.b], writes=[SPt.b])
                elif st_ == 3:
                    op("dve", lambda e: e.scalar_tensor_tensor(out=Lt[:], in0=zp[:], scalar=-1.0, in1=SPt[:], op0=ALU.mult, op1=ALU.subtract),
                       reads=[zp.b, SPt.b], writes=[Lt.b])
                    if masked:
                        op("dve", lambda e: e.tensor_tensor(out=Lt[:], in0=Lt[:], in1=maskS[:, r * 512:(r + 1) * 512], op=ALU.mult),
                           reads=[Lt.b, maskS.b], writes=[Lt.b])
                elif st_ == 4:
                    op("pe", lambda e: e.matmul(xp[:], trib[:], Lt[:], start=True, stop=first), reads=[trib.b, Lt.b], writes=[xp.b])
                    if not first:
                        lsp = Lsum[(c - 1) % 2]
                        op("pe", lambda e: e.matmul(xp[:], onesb[:], lsp[:], start=False, stop=True), reads=[onesb.b, lsp.b], writes=[xp.b])
                    if not last:
                        lsn = Lsum[c % 2]
                        if first:
                            op("pool", lambda e: e.tensor_copy(lsn[:], Lt[:]), reads=[Lt.b], writes=[lsn.b])
                        else:
                            lsp = Lsum[(c - 1) % 2]
                            op("pool", lambda e: e.tensor_tensor(out=lsn[:], in0=lsp[:], in1=Lt[:], op=ALU.add),
                               reads=[Lt.b, lsp.b], writes=[lsn.b])
                elif st_ == 5:
                    op("dve", lambda e: e.tensor_tensor(out=LW[:], in0=xp[:], in1=SPt[:], op=ALU.subtract), reads=[xp.b, SPt.b], writes=[LW.b])
                elif st_ == 6:
                    op("act", lambda e: e.activation(W[:], LW[:], AF.Exp), reads=[LW.b], writes=[W.b])
                    if masked:
                        op("pool", lambda e: e.tensor_tensor(out=W[:], in0=W[:], in1=maskS[:, r * 512:(r + 1) * 512], op=ALU.mult),
                           reads=[W.b, maskS.b], writes=[W.b])
                elif st_ == 7:
                    op("pe", lambda e: e.matmul(ob[:], vv[sl][:, kb, :], W[:], start=first, stop=last), reads=[vv[sl].b, W.b], writes=[ob.b])
                elif st_ == 8:
                    if ch["newhead"] and h + 1 < SBH:
                        load_head(h + 1)
                    if last:
                        og = ostg[ch["g"] % 2]
                        op("act", lambda e: e.copy(og[:], ob[:]), reads=[ob.b], writes=[og.b])
                        dma("sp", lambda e: e.dma_start(out=OsbT[h, :, i * 512:(i + 1) * 512], in_=og[:]), og.b, reads=[og.b])

            NST = 9
            for t in range(NCH_B + NST - 1):
                for st_ in (0, 4, 7, 1, 2, 6, 8, 3, 5):
                    c = t - st_
                    if 0 <= c < NCH_B:
                        stage(st_, c)
            S_.barrier()
            S_.release([maskS.b] + [t.b for t in kT + vv + qq + ostg])

        with contextlib.ExitStack() as ph:
            maskD = T(ph, "maskD", [128, 8 * 512], BF16)
            dma("sp", lambda e: e.dma_start(out=maskD[:], in_=c_maskD), maskD.b, writes=[maskD.b])
            gsc = T(ph, "gsc", [128, 2], F32)
            dma("sp", lambda e: e.dma_start(out=gsc[:], in_=sublnT), gsc.b, writes=[gsc.b])
            op("dve", lambda e: e.tensor_scalar(out=gsc[:], in0=gsc[:], scalar1=float(1.0 - lam_init), scalar2=None, op0=ALU.mult),
               reads=[gsc.b], writes=[gsc.b])
            k1 = [T(ph, f"k1{i}", [128, S], BF16) for i in range(2)]
            k2 = [T(ph, f"k2{i}", [128, S], BF16) for i in range(2)]
            vd = [T(ph, f"vd{i}", [128, NKB, 256], BF16) for i in range(2)]
            q1 = [T(ph, f"q1{i}", [128, SO], BF16) for i in range(2)]
            q2 = [T(ph, f"q2{i}", [128, SO], BF16) for i in range(2)]
            E1 = [T(ph, f"E1{i}", [128, 512], BF16) for i in range(2)]
            E2 = [T(ph, f"E2{i}", [128, 512], BF16) for i in range(2)]
            rz1 = T(ph, "rz1", [128, 512], F32)
            rz2 = T(ph, "rz2", [128, 512], F32)
            ta = T(ph, "ta", [128, 512], F32)
            tb = T(ph, "tb", [128, 512], F32)
            od = [T(ph, f"od{i}", [128, 512], F32) for i in range(2)]
            sq = [T(ph, f"sq{i}", [128, 512], F32) for i in range(2)]
            sd = T(ph, "sd", [128, 512], F32)
            rstd = T(ph, "rstd", [128, 512], F32)
            ystg = [T(ph, f"ystg{i}", [128, 512], BF16) for i in range(2)]
            Z1, Z2 = PS[0], PS[1]
            O1a, O1b, Z1s, O2a, O2b, Z2s = PS[2], PS[3], PS[4], PS[5], PS[6], PS[7]
            blk = 0
            def load_da_head(h):
                sl = h % 2
                dma("sp", lambda e: e.dma_start(out=k1[sl][:], in_=KdaT[2 * h, :, :]), k1[sl].b, writes=[k1[sl].b])
                dma("sp", lambda e: e.dma_start(out=k2[sl][:], in_=KdaT[2 * h + 1, :, :]), k2[sl].b, writes=[k2[sl].b])
                vsrc = Vda.rearrange("(kb s) c -> s kb c", s=128)
                dma("sp", lambda e: e.dma_start(out=vd[sl][:], in_=vsrc[:, :, h * 256:(h + 1) * 256]), vd[sl].b, writes=[vd[sl].b])
                dma("sp", lambda e: e.dma_start(out=q1[sl][:], in_=QdaT[2 * h, :, :]), q1[sl].b, writes=[q1[sl].b])
                dma("sp", lambda e: e.dma_start(out=q2[sl][:], in_=QdaT[2 * h + 1, :, :]), q2[sl].b, writes=[q2[sl].b])

            for h in range(DAH):
                sl = h % 2
                if h == 0:
                    load_da_head(0)
                if h + 1 < DAH:
                    load_da_head(h + 1)
                for i in range(NQ):
                    nkb = 8 * (i + 1)

                    def emit_z(kb, which, sl=sl, i=i):
                        zp = Z1 if which == 1 else Z2
                        kk = k1 if which == 1 else k2
                        qx = q1 if which == 1 else q2
                        op("pe", lambda e, zp=zp, kk=kk, qx=qx, kb=kb: e.matmul(
                            zp[:], kk[sl][:, kb * 128:(kb + 1) * 128], qx[sl][:, i * 512:(i + 1) * 512], start=True, stop=True),
                           reads=[kk[sl].b, qx[sl].b], writes=[zp.b])

                    if pend_t:
                        pend_t.pop(0)()
                    emit_z(0, 1)
                    emit_z(0, 2)
                    for kb in range(nkb):
                        par = (blk + kb) % 2
                        first = kb == 0
                        last = kb == nkb - 1
                        masked = kb >= 8 * i
                        r = kb - 8 * i
                        A1, A2 = E1[par], E2[par]
                        op("act", lambda e, A1=A1: e.activation(A1[:], Z1[:], AF.Exp), reads=[Z1.b], writes=[A1.b])
                        op("act", lambda e, A2=A2: e.activation(A2[:], Z2[:], AF.Exp), reads=[Z2.b], writes=[A2.b])
                        if masked:
                            op("dve", lambda e, A1=A1, r=r: e.tensor_tensor(out=A1[:], in0=A1[:], in1=maskD[:, r * 512:(r + 1) * 512], op=ALU.mult),
                               reads=[A1.b, maskD.b], writes=[A1.b])
                            op("dve", lambda e, A2=A2, r=r: e.tensor_tensor(out=A2[:], in0=A2[:], in1=maskD[:, r * 512:(r + 1) * 512], op=ALU.mult),
                               reads=[A2.b, maskD.b], writes=[A2.b])
                        if not last:
                            emit_z(kb + 1, 1)
                        for (pp, lo) in ((O1a, 0), (O1b, 128)):
                            op("pe", lambda e, pp=pp, lo=lo, kb=kb, A1=A1, first=first, last=last, sl=sl: e.matmul(
                                pp[:], vd[sl][:, kb, lo:lo + 128], A1[:], start=first, stop=last),
                               reads=[vd[sl].b, A1.b], writes=[pp.b])
                        op("pe", lambda e, A1=A1, first=first, last=last: e.matmul(Z1s[:], onesb[:], A1[:], start=first, stop=last),
                           reads=[onesb.b, A1.b], writes=[Z1s.b])
                        if not last:
                            emit_z(kb + 1, 2)
                        for (pp, lo) in ((O2a, 0), (O2b, 128)):
                            op("pe", lambda e, pp=pp, lo=lo, kb=kb, A2=A2, first=first, last=last, sl=sl: e.matmul(
                                pp[:], vd[sl][:, kb, lo:lo + 128], A2[:], start=first, stop=last),
                               reads=[vd[sl].b, A2.b], writes=[pp.b])
                        op("pe", lambda e, A2=A2, first=first, last=last: e.matmul(Z2s[:], onesb[:], A2[:], start=first, stop=last),
                           reads=[onesb.b, A2.b], writes=[Z2s.b])
                    blk += nkb
                    op("dve", lambda e: e.reciprocal(rz1[:], Z1s[:]), reads=[Z1s.b], writes=[rz1.b])
                    op("dve", lambda e: e.reciprocal(rz2[:], Z2s[:]), reads=[Z2s.b], writes=[rz2.b])
                    op("dve", lambda e: e.tensor_scalar(out=rz2[:], in0=rz2[:], scalar1=lam[:, 0:1], scalar2=None, op0=ALU.mult),
                       reads=[rz2.b, lam.b], writes=[rz2.b])
                    for hf, (pa, pb) in enumerate(((O1a, O2a), (O1b, O2b))):
                        op("dve", lambda e, pa=pa: e.tensor_tensor(out=ta[:], in0=pa[:], in1=rz1[:], op=ALU.mult),
                           reads=[pa.b, rz1.b], writes=[ta.b])
                        op("dve", lambda e, pb=pb: e.tensor_tensor(out=tb[:], in0=pb[:], in1=rz2[:], op=ALU.mult),
                           reads=[pb.b, rz2.b], writes=[tb.b])
                        op("dve", lambda e, hf=hf: e.tensor_tensor(out=od[hf][:], in0=ta[:], in1=tb[:], op=ALU.subtract),
                           reads=[ta.b, tb.b], writes=[od[hf].b])
                        op("act", lambda e, hf=hf: e.activation(sq[hf][:], od[hf][:], AF.Square), reads=[od[hf].b], writes=[sq[hf].b])
                    op("pe", lambda e: e.matmul(Z1[:], ones[:], sq[0][:], start=True, stop=False), reads=[ones.b, sq[0].b], writes=[Z1.b])
                    op("pe", lambda e: e.matmul(Z1[:], ones[:], sq[1][:], start=False, stop=True), reads=[ones.b, sq[1].b], writes=[Z1.b])
                    op("act", lambda e: e.activation(sd[:], Z1[:], AF.Sqrt, bias=epsc[:, 0:1], scale=1.0 / 256.0),
                       reads=[Z1.b, epsc.b], writes=[sd.b])
                    op("dve", lambda e: e.reciprocal(rstd[:], sd[:]), reads=[sd.b], writes=[rstd.b])
                    for hf in range(2):
                        yg = ystg[hf]
                        op("dve", lambda e, hf=hf, yg=yg: e.scalar_tensor_tensor(out=yg[:], in0=od[hf][:], scalar=gsc[:, hf:hf + 1], in1=rstd[:],
                                                                                 op0=ALU.mult, op1=ALU.mult),
                           reads=[od[hf].b, gsc.b, rstd.b], writes=[yg.b])
                        dma("sp", lambda e, yg=yg, h=h, hf=hf, i=i: e.dma_start(out=OdaT[2 * h + hf, :, i * 512:(i + 1) * 512], in_=yg[:]),
                            yg.b, reads=[yg.b])
            while pend_t:
                pend_t.pop(0)()
            S_.barrier()
            S_.release([maskD.b, gsc.b] + [t.b for t in k1 + k2 + vd + q1 + q2 + ystg])

        with contextlib.ExitStack() as ph:
            xt = T(ph, "xtD", [128, KC, 512], BF16)
            osb = T(ph, "osb", [128, SBH, 512], BF16)
            oda = T(ph, "oda", [128, 2 * DAH, 512], BF16)
            bg = T(ph, "bg", [128, 2 * KC], F32)
            dma("sp", lambda e: e.dma_start(out=bg[:], in_=bgT), bg.b, writes=[bg.b])
            wsb_ = [T(ph, f"wsb{i}", [128, SBH, 256], BF16) for i in range(2)]
            wda_ = [T(ph, f"wda{i}", [128, 2 * DAH, 256], BF16) for i in range(2)]
            wg1_ = [T(ph, f"wg1{i}", [128, KC, 256], BF16) for i in range(2)]
            wg2_ = [T(ph, f"wg2{i}", [128, KC, 256], BF16) for i in range(2)]
            s1 = [T(ph, f"s1{i}", [128, 512], F32) for i in range(2)]
            s2 = [T(ph, f"s2{i}", [128, 512], F32) for i in range(2)]
            mm1 = [T(ph, f"mm1{i}", [128, 512], F32) for i in range(2)]
            mm2 = [T(ph, f"mm2{i}", [128, 512], F32) for i in range(2)]
            mstg = [T(ph, f"mstg{i}", [128, 512], BF16) for i in range(2)]
            gcnt = 0
            ccnt = 0
            for ti in range(NQ):
                t0 = ti * 512
                xv = xT_own.rearrange("(kc p) t -> p kc t", p=128)
                for k0 in range(0, KC, 4):
                    dma("pool", lambda e, k0=k0, t0=t0, xv=xv: e.dma_start(out=xt[:, k0:k0 + 4, :], in_=xv[:, k0:k0 + 4, t0:t0 + 512]),
                        xt.b, writes=[xt.b])
                dma("pool", lambda e, t0=t0: e.dma_start(out=osb[:], in_=OsbT.rearrange("h p t -> p h t")[:, :, t0:t0 + 512]),
                    osb.b, writes=[osb.b])
                dma("pool", lambda e, t0=t0: e.dma_start(out=oda[:], in_=OdaT.rearrange("h p t -> p h t")[:, :, t0:t0 + 512]),
                    oda.b, writes=[oda.b])
                for cg in range(D // 256):
                    sl = gcnt % 2
                    gcnt += 1
                    c0 = cg * 256
                    dma("pool", lambda e, sl=sl, c0=c0: e.dma_start(out=wsb_[sl][:], in_=w_sbb_b.rearrange("(kc p) c -> p kc c", p=128)[:, :, c0:c0 + 256]),
                        wsb_[sl].b, reads=[CB["sbb"]], writes=[wsb_[sl].b])
                    dma("pool", lambda e, sl=sl, c0=c0: e.dma_start(out=wda_[sl][:], in_=w_dab_b.rearrange("(kc p) c -> p kc c", p=128)[:, :, c0:c0 + 256]),
                        wda_[sl].b, reads=[CB["dab"]], writes=[wda_[sl].b])
                    wv = w_in.rearrange("(kc p) c -> p kc c", p=128)
                    for k0 in range(0, KC, 8):
                        k1_ = min(KC, k0 + 8)
                        dma("pool", lambda e, sl=sl, c0=c0, k0=k0, k1_=k1_, wv=wv: e.dma_start(
                            out=wg1_[sl][:, k0:k1_, :], in_=wv[:, k0:k1_, c_g + c0:c_g + c0 + 256]), wg1_[sl].b, writes=[wg1_[sl].b])
                        dma("pool", lambda e, sl=sl, c0=c0, k0=k0, k1_=k1_, wv=wv: e.dma_start(
                            out=wg2_[sl][:, k0:k1_, :], in_=wv[:, k0:k1_, c_g + D + c0:c_g + D + c0 + 256]), wg2_[sl].b, writes=[wg2_[sl].b])
                    for cc in range(2):
                        c = cg * 2 + cc
                        par = ccnt % 2
                        ccnt += 1
                        pb1, pb2, pg1, pg2 = PS[4 * par], PS[4 * par + 1], PS[4 * par + 2], PS[4 * par + 3]
                        for kc in range(SBH):
                            op("pe", lambda e, pb1=pb1, sl=sl, kc=kc, cc=cc: e.matmul(
                                pb1[:], wsb_[sl][:, kc, cc * 128:(cc + 1) * 128], osb[:, kc, :], start=(kc == 0), stop=(kc == SBH - 1)),
                               reads=[wsb_[sl].b, osb.b], writes=[pb1.b])
                        for kc in range(2 * DAH):
                            op("pe", lambda e, pb2=pb2, sl=sl, kc=kc, cc=cc: e.matmul(
                                pb2[:], wda_[sl][:, kc, cc * 128:(cc + 1) * 128], oda[:, kc, :], start=(kc == 0), stop=(kc == 2 * DAH - 1)),
                               reads=[wda_[sl].b, oda.b], writes=[pb2.b])
                        for kc in range(KC):
                            op("pe", lambda e, pg1=pg1, sl=sl, kc=kc, cc=cc: e.matmul(
                                pg1[:], wg1_[sl][:, kc, cc * 128:(cc + 1) * 128], xt[:, kc, :], start=(kc == 0), stop=(kc == KC - 1)),
                               reads=[wg1_[sl].b, xt.b], writes=[pg1.b])
                        for kc in range(KC):
                            op("pe", lambda e, pg2=pg2, sl=sl, kc=kc, cc=cc: e.matmul(
                                pg2[:], wg2_[sl][:, kc, cc * 128:(cc + 1) * 128], xt[:, kc, :], start=(kc == 0), stop=(kc == KC - 1)),
                               reads=[wg2_[sl].b, xt.b], writes=[pg2.b])
                        op("act", lambda e, par=par, pg1=pg1, c=c: e.activation(s1[par][:], pg1[:], AF.Sigmoid, bias=bg[:, c:c + 1]),
                           reads=[pg1.b, bg.b], writes=[s1[par].b])
                        op("act", lambda e, par=par, pg2=pg2, c=c: e.activation(s2[par][:], pg2[:], AF.Sigmoid, bias=bg[:, KC + c:KC + c + 1]),
                           reads=[pg2.b, bg.b], writes=[s2[par].b])
                        op("dve", lambda e, par=par, pb1=pb1: e.tensor_tensor(out=mm1[par][:], in0=pb1[:], in1=s1[par][:], op=ALU.mult),
                           reads=[pb1.b, s1[par].b], writes=[mm1[par].b])
                        op("dve", lambda e, par=par, pb2=pb2: e.tensor_tensor(out=mm2[par][:], in0=pb2[:], in1=s2[par][:], op=ALU.mult),
                           reads=[pb2.b, s2[par].b], writes=[mm2[par].b])
                        op("dve", lambda e, par=par: e.tensor_tensor(out=mstg[par][:], in0=mm1[par][:], in1=mm2[par][:], op=ALU.add),
                           reads=[mm1[par].b, mm2[par].b], writes=[mstg[par].b])
                        dma("sp", lambda e, par=par, c=c, t0=t0: e.dma_start(out=mergedT[c, :, t0:t0 + 512], in_=mstg[par][:]),
                            mstg[par].b, reads=[mstg[par].b])
            S_.barrier()
            S_.release([xt.b, osb.b, oda.b, bg.b] + [t.b for t in wsb_ + wda_ + wg1_ + wg2_ + mstg])

        def layer_norm(xr, G, Bt, junk, stat):
            op("dve", lambda e: e.tensor_reduce(out=stat[:, 0:1], in_=xr[:], axis=AX.X, op=ALU.add), reads=[xr.b], writes=[stat.b])
            op("dve", lambda e: e.tensor_scalar(out=stat[:, 0:1], in0=stat[:, 0:1], scalar1=float(-1.0 / D), scalar2=None, op0=ALU.mult),
               reads=[stat.b], writes=[stat.b])
            op("dve", lambda e: e.tensor_scalar(out=xr[:], in0=xr[:], scalar1=stat[:, 0:1], scalar2=None, op0=ALU.add),
               reads=[xr.b, stat.b], writes=[xr.b])
            op("dve", lambda e: e.memset(stat[:, 1:2], 0.0), writes=[stat.b])
            op("dve", lambda e: e.scalar_tensor_tensor(out=junk[:], in0=xr[:], scalar=1.0, in1=xr[:], op0=ALU.mult, op1=ALU.mult,
                                                       accum_out=stat[:, 1:2]),
               reads=[xr.b, stat.b], writes=[junk.b, stat.b])
            op("act", lambda e: e.activation(stat[:, 2:3], stat[:, 1:2], AF.Sqrt, bias=epsc[:, 0:1], scale=float(1.0 / D)),
               reads=[stat.b, epsc.b], writes=[stat.b])
            op("dve", lambda e: e.reciprocal(stat[:, 3:4], stat[:, 2:3]), reads=[stat.b], writes=[stat.b])
            op("dve", lambda e: e.scalar_tensor_tensor(out=xr[:], in0=xr[:], scalar=stat[:, 3:4], in1=G[:], op0=ALU.mult, op1=ALU.mult),
               reads=[xr.b, stat.b, G.b], writes=[xr.b])
            op("dve", lambda e: e.tensor_tensor(out=xr[:], in0=xr[:], in1=Bt[:], op=ALU.add), reads=[xr.b, Bt.b], writes=[xr.b])

        with contextlib.ExitStack() as ph:
            mT = T(ph, "mT", [128, KC, 256], BF16)
            xr = [T(ph, f"xr{i}", [128, D], F32) for i in range(2)]
            wo = [T(ph, f"wo{i}", [128, KC, 512], BF16) for i in range(2)]
            G1 = T(ph, "G1", [128, D], F32)
            B1 = T(ph, "B1", [128, D], F32)
            junk = T(ph, "junkD", [128, D], F32)
            stat = T(ph, "statD", [128, 4], F32)
            hT = [T(ph, f"hT{i}", [128, KC, 128], BF16) for i in range(2)]
            dma("sp", lambda e: e.dma_start(out=G1[:], in_=ln1g[0:1, :].partition_broadcast(128)), G1.b, writes=[G1.b])
            dma("sp", lambda e: e.dma_start(out=B1[:], in_=ln1b[0:1, :].partition_broadcast(128)), B1.b, writes=[B1.b])
            wc = 0
            pc = 0
            hc_ = 0
            for ti in range(SO // 256):
                t0 = ti * 256
                dma("sp", lambda e, t0=t0: e.dma_start(out=mT[:], in_=mergedT.rearrange("c p t -> p c t")[:, :, t0:t0 + 256]),
                    mT.b, writes=[mT.b])
                for tc in range(2):
                    dma("sp", lambda e, tc=tc, t0=t0: e.dma_start(out=xr[tc][:], in_=x_own[t0 + tc * 128:t0 + (tc + 1) * 128, :]),
                        xr[tc].b, writes=[xr[tc].b])
                for cg in range(D // 512):
                    w = wo[wc % 2]
                    wc += 1
                    wv = w_out_b.rearrange("(kc p) c -> p kc c", p=128)
                    for k0 in range(0, KC, 8):
                        k1_ = min(KC, k0 + 8)
                        dma("sp", lambda e, w=w, k0=k0, k1_=k1_, cg=cg, wv=wv: e.dma_start(out=w[:, k0:k1_, :], in_=wv[:, k0:k1_, cg * 512:(cg + 1) * 512]),
                            w.b, reads=[CB["out"]], writes=[w.b])
                    for tc in range(2):
                        p = PS[pc % 4]
                        pc += 1
                        for kc in range(KC):
                            op("pe", lambda e, p=p, w=w, kc=kc, tc=tc: e.matmul(p[:], mT[:, kc, tc * 128:(tc + 1) * 128], w[:, kc, :],
                                                                              start=(kc == 0), stop=(kc == KC - 1)),
                               reads=[mT.b, w.b], writes=[p.b])
                        op("dve", lambda e, p=p, tc=tc, cg=cg: e.scalar_tensor_tensor(
                            out=xr[tc][:, cg * 512:(cg + 1) * 512], in0=xr[tc][:, cg * 512:(cg + 1) * 512], scalar=float(alpha), in1=p[:],
                            op0=ALU.mult, op1=ALU.add), reads=[xr[tc].b, p.b], writes=[xr[tc].b])
                for tc in range(2):
                    layer_norm(xr[tc], G1, B1, junk, stat)
                    r0 = t0 + tc * 128
                    dma("pool", lambda e, tc=tc, r0=r0: e.dma_start(out=h1_scr[r0:r0 + 128, :], in_=xr[tc][:]), xr[tc].b, reads=[xr[tc].b])
                    ht = hT[hc_ % 2]
                    hc_ += 1
                    for k0 in range(0, KC, 4):
                        p = PS[4 + (pc % 4)]
                        pc += 1
                        for kk in range(4):
                            kc = k0 + kk
                            op("pe", lambda e, p=p, kk=kk, kc=kc, tc=tc: e.transpose(p[:, kk * 128:(kk + 1) * 128], xr[tc][:, kc * 128:(kc + 1) * 128], ident[:]),
                               reads=[xr[tc].b, ident.b], writes=[p.b])
                        op("act", lambda e, p=p, ht=ht, k0=k0: e.copy(ht[:, k0:k0 + 4, :], p[:].rearrange("p (a t) -> p a t", a=4)),
                           reads=[p.b], writes=[ht.b])
                    dma("pool", lambda e, ht=ht, r0=r0: e.dma_start(out=h1T_scr.rearrange("c p t -> p c t")[:, :, r0:r0 + 128], in_=ht[:]),
                        ht.b, reads=[ht.b])
            S_.barrier()
            S_.release([mT.b, G1.b, B1.b] + [t.b for t in xr + wo + hT])

        with contextlib.ExitStack() as ph:
            sk = T(ph, "sk", [128, HC, NKEYS], F32)
            dma("sp", lambda e: e.dma_start(out=sk[:], in_=skT.rearrange("h p k -> p h k")), sk.b, writes=[sk.b])
            hTt = T(ph, "hTt", [128, KC, 512], BF16)
            wq = [T(ph, f"wq{i}", [128, KC, 512], BF16) for i in range(2)]
            qpT = T(ph, "qpT", [128, HC, 512], F32)
            scst = [T(ph, f"scst{i}", [128, HC, NKEYS], F32) for i in range(2)]
            wc = 0
            pc = 0
            scn = 0
            for ti in range(NQ):
                t0 = ti * 512
                dma("sp", lambda e, t0=t0: e.dma_start(out=hTt[:], in_=h1T_scr.rearrange("c p t -> p c t")[:, :, t0:t0 + 512]),
                    hTt.b, writes=[hTt.b])
                for g in range(QW // 512):
                    w = wq[wc % 2]
                    wc += 1
                    wv = w_q_b.rearrange("(kc p) c -> p kc c", p=128)
                    for k0 in range(0, KC, 8):
                        k1_ = min(KC, k0 + 8)
                        dma("sp", lambda e, w=w, k0=k0, k1_=k1_, g=g, wv=wv: e.dma_start(out=w[:, k0:k1_, :], in_=wv[:, k0:k1_, g * 512:(g + 1) * 512]),
                            w.b, reads=[CB["wq"]], writes=[w.b])
                    for cb in range(4):
                        hcx = g * 4 + cb
                        p = PS[pc % 4]
                        pc += 1
                        for kc in range(KC):
                            op("pe", lambda e, p=p, w=w, kc=kc, cb=cb: e.matmul(p[:], w[:, kc, cb * 128:(cb + 1) * 128], hTt[:, kc, :],
                                                                              start=(kc == 0), stop=(kc == KC - 1)),
                               reads=[w.b, hTt.b], writes=[p.b])
                        op("act", lambda e, p=p, hcx=hcx: e.copy(qpT[:, hcx, :], p[:]), reads=[p.b], writes=[qpT.b])
                for tc in range(4):
                    r0 = t0 + tc * 128
                    sct = scst[scn % 2]
                    scn += 1
                    for g4 in range(0, HC, 4):
                        p = PS[4 + (pc % 4)]
                        pc += 1
                        for kk in range(4):
                            hcx = g4 + kk
                            op("pe", lambda e, p=p, kk=kk, hcx=hcx, tc=tc: e.matmul(p[:, kk * 128:(kk + 1) * 128], qpT[:, hcx, tc * 128:(tc + 1) * 128],
                                                                                  sk[:, hcx, :], start=True, stop=True),
                               reads=[qpT.b, sk.b], writes=[p.b])
                        op("act", lambda e, p=p, g4=g4, sct=sct: e.copy(sct[:, g4:g4 + 4, :], p[:].rearrange("p (a t) -> p a t", a=4)),
                           reads=[p.b], writes=[sct.b])
                    dma("pool", lambda e, sct=sct, r0=r0: e.dma_start(out=sc_scr[r0:r0 + 128, :], in_=sct[:].rearrange("p h k -> p (h k)")),
                        sct.b, reads=[sct.b])
            S_.barrier()
            S_.release([sk.b, hTt.b] + [t.b for t in wq + scst])

        with contextlib.ExitStack() as ph:
            G2 = T(ph, "G2", [128, D], F32)
            B2 = T(ph, "B2", [128, D], F32)
            dma("sp", lambda e: e.dma_start(out=G2[:], in_=ln2g[0:1, :].partition_broadcast(128)), G2.b, writes=[G2.b])
            dma("sp", lambda e: e.dma_start(out=B2[:], in_=ln2b[0:1, :].partition_broadcast(128)), B2.b, writes=[B2.b])
            identb = T(ph, "identb", [128, 128], BF16)
            dma("pool", lambda e: e.dma_start(out=identb[:], in_=c_ident), identb.b, writes=[identb.b])
            sc = T(ph, "sc", [128, HC, NKEYS], F32)
            top = T(ph, "top", [128, PH, 2, TOPK], F32)
            topi = T(ph, "topi", [128, PH, 2, TOPK], U32)
            topf = T(ph, "topf", [128, PH, 2, TOPK], F32)
            cand = T(ph, "cand", [128, PH, TOPK, TOPK], F32)
            cidx = T(ph, "cidx", [128, PH, TOPK, TOPK], F32)
            cwork = T(ph, "cwork", [128, PH, TOPK * TOPK], F32)
            best = T(ph, "best", [128, PH, TOPK], F32)
            junk2 = T(ph, "junk2", [128, TOPK * TOPK], F32)
            idxf = T(ph, "idxf", [128, NSEL], F32)
            dd = T(ph, "dd", [128, PH, TOPK], F32)
            eb = T(ph, "eb", [128, PH, TOPK], F32)
            zs = T(ph, "zs", [128, PH], F32)
            idxi_ = [T(ph, f"idxi{i}", [128, NSEL], I32) for i in range(2)]
            gate_ = [T(ph, f"gate{i}", [128, PH, TOPK], F32) for i in range(2)]
            hb = [T(ph, f"hb{i}", [128, D], BF16) for i in range(2)]
            acc = T(ph, "acc", [128, D], F32)
            junkb = T(ph, "junkb", [128, D], BF16)
            stat = T(ph, "statE", [128, 4], F32)
            NRG = 5
            ring = [T(ph, f"ring{i}", [128, 2 * D], BF16) for i in range(NRG)]
            dgs = [T(ph, f"dgs{i}", [128, 128], BF16) for i in range(4)]
            gel = [T(ph, f"gel{i}", [128, 2], F32) for i in range(4)]
            NCH = SO // 128
            NDG = D // 512
            cnt = dict(r=0, d=0)

            def e2_s1(c):
                par = c % 2
                r0 = c * 128
                idxi, gate = idxi_[par], gate_[par]
                workv = cwork[:].rearrange("p h (c k) -> p (h c) k", c=2)
                dma("pool", lambda e: e.dma_start(out=hb[par][:], in_=h1_scr[r0:r0 + 128, :]), hb[par].b, writes=[hb[par].b])
                dma("sp", lambda e: e.dma_start(out=sc[:].rearrange("p h k -> p (h k)"), in_=sc_scr[r0:r0 + 128, :]), sc.b, writes=[sc.b])
                for hcx in range(HC):
                    hh, c2 = hcx // 2, hcx % 2
                    op("dve", lambda e, hh=hh, c2=c2, hcx=hcx: e.max(out=top[:, hh, c2, 0:8], in_=sc[:, hcx, :]),
                       reads=[sc.b], writes=[top.b])
                    op("dve", lambda e, hh=hh, c2=c2, hcx=hcx: e.max_index(out=topi[:, hh, c2, 0:8], in_max=top[:, hh, c2, 0:8], in_values=sc[:, hcx, :]),
                       reads=[sc.b, top.b], writes=[topi.b])
                    op("dve", lambda e, hh=hh, c2=c2, hcx=hcx: e.match_replace(out=workv[:, hcx, :], in_to_replace=top[:, hh, c2, 0:8],
                                                                               in_values=sc[:, hcx, :], imm_value=-1e30),
                       reads=[sc.b, top.b], writes=[cwork.b])
                    op("dve", lambda e, hh=hh, c2=c2, hcx=hcx: e.max(out=top[:, hh, c2, 8:16], in_=workv[:, hcx, :]),
                       reads=[cwork.b], writes=[top.b])
                    op("dve", lambda e, hh=hh, c2=c2, hcx=hcx: e.max_index(out=topi[:, hh, c2, 8:16], in_max=top[:, hh, c2, 8:16], in_values=workv[:, hcx, :]),
                       reads=[cwork.b, top.b], writes=[topi.b])
                op("dve", lambda e: e.tensor_copy(topf[:], topi[:]), reads=[topi.b], writes=[topf.b])
                bshape = [128, TOPK, TOPK]
                for hh in range(PH):
                    op("dve", lambda e, hh=hh: e.tensor_tensor(out=cand[:, hh, :, :], in0=top[:, hh, 0, :].unsqueeze(2).to_broadcast(bshape),
                                                               in1=top[:, hh, 1, :].unsqueeze(1).to_broadcast(bshape), op=ALU.add),
                       reads=[top.b], writes=[cand.b])
                    op("dve", lambda e, hh=hh: e.scalar_tensor_tensor(out=cidx[:, hh, :, :], in0=topf[:, hh, 0, :].unsqueeze(2).to_broadcast(bshape),
                                                                      scalar=float(NKEYS), in1=topf[:, hh, 1, :].unsqueeze(1).to_broadcast(bshape),
                                                                      op0=ALU.mult, op1=ALU.add),
                       reads=[topf.b], writes=[cidx.b])
                for hh in range(PH):
                    cv = cand[:, hh, :, :].rearrange("p a b -> p (a b)")
                    op("dve", lambda e, hh=hh, cv=cv: e.max(out=best[:, hh, 0:8], in_=cv), reads=[cand.b], writes=[best.b])
                    op("dve", lambda e, hh=hh, cv=cv: e.match_replace(out=cwork[:, hh, :], in_to_replace=best[:, hh, 0:8], in_values=cv, imm_value=-1e30),
                       reads=[cand.b, best.b], writes=[cwork.b])
                    op("dve", lambda e, hh=hh: e.max(out=best[:, hh, 8:16], in_=cwork[:, hh, :]), reads=[cwork.b], writes=[best.b])
                for hh in range(PH):
                    cv = cand[:, hh, :, :].rearrange("p a b -> p (a b)")
                    iv = cidx[:, hh, :, :].rearrange("p a b -> p (a b)")
                    for k in range(TOPK):
                        n = hh * TOPK + k
                        op("dve", lambda e, hh=hh, k=k, n=n, cv=cv, iv=iv: e.scalar_tensor_tensor(
                            out=junk2[:], in0=cv, scalar=best[:, hh, k:k + 1], in1=iv, op0=ALU.is_equal, op1=ALU.mult),
                           reads=[cand.b, cidx.b, best.b], writes=[junk2.b])
                        op("dve", lambda e, n=n: e.tensor_reduce(out=idxf[:, n:n + 1], in_=junk2[:], axis=AX.X, op=ALU.max),
                           reads=[junk2.b, idxf.b], writes=[idxf.b])
                op("dve", lambda e: e.tensor_copy(idxi[:], idxf[:]), reads=[idxf.b], writes=[idxi.b])
                op("dve", lambda e: e.tensor_tensor(out=dd[:], in0=best[:], in1=best[:, :, 0:1].to_broadcast([128, PH, TOPK]), op=ALU.subtract),
                   reads=[best.b], writes=[dd.b])
                op("act", lambda e: e.activation(eb[:], dd[:], AF.Exp), reads=[dd.b], writes=[eb.b])
                op("dve", lambda e: e.tensor_reduce(out=zs[:], in_=eb[:], axis=AX.X, op=ALU.add), reads=[eb.b], writes=[zs.b])
                op("dve", lambda e: e.reciprocal(zs[:], zs[:]), reads=[zs.b], writes=[zs.b])
                op("dve", lambda e: e.tensor_tensor(out=gate[:], in0=eb[:], in1=zs[:].unsqueeze(2).to_broadcast([128, PH, TOPK]), op=ALU.mult),
                   reads=[eb.b, zs.b], writes=[gate.b])

            def e2_n(c, n):
                par = c % 2
                rg = ring[cnt["r"] % NRG]
                cnt["r"] += 1
                dg = dgs[cnt["d"] % 4]
                ge = gel[cnt["d"] % 4]
                cnt["d"] += 1
                idxi = idxi_[par]
                gflat = gate_[par][:].rearrange("p h k -> p (h k)")
                dma("pool", lambda e: e.indirect_dma_start(
                    out=rg[:], out_offset=None, in_=puv_b, in_offset=bass.IndirectOffsetOnAxis(ap=idxi[:, n:n + 1], axis=0)),
                    rg.b, reads=[idxi.b, CB["puv"]], writes=[rg.b])
                op("dve", lambda e: e.memset(ge[:, 0:1], 0.0), writes=[ge.b])
                op("dve", lambda e: e.scalar_tensor_tensor(out=rg[:, 0:D], in0=rg[:, 0:D], scalar=1.0, in1=hb[par][:], op0=ALU.mult, op1=ALU.mult,
                                                           accum_out=ge[:, 0:1]),
                   reads=[rg.b, hb[par].b, ge.b], writes=[rg.b, ge.b])
                op("act", lambda e: e.activation(ge[:, 1:2], ge[:, 0:1], AF.Gelu), reads=[ge.b], writes=[ge.b])
                op("act", lambda e: e.activation(ge[:, 1:2], ge[:, 1:2], AF.Copy, scale=gflat[:, n:n + 1]),
                   reads=[ge.b, gate_[par].b], writes=[ge.b])
                op("act", lambda e: e.activation(dg[:], identb[:], AF.Copy, scale=ge[:, 1:2]),
                   reads=[identb.b, ge.b], writes=[dg.b])
                for g in range(NDG):
                    op("pe", lambda e, g=g: e.matmul(PS[g][:], dg[:], rg[:, D + g * 512:D + (g + 1) * 512], start=(n == 0), stop=(n == NSEL - 1)),
                       reads=[dg.b, rg.b], writes=[PS[g].b])

            def e2_post(c):
                r0 = c * 128
                dma("sp", lambda e: e.dma_start(out=acc[:], in_=h1_scr[r0:r0 + 128, :]), acc.b, writes=[acc.b])
                for g in range(NDG):
                    op("dve", lambda e, g=g: e.scalar_tensor_tensor(out=acc[:, g * 512:(g + 1) * 512], in0=acc[:, g * 512:(g + 1) * 512],
                                                                    scalar=float(alpha), in1=PS[g][:], op0=ALU.mult, op1=ALU.add),
                       reads=[acc.b, PS[g].b], writes=[acc.b])
                layer_norm(acc, G2, B2, junkb, stat)
                dma("sp", lambda e: e.dma_start(out=out_own[r0:r0 + 128, :], in_=acc[:]), acc.b, reads=[acc.b])

            e2_s1(0)
            for c in range(NCH):
                for n in range(NSEL):
                    e2_n(c, n)
                    if n == NSEL // 2 and c + 1 < NCH:
                        e2_s1(c + 1)
                e2_post(c)
            S_.barrier(include_bg=True)

        S_.emit()
    return nc


def make_consts(j):
    ident = np.eye(128, dtype=np.float32)
    jj = np.arange(128)[:, None]
    ss = np.arange(128)[None, :]
    tri = (jj > ss).astype(np.float32)
    ones = np.ones((128, 128), np.float32)
    perm = np.zeros((128, 128), np.float32)
    for m in range(128):
        perm[(m + 64) % 128, m] = 1.0
    inv_freq = (ROPE_THETA ** (-np.arange(0, 128, 2, dtype=np.float32) / np.float32(128))).astype(np.float32)
    rope = np.zeros((128, 3), np.float32)
    rope[:, 0] = np.concatenate([inv_freq, inv_freq])
    rope[:64, 1] = -2.0 * math.pi
    rope[64:, 1] = 2.0 * math.pi
    s = np.arange(128)[:, None, None]
    r = np.arange(8)[None, :, None]
    t = np.arange(512)[None, None, :]
    kpos = r * 128 + s
    qpos = j * 512 + t
    maskS = (kpos < qpos).astype(np.float32).reshape(128, 8 * 512).astype(ml_dtypes.bfloat16)
    maskD = (kpos <= qpos).astype(np.float32).reshape(128, 8 * 512).astype(ml_dtypes.bfloat16)
    return dict(c_ident=ident, c_tri=tri, c_ones=ones, c_perm=perm, c_rope=rope, c_maskS=maskS, c_maskD=maskD)


def prep_inputs(inp, cfg):
    D, S, NB, PH = cfg["D"], cfg["S"], cfg["NB"], cfg["PH"]
    KC = D // 128
    f = lambda a: np.ascontiguousarray(np.asarray(a))
    x = f(inp["x"])
    pos = f(inp["positions"]).astype(np.int32)
    shared = dict(
        w_in=f(inp["w_in"][0]),
        bgT=f(np.asarray(inp["b_gate"][0]).reshape(2 * KC, 128).T),
        lamv=f(np.stack([np.asarray(inp["lambda_q1"][0]), np.asarray(inp["lambda_k1"][0]),
                         np.asarray(inp["lambda_q2"][0]), np.asarray(inp["lambda_k2"][0])])),
        sublnT=f(np.asarray(inp["subln_g"][0]).reshape(2, 128).T),
        w_sbb=f(inp["w_sb_branch"][0]), w_dab=f(inp["w_da_branch"][0]), w_out=f(inp["w_out"][0]),
        ln1g=f(np.asarray(inp["ln1_g"][0])[None, :]), ln1b=f(np.asarray(inp["ln1_b"][0])[None, :]),
        ln2g=f(np.asarray(inp["ln2_g"][0])[None, :]), ln2b=f(np.asarray(inp["ln2_b"][0])[None, :]),
        w_q=f(inp["peer_w_q"][0]),
        skT=f(np.asarray(inp["peer_sub_keys"][0]).reshape(2 * PH, NKEYS, 128).transpose(0, 2, 1)),
        pu=f(inp["peer_u"][0]), pv=f(inp["peer_v"][0]),
    )
    consts = [make_consts(0), make_consts(1)]
    in_maps = []
    for c in range(2 * NB):
        b, j = c // 2, c % 2
        tiles = [2 * i + j for i in range(S // 1024)]
        rows = np.concatenate([np.arange(t * 512, (t + 1) * 512) for t in tiles])
        xb = x[b]
        m = dict(shared)
        m.update(consts[j])
        m["xT_all"] = f(xb.T)
        m["x_own"] = f(xb[rows])
        m["xT_own"] = f(xb[rows].T)
        m["pos_all"] = f(pos[b][None, :])
        m["pos_own"] = f(pos[b][rows][None, :])
        in_maps.append(m)
    return in_maps


def assemble(results, cfg):
    D, S, NB = cfg["D"], cfg["S"], cfg["NB"]
    out = np.zeros((NB, S, D), np.float32)
    for c in range(2 * NB):
        b, j = c // 2, c % 2
        tiles = [2 * i + j for i in range(S // 1024)]
        rows = np.concatenate([np.arange(t * 512, (t + 1) * 512) for t in tiles])
        out[b, rows] = results[c]["out_own"]
    return out


def run(inputs, cfg):
    nc = build(cfg)
    in_maps = prep_inputs(inputs, cfg)
    res = run_bass_kernel_spmd(nc, in_maps, core_ids=list(range(2 * cfg["NB"])))
    return assemble(res.results, cfg)


def kernel(**inputs):
    return run(inputs, FULL_CFG)
```

```python
import contextlib
import math
import numpy as np
import ml_dtypes
import concourse.bass as bass
import concourse.mybir as mybir
from concourse.bass_utils import run_bass_kernel_spmd

F32 = mybir.dt.float32
BF16 = mybir.dt.bfloat16
I32 = mybir.dt.int32
U32 = mybir.dt.uint32
AF = mybir.ActivationFunctionType
ALU = mybir.AluOpType
AX = mybir.AxisListType

SEM_ROT = 30000
LN_EPS = 1e-5
NKEYS = 128
TOPK = 16
NEXP = NKEYS * NKEYS
ROPE_THETA = 10000.0

FULL_CFG = dict(D=4096, S=4096, NB=4, SBH=16, DAH=8, PH=8, DEPTH=1)


class Buf:
    __slots__ = ("name", "w", "r", "ds")

    def __init__(self, name=""):
        self.name = name
        self.w = None
        self.r = {}
        self.ds = None


class DSem:
    def __init__(self, handle):
        self.h = handle
        self.total = 0
        self.bg = False


class Sched:
    ENGS = ("pe", "act", "dve", "pool", "sp")

    def __init__(self, nc, stack):
        self.nc = nc
        self.stack = stack
        self.streams = {e: [] for e in self.ENGS}
        self.esem = {}
        self.ecount = {}
        self.known = {e: {} for e in self.ENGS}
        self.nsem = 0
        self.dsems = []
        self.free_ds = []
        for e in ("pe", "act", "dve", "pool"):
            self._new_esem(e)

    def _alloc(self, name):
        self.nsem += 1
        return self.stack.enter_context(self.nc.semaphore(f"{name}_{self.nsem}"))

    def _new_esem(self, e):
        self.esem[e] = self._alloc(f"es_{e}")
        self.ecount[e] = 0

    def dsem(self, name="d"):
        if self.free_ds:
            return self.free_ds.pop()
        d = DSem(self._alloc(f"ds_{name}"))
        self.dsems.append(d)
        return d

    def release(self, bufs):
        for b in bufs:
            if b.ds is not None:
                self.free_ds.append(b.ds)
                b.ds = None

    def _collect(self, eng, reads, writes, dma=False):
        need = {}

        def add(tok, kind):
            if tok is None:
                return
            semh, val, teng, ds = tok
            if teng == eng and teng is not None and not dma:
                if eng == "pe":
                    return
                if kind == "war":
                    return
            if ds is not None:
                val = ds.total
            k = id(semh)
            if k not in need or need[k][1] < val:
                need[k] = (semh, val)

        for b in reads:
            add(b.w, "raw")
        for b in writes:
            add(b.w, "waw")
            for t in b.r.values():
                add(t, "war")
        return self._filter(eng, need.values())

    def _filter(self, eng, pairs):
        waits = []
        kn = self.known[eng]
        for semh, val in pairs:
            k = id(semh)
            if kn.get(k, 0) >= val:
                continue
            kn[k] = val
            waits.append((semh, val))
        return waits

    def _commit(self, tok, reads, writes):
        k = id(tok[0])
        for b in reads:
            b.r[k] = tok
        for b in writes:
            b.w = tok
            b.r = {}

    def op(self, eng, fn, reads=(), writes=()):
        waits = self._collect(eng, reads, writes)
        if self.ecount[eng] >= SEM_ROT:
            self._new_esem(eng)
        self.ecount[eng] += 1
        semh = self.esem[eng]
        tok = (semh, self.ecount[eng], eng, None)
        self.streams[eng].append((waits, fn, (semh, 1)))
        self._commit(tok, reads, writes)
        return tok

    def dma(self, q, fn, owner, reads=(), writes=()):
        if owner.ds is None:
            owner.ds = self.dsem(owner.name)
        ds = owner.ds
        waits = self._collect(q, reads, writes, dma=True)
        ds.total += 16
        tok = (ds.h, ds.total, None, ds)
        self.streams[q].append((waits, fn, (ds.h, 16)))
        self._commit(tok, reads, writes)
        return tok

    def barrier(self, include_bg=False):
        pairs = [(self.esem[e], self.ecount[e]) for e in ("pe", "act", "dve", "pool") if self.ecount[e] > 0]
        pairs += [(d.h, d.total) for d in self.dsems if d.total > 0 and (include_bg or not d.bg)]
        for e in self.ENGS:
            w = self._filter(e, pairs)
            if w:
                self.streams[e].append((w, None, None))

    def emit(self):
        nc = self.nc
        streams = self.streams

        def run(engh, items):
            for waits, fn, inc in items:
                for semh, val in waits:
                    engh.wait_ge(semh, val)
                if fn is None:
                    continue
                ins = fn(engh)
                if inc is not None:
                    ins.then_inc(inc[0], inc[1])

        with nc.Block() as block:
            @block.tensor
            def _(e):
                run(e, streams["pe"])

            @block.scalar
            def _(e):
                run(e, streams["act"])

            @block.vector
            def _(e):
                run(e, streams["dve"])

            @block.gpsimd
            def _(e):
                run(e, streams["pool"])

            @block.sync
            def _(e):
                run(e, streams["sp"])


def build(cfg):
    D, S, SBH, DAH, PH = cfg["D"], cfg["S"], cfg["SBH"], cfg["DAH"], cfg["PH"]
    DEPTH = cfg["DEPTH"]
    assert DEPTH == 1
    KC = D // 128
    SO = S // 2
    SBW = SBH * 128
    DAW = DAH * 256
    INW = 3 * SBW + 3 * DAW + 2 * D
    c_sbq, c_sbk, c_sbv = 0, SBW, 2 * SBW
    c_daq, c_dak, c_dav = 3 * SBW, 3 * SBW + DAW, 3 * SBW + 2 * DAW
    c_g = 3 * SBW + 3 * DAW
    NQ = SO // 512
    HC = 2 * PH
    NSEL = PH * TOPK
    QW = PH * 256
    alpha = (2 * DEPTH) ** 0.25
    lam_init = 0.8 - 0.6 * math.exp(-0.3 * 0)
    scale = 128 ** -0.5
    PI = math.pi

    nc = bass.Bass("TRN2", target_bir_lowering=False)

    def din(name, shape, dt=F32):
        return nc.dram_tensor(name, list(shape), dt, kind="ExternalInput").ap()

    xT_all = din("xT_all", [D, S])
    xT_own = din("xT_own", [D, SO])
    x_own = din("x_own", [SO, D])
    pos_all = din("pos_all", [1, S], I32)
    pos_own = din("pos_own", [1, SO], I32)
    w_in = din("w_in", [D, INW])
    bgT = din("bgT", [128, 2 * KC])
    lamv = din("lamv", [4, 128])
    sublnT = din("sublnT", [128, 2])
    w_sbb = din("w_sbb", [SBW, D])
    w_dab = din("w_dab", [DAW, D])
    w_out = din("w_out", [D, D])
    ln1g = din("ln1g", [1, D]); ln1b = din("ln1b", [1, D])
    ln2g = din("ln2g", [1, D]); ln2b = din("ln2b", [1, D])
    w_q = din("w_q", [D, QW])
    skT = din("skT", [HC, 128, NKEYS])
    pu = din("pu", [NEXP, D])
    pv = din("pv", [NEXP, D])
    c_ident = din("c_ident", [128, 128])
    c_tri = din("c_tri", [128, 128])
    c_ones = din("c_ones", [128, 128])
    c_perm = din("c_perm", [128, 128])
    c_rope = din("c_rope", [128, 3])
    c_maskS = din("c_maskS", [128, 8 * 512], BF16)
    c_maskD = din("c_maskD", [128, 8 * 512], BF16)
    out_own = nc.dram_tensor("out_own", [SO, D], F32, kind="ExternalOutput").ap()

    def scr(name, shape, dt):
        return nc.dram_tensor(name, list(shape), dt).ap()

    KsbT = scr("KsbT", [SBH, 128, S], BF16)
    Vsb = scr("Vsb", [S, SBW], BF16)
    KdaT = scr("KdaT", [2 * DAH, 128, S], BF16)
    Vda = scr("Vda", [S, DAW], BF16)
    QsbT = scr("QsbT", [SBH, 128, SO], BF16)
    QdaT = scr("QdaT", [2 * DAH, 128, SO], BF16)
    OsbT = scr("OsbT", [SBH, 128, SO], BF16)
    OdaT = scr("OdaT", [2 * DAH, 128, SO], BF16)
    mergedT = scr("mergedT", [KC, 128, SO], BF16)
    h1_scr = scr("h1_scr", [SO, D], F32)
    h1T_scr = scr("h1T_scr", [KC, 128, SO], BF16)
    sc_scr = scr("sc_scr", [SO, 2 * PH * NKEYS], F32)
    w_in_b = scr("w_in_b", [D, INW], BF16)
    w_sbb_b = scr("w_sbb_b", [SBW, D], BF16)
    w_dab_b = scr("w_dab_b", [DAW, D], BF16)
    w_out_b = scr("w_out_b", [D, D], BF16)
    w_q_b = scr("w_q_b", [D, QW], BF16)
    puv_b = scr("puv_b", [NEXP, 2 * D], BF16)

    with contextlib.ExitStack() as st:
        S_ = Sched(nc, st)
        op = S_.op
        dma = S_.dma

        class T:
            def __init__(self, stack, name, shape, dt, psum=False):
                if psum:
                    self.t = stack.enter_context(nc.psum_tensor(name, list(shape), dt))
                else:
                    self.t = stack.enter_context(nc.sbuf_tensor(name, list(shape), dt))
                self.b = Buf(name)

            def __getitem__(self, k):
                return self.t[k]

        PS = [T(st, f"ps{i}", [128, 512], F32, psum=True) for i in range(8)]
        ident = T(st, "ident", [128, 128], F32)
        tri = T(st, "tri", [128, 128], F32)
        ones = T(st, "ones", [128, 128], F32)
        onesb = T(st, "onesb", [128, 128], BF16)
        trib = T(st, "trib", [128, 128], BF16)
        perm = T(st, "perm", [128, 128], F32)
        rope_c = T(st, "rope_c", [128, 3], F32)
        epsc = T(st, "epsc", [128, 1], F32)
        lam = T(st, "lam", [128, 1], F32)

        dma("sp", lambda e: e.dma_start(out=ident[:], in_=c_ident), ident.b, writes=[ident.b])
        dma("sp", lambda e: e.dma_start(out=tri[:], in_=c_tri), tri.b, writes=[tri.b])
        dma("sp", lambda e: e.dma_start(out=ones[:], in_=c_ones), ones.b, writes=[ones.b])
        dma("pool", lambda e: e.dma_start(out=onesb[:], in_=c_ones), onesb.b, writes=[onesb.b])
        dma("pool", lambda e: e.dma_start(out=trib[:], in_=c_tri), trib.b, writes=[trib.b])
        dma("sp", lambda e: e.dma_start(out=perm[:], in_=c_perm), perm.b, writes=[perm.b])
        dma("sp", lambda e: e.dma_start(out=rope_c[:], in_=c_rope), rope_c.b, writes=[rope_c.b])
        op("dve", lambda e: e.memset(epsc[:], LN_EPS), writes=[epsc.b])

        with contextlib.ExitStack() as ph:
            lv = T(ph, "lv", [128, 4, 128], F32)
            lp = T(ph, "lp", [128, 2, 128], F32)
            ls = T(ph, "ls", [128, 2], F32)
            le = T(ph, "le", [128, 2], F32)
            for r in range(4):
                dma("sp", lambda e, r=r: e.dma_start(out=lv[:, r, :], in_=lamv[r:r + 1, :].partition_broadcast(128)),
                    lv.b, writes=[lv.b])
            op("dve", lambda e: e.tensor_tensor(out=lp[:, 0, :], in0=lv[:, 0, :], in1=lv[:, 1, :], op=ALU.mult),
               reads=[lv.b], writes=[lp.b])
            op("dve", lambda e: e.tensor_tensor(out=lp[:, 1, :], in0=lv[:, 2, :], in1=lv[:, 3, :], op=ALU.mult),
               reads=[lv.b], writes=[lp.b])
            op("dve", lambda e: e.tensor_reduce(out=ls[:], in_=lp[:], axis=AX.X, op=ALU.add), reads=[lp.b], writes=[ls.b])
            op("act", lambda e: e.activation(le[:], ls[:], AF.Exp), reads=[ls.b], writes=[le.b])
            op("dve", lambda e: e.tensor_tensor(out=lam[:], in0=le[:, 0:1], in1=le[:, 1:2], op=ALU.subtract),
               reads=[le.b], writes=[lam.b])
            op("dve", lambda e: e.tensor_scalar(out=lam[:], in0=lam[:], scalar1=float(lam_init), scalar2=None, op0=ALU.add),
               reads=[lam.b], writes=[lam.b])
            S_.barrier()
            S_.release([lv.b])

        CB = {}

        pending = []

        def precast(key, dst, src, r_tot, c0, c1, rblk=1024, dc0=None, defer=False):
            if key not in CB:
                CB[key] = Buf("cb_" + key)
                CB[key].ds = S_.dsem("cb_" + key)
                CB[key].ds.bg = True
            b = CB[key]
            if dc0 is None:
                dc0 = c0
            for r in range(0, r_tot, rblk):
                r1 = min(r_tot, r + rblk)
                th = (lambda r=r, r1=r1: dma("pool", lambda e: e.dma_start(out=dst[r:r1, dc0:dc0 + (c1 - c0)], in_=src[r:r1, c0:c1]), b, writes=[b]))
                th.key = key
                if defer:
                    pending.append(th)
                else:
                    th()

        def flush_pending(k=None):
            n = len(pending) if k is None else min(k, len(pending))
            for _ in range(n):
                pending.pop(0)()

        precast("sbkv", w_in_b, w_in, D, c_sbk, c_sbv + SBW, defer=True)
        precast("dakv", w_in_b, w_in, D, c_dak, c_dav + DAW, defer=True)
        precast("q", w_in_b, w_in, D, c_sbq, c_sbq + SBW, defer=True)
        precast("q", w_in_b, w_in, D, c_daq, c_daq + DAW, defer=True)
        pend_e = pending[:]
        del pending[:]
        precast("sbb", w_sbb_b, w_sbb, SBW, 0, D, defer=True)
        precast("dab", w_dab_b, w_dab, DAW, 0, D, defer=True)
        precast("out", w_out_b, w_out, D, 0, D, defer=True)
        precast("wq", w_q_b, w_q, D, 0, QW, defer=True)
        n_w_pending = len(pending)
        precast("puv", puv_b, pu, NEXP, 0, D, dc0=0, defer=True)
        precast("puv", puv_b, pv, NEXP, 0, D, dc0=D, defer=True)
        pend_w = pending[:n_w_pending]
        pend_t = pending[n_w_pending:]
        del pending[:]

        with contextlib.ExitStack() as ph:
            xT = T(ph, "xT", [128, KC, 1024], BF16)
            posi = T(ph, "posi", [128, 1024], I32)
            ang = T(ph, "ang", [128, 1024], F32)
            m1 = T(ph, "m1", [128, 1024], F32)
            ki = T(ph, "ki", [128, 1024], I32)
            kf = T(ph, "kf", [128, 1024], F32)
            CT = T(ph, "CT", [128, 1024], F32)
            ST = T(ph, "ST", [128, 1024], F32)
            wsl = [T(ph, f"wA{i}", [128, KC, 512], BF16) for i in range(2)]
            stg = [T(ph, f"stA{i}", [128, 512], BF16) for i in range(4)]
            qf = [T(ph, f"qf{i}", [128, 512], F32) for i in range(2)]
            t1 = [T(ph, f"t1{i}", [128, 512], F32) for i in range(2)]
            t2 = [T(ph, f"t2{i}", [128, 512], F32) for i in range(2)]
            wcnt = [0]
            scnt = [0]
            pcnt = [0]
            rcnt = [0]

            def phaseA_pass(xsrc, possrc, ntiles, groups, first_pass=False):
                for tile in range(ntiles):
                    t0 = tile * 1024
                    xv = xsrc.rearrange("(kc p) t -> p kc t", p=128)
                    for k0 in range(0, KC, 4):
                        dma("pool", lambda e, k0=k0, t0=t0, xv=xv: e.dma_start(out=xT[:, k0:k0 + 4, :], in_=xv[:, k0:k0 + 4, t0:t0 + 1024]),
                            xT.b, writes=[xT.b])
                    dma("sp", lambda e, t0=t0, possrc=possrc: e.dma_start(out=posi[:], in_=possrc[0:1, t0:t0 + 1024].partition_broadcast(128)),
                        posi.b, writes=[posi.b])
                    op("dve", lambda e: e.tensor_copy(ang[:], posi[:]), reads=[posi.b], writes=[ang.b])
                    op("dve", lambda e: e.tensor_scalar(out=ang[:], in0=ang[:], scalar1=rope_c[:, 0:1], scalar2=None, op0=ALU.mult),
                       reads=[ang.b, rope_c.b], writes=[ang.b])
                    def range_red(add):
                        op("dve", lambda e: e.tensor_scalar(out=m1[:], in0=ang[:], scalar1=float(1.0 / (2 * PI)), scalar2=float(add),
                                                            op0=ALU.mult, op1=ALU.add), reads=[ang.b], writes=[m1.b])
                        op("dve", lambda e: e.tensor_copy(ki[:], m1[:]), reads=[m1.b], writes=[ki.b])
                        op("dve", lambda e: e.tensor_copy(kf[:], ki[:]), reads=[ki.b], writes=[kf.b])
                        op("dve", lambda e: e.tensor_tensor(out=m1[:], in0=m1[:], in1=kf[:], op=ALU.subtract), reads=[m1.b, kf.b], writes=[m1.b])
                        op("dve", lambda e: e.tensor_scalar(out=kf[:], in0=m1[:], scalar1=0.5, scalar2=None, op0=ALU.is_gt),
                           reads=[m1.b], writes=[kf.b])
                        op("dve", lambda e: e.tensor_tensor(out=m1[:], in0=m1[:], in1=kf[:], op=ALU.subtract), reads=[m1.b, kf.b], writes=[m1.b])
                    range_red(0.0)
                    op("act", lambda e: e.activation(ST[:], m1[:], AF.Sin, scale=rope_c[:, 1:2]),
                       reads=[m1.b, rope_c.b], writes=[ST.b])
                    range_red(0.25)
                    op("act", lambda e: e.activation(CT[:], m1[:], AF.Sin, scale=float(2 * PI)), reads=[m1.b], writes=[CT.b])

                    for (c0, kind, dst, hbase, rope, sc, ck) in groups:
                        direct = first_pass and tile == 0
                        if wcnt[0] >= 30 and wcnt[0] % 2 == 0 and pend_w:
                            pend_w.pop(0)()
                        w = wsl[wcnt[0] % 2]
                        wcnt[0] += 1
                        if direct:
                            wv = w_in.rearrange("(kc p) c -> p kc c", p=128)
                            for k0 in range(0, KC, 4):
                                dma("pool", lambda e, w=w, k0=k0, c0=c0, wv=wv: e.dma_start(out=w[:, k0:k0 + 4, :], in_=wv[:, k0:k0 + 4, c0:c0 + 512]),
                                    w.b, writes=[w.b])
                            for _ in range(2):
                                if pend_e:
                                    pend_e.pop(0)()
                        else:
                            while pend_e:
                                pend_e.pop(0)()
                            for th in [t_ for t_ in pend_w if t_.key == ck]:
                                pend_w.remove(th)
                                th()
                            wv = w_in_b.rearrange("(kc p) c -> p kc c", p=128)
                            for k0 in range(0, KC, 8):
                                k1_ = min(KC, k0 + 8)
                                dma("sp", lambda e, w=w, k0=k0, k1_=k1_, c0=c0, wv=wv: e.dma_start(out=w[:, k0:k1_, :], in_=wv[:, k0:k1_, c0:c0 + 512]),
                                    w.b, reads=[CB[ck]], writes=[w.b])
                        if kind == "fm":
                            for cb in range(4):
                                head = hbase + cb
                                for ts in range(2):
                                    p = PS[pcnt[0] % 4]
                                    pcnt[0] += 1
                                    for kc in range(KC):
                                        op("pe", lambda e, p=p, w=w, kc=kc, cb=cb, ts=ts: e.matmul(
                                            p[:], w[:, kc, cb * 128:(cb + 1) * 128], xT[:, kc, ts * 512:(ts + 1) * 512],
                                            start=(kc == 0), stop=(kc == KC - 1)),
                                           reads=[w.b, xT.b], writes=[p.b])
                                    sg = stg[scnt[0] % 4]
                                    scnt[0] += 1
                                    if not rope:
                                        op("act", lambda e, sg=sg, p=p, sc=sc: e.activation(sg[:], p[:], AF.Copy, scale=float(sc)),
                                           reads=[p.b], writes=[sg.b])
                                    else:
                                        r = rcnt[0] % 2
                                        rcnt[0] += 1
                                        p2 = PS[4 + r]
                                        op("act", lambda e, r=r, p=p, sc=sc: e.activation(qf[r][:], p[:], AF.Copy, scale=float(sc)),
                                           reads=[p.b], writes=[qf[r].b])
                                        op("pe", lambda e, r=r, p2=p2: e.matmul(p2[:], perm[:], qf[r][:], start=True, stop=True),
                                           reads=[perm.b, qf[r].b], writes=[p2.b])
                                        op("dve", lambda e, r=r, ts=ts: e.tensor_tensor(out=t1[r][:], in0=qf[r][:], in1=CT[:, ts * 512:(ts + 1) * 512], op=ALU.mult),
                                           reads=[qf[r].b, CT.b], writes=[t1[r].b])
                                        op("dve", lambda e, r=r, ts=ts, p2=p2: e.tensor_tensor(out=t2[r][:], in0=p2[:], in1=ST[:, ts * 512:(ts + 1) * 512], op=ALU.mult),
                                           reads=[p2.b, ST.b], writes=[t2[r].b])
                                        op("dve", lambda e, r=r, sg=sg: e.tensor_tensor(out=sg[:], in0=t1[r][:], in1=t2[r][:], op=ALU.add),
                                           reads=[t1[r].b, t2[r].b], writes=[sg.b])
                                    dma("pool", lambda e, sg=sg, dst=dst, head=head, t0=t0, ts=ts: e.dma_start(
                                        out=dst[head, :, t0 + ts * 512:t0 + (ts + 1) * 512], in_=sg[:]),
                                        sg.b, reads=[sg.b])
                        else:
                            for tc in range(8):
                                p = PS[pcnt[0] % 4]
                                pcnt[0] += 1
                                for kc in range(KC):
                                    op("pe", lambda e, p=p, w=w, kc=kc, tc=tc: e.matmul(
                                        p[:], xT[:, kc, tc * 128:(tc + 1) * 128], w[:, kc, :],
                                        start=(kc == 0), stop=(kc == KC - 1)),
                                       reads=[w.b, xT.b], writes=[p.b])
                                sg = stg[scnt[0] % 4]
                                scnt[0] += 1
                                if tc % 2 == 0:
                                    op("act", lambda e, sg=sg, p=p: e.copy(sg[:], p[:]), reads=[p.b], writes=[sg.b])
                                else:
                                    op("dve", lambda e, sg=sg, p=p: e.tensor_copy(sg[:], p[:]), reads=[p.b], writes=[sg.b])
                                dma("pool", lambda e, sg=sg, dst=dst, hbase=hbase, t0=t0, tc=tc: e.dma_start(
                                    out=dst[t0 + tc * 128:t0 + (tc + 1) * 128, hbase:hbase + 512], in_=sg[:]),
                                    sg.b, reads=[sg.b])

            kv_groups = []
            for g in range(SBW // 512):
                kv_groups.append((c_sbk + g * 512, "fm", KsbT, g * 4, False, 1.0, "sbkv"))
            for g in range(SBW // 512):
                kv_groups.append((c_sbv + g * 512, "tm", Vsb, g * 512, False, 1.0, "sbkv"))
            for g in range(DAW // 512):
                kv_groups.append((c_dak + g * 512, "fm", KdaT, g * 4, True, 1.0, "dakv"))
            for g in range(DAW // 512):
                kv_groups.append((c_dav + g * 512, "tm", Vda, g * 512, False, 1.0, "dakv"))
            q_groups = []
            for g in range(SBW // 512):
                q_groups.append((c_sbq + g * 512, "fm", QsbT, g * 4, False, scale, "q"))
            for g in range(DAW // 512):
                q_groups.append((c_daq + g * 512, "fm", QdaT, g * 4, True, scale, "q"))
            phaseA_pass(xT_all, pos_all, S // 1024, kv_groups, first_pass=True)
            phaseA_pass(xT_own, pos_own, SO // 1024, q_groups)
            while pend_w:
                pend_w.pop(0)()
            S_.barrier()
            S_.release([xT.b, posi.b] + [t.b for t in wsl] + [t.b for t in stg])

        NKB = S // 128
        with contextlib.ExitStack() as ph:
            maskS = T(ph, "maskS", [128, 8 * 512], BF16)
            dma("sp", lambda e: e.dma_start(out=maskS[:], in_=c_maskS), maskS.b, writes=[maskS.b])
            kT = [T(ph, f"kT{i}", [128, S], BF16) for i in range(2)]
            vv = [T(ph, f"vv{i}", [128, NKB, 128], BF16) for i in range(2)]
            qq = [T(ph, f"qq{i}", [128, SO], BF16) for i in range(2)]
            NE, NSP, NL, NLW, NW = 3, 6, 4, 3, 4
            e_sb = [T(ph, f"e_sb{i}", [128, 512], F32) for i in range(NE)]
            sp_sb = [T(ph, f"sp_sb{i}", [128, 512], F32) for i in range(NSP)]
            L_sb = [T(ph, f"L_sb{i}", [128, 512], BF16) for i in range(NL)]
            lw_sb = [T(ph, f"lw_sb{i}", [128, 512], F32) for i in range(NLW)]
            w_sb = [T(ph, f"w_sb{i}", [128, 512], BF16) for i in range(NW)]
            Lsum = [T(ph, f"Lsum{i}", [128, 512], BF16) for i in range(2)]
            ostg = [T(ph, f"ostg{i}", [128, 512], BF16) for i in range(2)]
            chains = []
            gi = 0
            for h in range(SBH):
                for i in range(NQ):
                    nkb = 8 * (i + 1)
                    for n, kb in enumerate(range(nkb - 1, -1, -1)):
                        chains.append(dict(h=h, i=i, kb=kb, first=(n == 0), last=(n == nkb - 1), g=gi,
                                           masked=(kb >= 8 * i), r=kb - 8 * i, newhead=(i == 0 and n == 0)))
                    gi += 1
            NCH_B = len(chains)
            ZB = lambda c: PS[c % 4]
            XB = lambda c: PS[4 + c % 2]
            OB = lambda g: PS[6 + g % 2]

            def load_head(h):
                sl = h % 2
                dma("sp", lambda e: e.dma_start(out=kT[sl][:], in_=KsbT[h, :, :]), kT[sl].b, writes=[kT[sl].b])
                vsrc = Vsb.rearrange("(kb s) c -> s kb c", s=128)
                dma("sp", lambda e: e.dma_start(out=vv[sl][:], in_=vsrc[:, :, h * 128:(h + 1) * 128]), vv[sl].b, writes=[vv[sl].b])
                dma("sp", lambda e: e.dma_start(out=qq[sl][:], in_=QsbT[h, :, :]), qq[sl].b, writes=[qq[sl].b])

            def stage(st_, c):
                ch = chains[c]
                h, i, kb, sl = ch["h"], ch["i"], ch["kb"], ch["h"] % 2
                zp, xp, ob = ZB(c), XB(c), OB(ch["g"])
                E, SPt, Lt, LW, W = e_sb[c % NE], sp_sb[c % NSP], L_sb[c % NL], lw_sb[c % NLW], w_sb[c % NW]
                first, last, masked, r = ch["first"], ch["last"], ch["masked"], ch["r"]
                if st_ == 0:
                    if ch["newhead"] and h == 0:
                        load_head(0)
                    op("pe", lambda e: e.matmul(zp[:], kT[sl][:, kb * 128:(kb + 1) * 128], qq[sl][:, i * 512:(i + 1) * 512], start=True, stop=True),
                       reads=[kT[sl].b, qq[sl].b], writes=[zp.b])
                elif st_ == 1:
                    op("act", lambda e: e.activation(E[:], zp[:], AF.Exp, scale=-1.0), reads=[zp.b], writes=[E.b])
                elif st_ == 2:
                    op("act", lambda e: e.activation(SPt[:], E[:], AF.Ln, bias=1.0), reads=[E.b], writes=[SPt.b])
                elif st_ == 3:
                    op("dve", lambda e: e.scalar_tensor_tensor(out=Lt[:], in0=zp[:], scalar=-1.0, in1=SPt[:], op0=ALU.mult, op1=ALU.subtract),
                       reads=[zp.b, SPt.b], writes=[Lt.b])
                    if masked:
                        op("dve", lambda e: e.tensor_tensor(out=Lt[:], in0=Lt[:], in1=maskS[:, r * 512:(r + 1) * 512], op=ALU.mult),
                           reads=[Lt.b, maskS.b], writes=[Lt.b])
                elif st_ == 4:
                    op("pe", lambda e: e.matmul(xp[:], trib[:], Lt[:], start=True, stop=first), reads=[trib.b, Lt.b], writes=[xp.b])
                    if not first:
                        lsp = Lsum[(c - 1) % 2]
                        op("pe", lambda e: e.matmul(xp[:], onesb[:], lsp[:], start=False, stop=True), reads=[onesb.b, lsp.b], writes=[xp.b])
                    if not last:
                        lsn = Lsum[c % 2]
                        if first:
                            op("pool", lambda e: e.tensor_copy(lsn[:], Lt[:]), reads=[Lt.b], writes=[lsn.b])
                        else:
                            lsp = Lsum[(c - 1) % 2]
                            op("pool", lambda e: e.tensor_tensor(out=lsn[:], in0=lsp[:], in1=Lt[:], op=ALU.add),
                               reads=[Lt.b, lsp.b], writes=[lsn.b])
                elif st_ == 5:
                    op("dve", lambda e: e.tensor_tensor(out=LW[:], in0=xp[:], in1=SPt[:], op=ALU.subtract), reads=[xp.b, SPt.b], writes=[LW.b])
                elif st_ == 6:
                    op("act", lambda e: e.activation(W[:], LW[:], AF.Exp), reads=[LW.b], writes=[W.b])
                    if masked:
                        op("pool", lambda e: e.tensor_tensor(out=W[:], in0=W[:], in1=maskS[:, r * 512:(r + 1) * 512], op=ALU.mult),
                           reads=[W.b, maskS.b], writes=[W.b])
                elif st_ == 7:
                    op("pe", lambda e: e.matmul(ob[:], vv[sl][:, kb, :], W[:], start=first, stop=last), reads=[vv[sl].b, W.b], writes=[ob.b])
                elif st_ == 8:
                    if ch["newhead"] and h + 1 < SBH:
                        load_head(h + 1)
                    if last:
                        og = ostg[ch["g"] % 2]
                        op("act", lambda e: e.copy(og[:], ob[:]), reads=[ob.b], writes=[og.b])
                        dma("sp", lambda e: e.dma_start(out=OsbT[h, :, i * 512:(i + 1) * 512], in_=og[:]), og.b, reads=[og.b])

            NST = 9
            for t in range(NCH_B + NST - 1):
                for st_ in (0, 4, 7, 1, 2, 6, 8, 3, 5):
                    c = t - st_
                    if 0 <= c < NCH_B:
                        stage(st_, c)
            S_.barrier()
            S_.release([maskS.b] + [t.b for t in kT + vv + qq + ostg])

        with contextlib.ExitStack() as ph:
            maskD = T(ph, "maskD", [128, 8 * 512], BF16)
            dma("sp", lambda e: e.dma_start(out=maskD[:], in_=c_maskD), maskD.b, writes=[maskD.b])
            gsc = T(ph, "gsc", [128, 2], F32)
            dma("sp", lambda e: e.dma_start(out=gsc[:], in_=sublnT), gsc.b, writes=[gsc.b])
            op("dve", lambda e: e.tensor_scalar(out=gsc[:], in0=gsc[:], scalar1=float(1.0 - lam_init), scalar2=None, op0=ALU.mult),
               reads=[gsc.b], writes=[gsc.b])
            k1 = [T(ph, f"k1{i}", [128, S], BF16) for i in range(2)]
            k2 = [T(ph, f"k2{i}", [128, S], BF16) for i in range(2)]
            vd = [T(ph, f"vd{i}", [128, NKB, 256], BF16) for i in range(2)]
            q1 = [T(ph, f"q1{i}", [128, SO], BF16) for i in range(2)]
            q2 = [T(ph, f"q2{i}", [128, SO], BF16) for i in range(2)]
            E1 = [T(ph, f"E1{i}", [128, 512], BF16) for i in range(2)]
            E2 = [T(ph, f"E2{i}", [128, 512], BF16) for i in range(2)]
            rz1 = T(ph, "rz1", [128, 512], F32)
            rz2 = T(ph, "rz2", [128, 512], F32)
            ta = T(ph, "ta", [128, 512], F32)
            tb = T(ph, "tb", [128, 512], F32)
            od = [T(ph, f"od{i}", [128, 512], F32) for i in range(2)]
            sq = [T(ph, f"sq{i}", [128, 512], F32) for i in range(2)]
            sd = T(ph, "sd", [128, 512], F32)
            rstd = T(ph, "rstd", [128, 512], F32)
            ystg = [T(ph, f"ystg{i}", [128, 512], BF16) for i in range(2)]
            Z1, Z2 = PS[0], PS[1]
            O1a, O1b, Z1s, O2a, O2b, Z2s = PS[2], PS[3], PS[4], PS[5], PS[6], PS[7]
            blk = 0
            def load_da_head(h):
                sl = h % 2
                dma("sp", lambda e: e.dma_start(out=k1[sl][:], in_=KdaT[2 * h, :, :]), k1[sl].b, writes=[k1[sl].b])
                dma("sp", lambda e: e.dma_start(out=k2[sl][:], in_=KdaT[2 * h + 1, :, :]), k2[sl].b, writes=[k2[sl].b])
                vsrc = Vda.rearrange("(kb s) c -> s kb c", s=128)
                dma("sp", lambda e: e.dma_start(out=vd[sl][:], in_=vsrc[:, :, h * 256:(h + 1) * 256]), vd[sl].b, writes=[vd[sl].b])
                dma("sp", lambda e: e.dma_start(out=q1[sl][:], in_=QdaT[2 * h, :, :]), q1[sl].b, writes=[q1[sl].b])
                dma("sp", lambda e: e.dma_start(out=q2[sl][:], in_=QdaT[2 * h + 1, :, :]), q2[sl].b, writes=[q2[sl].b])

            for h in range(DAH):
                sl = h % 2
                if h == 0:
                    load_da_head(0)
                if h + 1 < DAH:
                    load_da_head(h + 1)
                for i in range(NQ):
                    nkb = 8 * (i + 1)

                    def emit_z(kb, which, sl=sl, i=i):
                        zp = Z1 if which == 1 else Z2
                        kk = k1 if which == 1 else k2
                        qx = q1 if which == 1 else q2
                        op("pe", lambda e, zp=zp, kk=kk, qx=qx, kb=kb: e.matmul(
                            zp[:], kk[sl][:, kb * 128:(kb + 1) * 128], qx[sl][:, i * 512:(i + 1) * 512], start=True, stop=True),
                           reads=[kk[sl].b, qx[sl].b], writes=[zp.b])

                    if pend_t:
                        pend_t.pop(0)()
                    emit_z(0, 1)
                    emit_z(0, 2)
                    for kb in range(nkb):
                        par = (blk + kb) % 2
                        first = kb == 0
                        last = kb == nkb - 1
                        masked = kb >= 8 * i
                        r = kb - 8 * i
                        A1, A2 = E1[par], E2[par]
                        op("act", lambda e, A1=A1: e.activation(A1[:], Z1[:], AF.Exp), reads=[Z1.b], writes=[A1.b])
                        op("act", lambda e, A2=A2: e.activation(A2[:], Z2[:], AF.Exp), reads=[Z2.b], writes=[A2.b])
                        if masked:
                            op("dve", lambda e, A1=A1, r=r: e.tensor_tensor(out=A1[:], in0=A1[:], in1=maskD[:, r * 512:(r + 1) * 512], op=ALU.mult),
                               reads=[A1.b, maskD.b], writes=[A1.b])
                            op("dve", lambda e, A2=A2, r=r: e.tensor_tensor(out=A2[:], in0=A2[:], in1=maskD[:, r * 512:(r + 1) * 512], op=ALU.mult),
                               reads=[A2.b, maskD.b], writes=[A2.b])
                        if not last:
                            emit_z(kb + 1, 1)
                        for (pp, lo) in ((O1a, 0), (O1b, 128)):
                            op("pe", lambda e, pp=pp, lo=lo, kb=kb, A1=A1, first=first, last=last, sl=sl: e.matmul(
                                pp[:], vd[sl][:, kb, lo:lo + 128], A1[:], start=first, stop=last),
                               reads=[vd[sl].b, A1.b], writes=[pp.b])
                        op("pe", lambda e, A1=A1, first=first, last=last: e.matmul(Z1s[:], onesb[:], A1[:], start=first, stop=last),
                           reads=[onesb.b, A1.b], writes=[Z1s.b])
                        if not last:
                            emit_z(kb + 1, 2)
                        for (pp, lo) in ((O2a, 0), (O2b, 128)):
                            op("pe", lambda e, pp=pp, lo=lo, kb=kb, A2=A2, first=first, last=last, sl=sl: e.matmul(
                                pp[:], vd[sl][:, kb, lo:lo + 128], A2[:], start=first, stop=last),
                               reads=[vd[sl].b, A2.b], writes=[pp.b])
                        op("pe", lambda e, A2=A2, first=first, last=last: e.matmul(Z2s[:], onesb[:], A2[:], start=first, stop=last),
                           reads=[onesb.b, A2.b], writes=[Z2s.b])
                    blk += nkb
                    op("dve", lambda e: e.reciprocal(rz1[:], Z1s[:]), reads=[Z1s.b], writes=[rz1.b])
                    op("dve", lambda e: e.reciprocal(rz2[:], Z2s[:]), reads=[Z2s.b], writes=[rz2.b])
                    op("dve", lambda e: e.tensor_scalar(out=rz2[:], in0=rz2[:], scalar1=lam[:, 0:1], scalar2=None, op0=ALU.mult),
                       reads=[rz2.b, lam.b], writes=[rz2.b])
                    for hf, (pa, pb) in enumerate(((O1a, O2a), (O1b, O2b))):
                        op("dve", lambda e, pa=pa: e.tensor_tensor(out=ta[:], in0=pa[:], in1=rz1[:], op=ALU.mult),
                           reads=[pa.b, rz1.b], writes=[ta.b])
                        op("dve", lambda e, pb=pb: e.tensor_tensor(out=tb[:], in0=pb[:], in1=rz2[:], op=ALU.mult),
                           reads=[pb.b, rz2.b], writes=[tb.b])
                        op("dve", lambda e, hf=hf: e.tensor_tensor(out=od[hf][:], in0=ta[:], in1=tb[:], op=ALU.subtract),
                           reads=[ta.b, tb.b], writes=[od[hf].b])
                        op("act", lambda e, hf=hf: e.activation(sq[hf][:], od[hf][:], AF.Square), reads=[od[hf].b], writes=[sq[hf].b])
                    op("pe", lambda e: e.matmul(Z1[:], ones[:], sq[0][:], start=True, stop=False), reads=[ones.b, sq[0].b], writes=[Z1.b])
                    op("pe", lambda e: e.matmul(Z1[:], ones[:], sq[1][:], start=False, stop=True), reads=[ones.b, sq[1].b], writes=[Z1.b])
                    op("act", lambda e: e.activation(sd[:], Z1[:], AF.Sqrt, bias=epsc[:, 0:1], scale=1.0 / 256.0),
                       reads=[Z1.b, epsc.b], writes=[sd.b])
                    op("dve", lambda e: e.reciprocal(rstd[:], sd[:]), reads=[sd.b], writes=[rstd.b])
                    for hf in range(2):
                        yg = ystg[hf]
                        op("dve", lambda e, hf=hf, yg=yg: e.scalar_tensor_tensor(out=yg[:], in0=od[hf][:], scalar=gsc[:, hf:hf + 1], in1=rstd[:],
                                                                                 op0=ALU.mult, op1=ALU.mult),
                           reads=[od[hf].b, gsc.b, rstd.b], writes=[yg.b])
                        dma("sp", lambda e, yg=yg, h=h, hf=hf, i=i: e.dma_start(out=OdaT[2 * h + hf, :, i * 512:(i + 1) * 512], in_=yg[:]),
                            yg.b, reads=[yg.b])
            while pend_t:
                pend_t.pop(0)()
            S_.barrier()
            S_.release([maskD.b, gsc.b] + [t.b for t in k1 + k2 + vd + q1 + q2 + ystg])

        with contextlib.ExitStack() as ph:
            xt = T(ph, "xtD", [128, KC, 512], BF16)
            osb = T(ph, "osb", [128, SBH, 512], BF16)
            oda = T(ph, "oda", [128, 2 * DAH, 512], BF16)
            bg = T(ph, "bg", [128, 2 * KC], F32)
            dma("sp", lambda e: e.dma_start(out=bg[:], in_=bgT), bg.b, writes=[bg.b])
            wsb_ = [T(ph, f"wsb{i}", [128, SBH, 256], BF16) for i in range(2)]
            wda_ = [T(ph, f"wda{i}", [128, 2 * DAH, 256], BF16) for i in range(2)]
            wg1_ = [T(ph, f"wg1{i}", [128, KC, 256], BF16) for i in range(2)]
            wg2_ = [T(ph, f"wg2{i}", [128, KC, 256], BF16) for i in range(2)]
            s1 = [T(ph, f"s1{i}", [128, 512], F32) for i in range(2)]
            s2 = [T(ph, f"s2{i}", [128, 512], F32) for i in range(2)]
            mm1 = [T(ph, f"mm1{i}", [128, 512], F32) for i in range(2)]
            mm2 = [T(ph, f"mm2{i}", [128, 512], F32) for i in range(2)]
            mstg = [T(ph, f"mstg{i}", [128, 512], BF16) for i in range(2)]
            gcnt = 0
            ccnt = 0
            for ti in range(NQ):
                t0 = ti * 512
                xv = xT_own.rearrange("(kc p) t -> p kc t", p=128)
                for k0 in range(0, KC, 4):
                    dma("pool", lambda e, k0=k0, t0=t0, xv=xv: e.dma_start(out=xt[:, k0:k0 + 4, :], in_=xv[:, k0:k0 + 4, t0:t0 + 512]),
                        xt.b, writes=[xt.b])
                dma("pool", lambda e, t0=t0: e.dma_start(out=osb[:], in_=OsbT.rearrange("h p t -> p h t")[:, :, t0:t0 + 512]),
                    osb.b, writes=[osb.b])
                dma("pool", lambda e, t0=t0: e.dma_start(out=oda[:], in_=OdaT.rearrange("h p t -> p h t")[:, :, t0:t0 + 512]),
                    oda.b, writes=[oda.b])
                for cg in range(D // 256):
                    sl = gcnt % 2
                    gcnt += 1
                    c0 = cg * 256
                    dma("pool", lambda e, sl=sl, c0=c0: e.dma_start(out=wsb_[sl][:], in_=w_sbb_b.rearrange("(kc p) c -> p kc c", p=128)[:, :, c0:c0 + 256]),
                        wsb_[sl].b, reads=[CB["sbb"]], writes=[wsb_[sl].b])
                    dma("pool", lambda e, sl=sl, c0=c0: e.dma_start(out=wda_[sl][:], in_=w_dab_b.rearrange("(kc p) c -> p kc c", p=128)[:, :, c0:c0 + 256]),
                        wda_[sl].b, reads=[CB["dab"]], writes=[wda_[sl].b])
                    wv = w_in.rearrange("(kc p) c -> p kc c", p=128)
                    for k0 in range(0, KC, 8):
                        k1_ = min(KC, k0 + 8)
                        dma("pool", lambda e, sl=sl, c0=c0, k0=k0, k1_=k1_, wv=wv: e.dma_start(
                            out=wg1_[sl][:, k0:k1_, :], in_=wv[:, k0:k1_, c_g + c0:c_g + c0 + 256]), wg1_[sl].b, writes=[wg1_[sl].b])
                        dma("pool", lambda e, sl=sl, c0=c0, k0=k0, k1_=k1_, wv=wv: e.dma_start(
                            out=wg2_[sl][:, k0:k1_, :], in_=wv[:, k0:k1_, c_g + D + c0:c_g + D + c0 + 256]), wg2_[sl].b, writes=[wg2_[sl].b])
                    for cc in range(2):
                        c = cg * 2 + cc
                        par = ccnt % 2
                        ccnt += 1
                        pb1, pb2, pg1, pg2 = PS[4 * par], PS[4 * par + 1], PS[4 * par + 2], PS[4 * par + 3]
                        for kc in range(SBH):
                            op("pe", lambda e, pb1=pb1, sl=sl, kc=kc, cc=cc: e.matmul(
                                pb1[:], wsb_[sl][:, kc, cc * 128:(cc + 1) * 128], osb[:, kc, :], start=(kc == 0), stop=(kc == SBH - 1)),
                               reads=[wsb_[sl].b, osb.b], writes=[pb1.b])
                        for kc in range(2 * DAH):
                            op("pe", lambda e, pb2=pb2, sl=sl, kc=kc, cc=cc: e.matmul(
                                pb2[:], wda_[sl][:, kc, cc * 128:(cc + 1) * 128], oda[:, kc, :], start=(kc == 0), stop=(kc == 2 * DAH - 1)),
                               reads=[wda_[sl].b, oda.b], writes=[pb2.b])
                        for kc in range(KC):
                            op("pe", lambda e, pg1=pg1, sl=sl, kc=kc, cc=cc: e.matmul(
                                pg1[:], wg1_[sl][:, kc, cc * 128:(cc + 1) * 128], xt[:, kc, :], start=(kc == 0), stop=(kc == KC - 1)),
                               reads=[wg1_[sl].b, xt.b], writes=[pg1.b])
                        for kc in range(KC):
                            op("pe", lambda e, pg2=pg2, sl=sl, kc=kc, cc=cc: e.matmul(
                                pg2[:], wg2_[sl][:, kc, cc * 128:(cc + 1) * 128], xt[:, kc, :], start=(kc == 0), stop=(kc == KC - 1)),
                               reads=[wg2_[sl].b, xt.b], writes=[pg2.b])
                        op("act", lambda e, par=par, pg1=pg1, c=c: e.activation(s1[par][:], pg1[:], AF.Sigmoid, bias=bg[:, c:c + 1]),
                           reads=[pg1.b, bg.b], writes=[s1[par].b])
                        op("act", lambda e, par=par, pg2=pg2, c=c: e.activation(s2[par][:], pg2[:], AF.Sigmoid, bias=bg[:, KC + c:KC + c + 1]),
                           reads=[pg2.b, bg.b], writes=[s2[par].b])
                        op("dve", lambda e, par=par, pb1=pb1: e.tensor_tensor(out=mm1[par][:], in0=pb1[:], in1=s1[par][:], op=ALU.mult),
                           reads=[pb1.b, s1[par].b], writes=[mm1[par].b])
                        op("dve", lambda e, par=par, pb2=pb2: e.tensor_tensor(out=mm2[par][:], in0=pb2[:], in1=s2[par][:], op=ALU.mult),
                           reads=[pb2.b, s2[par].b], writes=[mm2[par].b])
                        op("dve", lambda e, par=par: e.tensor_tensor(out=mstg[par][:], in0=mm1[par][:], in1=mm2[par][:], op=ALU.add),
                           reads=[mm1[par].b, mm2[par].b], writes=[mstg[par].b])
                        dma("sp", lambda e, par=par, c=c, t0=t0: e.dma_start(out=mergedT[c, :, t0:t0 + 512], in_=mstg[par][:]),
                            mstg[par].b, reads=[mstg[par].b])
            S_.barrier()
            S_.release([xt.b, osb.b, oda.b, bg.b] + [t.b for t in wsb_ + wda_ + wg1_ + wg2_ + mstg])

        def layer_norm(xr, G, Bt, junk, stat):
            op("dve", lambda e: e.tensor_reduce(out=stat[:, 0:1], in_=xr[:], axis=AX.X, op=ALU.add), reads=[xr.b], writes=[stat.b])
            op("dve", lambda e: e.tensor_scalar(out=stat[:, 0:1], in0=stat[:, 0:1], scalar1=float(-1.0 / D), scalar2=None, op0=ALU.mult),
               reads=[stat.b], writes=[stat.b])
            op("dve", lambda e: e.tensor_scalar(out=xr[:], in0=xr[:], scalar1=stat[:, 0:1], scalar2=None, op0=ALU.add),
               reads=[xr.b, stat.b], writes=[xr.b])
            op("dve", lambda e: e.memset(stat[:, 1:2], 0.0), writes=[stat.b])
            op("dve", lambda e: e.scalar_tensor_tensor(out=junk[:], in0=xr[:], scalar=1.0, in1=xr[:], op0=ALU.mult, op1=ALU.mult,
                                                       accum_out=stat[:, 1:2]),
               reads=[xr.b, stat.b], writes=[junk.b, stat.b])
            op("act", lambda e: e.activation(stat[:, 2:3], stat[:, 1:2], AF.Sqrt, bias=epsc[:, 0:1], scale=float(1.0 / D)),
               reads=[stat.b, epsc.b], writes=[stat.b])
            op("dve", lambda e: e.reciprocal(stat[:, 3:4], stat[:, 2:3]), reads=[stat.b], writes=[stat.b])
            op("dve", lambda e: e.scalar_tensor_tensor(out=xr[:], in0=xr[:], scalar=stat[:, 3:4], in1=G[:], op0=ALU.mult, op1=ALU.mult),
               reads=[xr.b, stat.b, G.b], writes=[xr.b])
            op("dve", lambda e: e.tensor_tensor(out=xr[:], in0=xr[:], in1=Bt[:], op=ALU.add), reads=[xr.b, Bt.b], writes=[xr.b])

        with contextlib.ExitStack() as ph:
            mT = T(ph, "mT", [128, KC, 256], BF16)
            xr = [T(ph, f"xr{i}", [128, D], F32) for i in range(2)]
            wo = [T(ph, f"wo{i}", [128, KC, 512], BF16) for i in range(2)]
            G1 = T(ph, "G1", [128, D], F32)
            B1 = T(ph, "B1", [128, D], F32)
            junk = T(ph, "junkD", [128, D], F32)
            stat = T(ph, "statD", [128, 4], F32)
            hT = [T(ph, f"hT{i}", [128, KC, 128], BF16) for i in range(2)]
            dma("sp", lambda e: e.dma_start(out=G1[:], in_=ln1g[0:1, :].partition_broadcast(128)), G1.b, writes=[G1.b])
            dma("sp", lambda e: e.dma_start(out=B1[:], in_=ln1b[0:1, :].partition_broadcast(128)), B1.b, writes=[B1.b])
            wc = 0
            pc = 0
            hc_ = 0
            for ti in range(SO // 256):
                t0 = ti * 256
                dma("sp", lambda e, t0=t0: e.dma_start(out=mT[:], in_=mergedT.rearrange("c p t -> p c t")[:, :, t0:t0 + 256]),
                    mT.b, writes=[mT.b])
                for tc in range(2):
                    dma("sp", lambda e, tc=tc, t0=t0: e.dma_start(out=xr[tc][:], in_=x_own[t0 + tc * 128:t0 + (tc + 1) * 128, :]),
                        xr[tc].b, writes=[xr[tc].b])
                for cg in range(D // 512):
                    w = wo[wc % 2]
                    wc += 1
                    wv = w_out_b.rearrange("(kc p) c -> p kc c", p=128)
                    for k0 in range(0, KC, 8):
                        k1_ = min(KC, k0 + 8)
                        dma("sp", lambda e, w=w, k0=k0, k1_=k1_, cg=cg, wv=wv: e.dma_start(out=w[:, k0:k1_, :], in_=wv[:, k0:k1_, cg * 512:(cg + 1) * 512]),
                            w.b, reads=[CB["out"]], writes=[w.b])
                    for tc in range(2):
                        p = PS[pc % 4]
                        pc += 1
                        for kc in range(KC):
                            op("pe", lambda e, p=p, w=w, kc=kc, tc=tc: e.matmul(p[:], mT[:, kc, tc * 128:(tc + 1) * 128], w[:, kc, :],
                                                                              start=(kc == 0), stop=(kc == KC - 1)),
                               reads=[mT.b, w.b], writes=[p.b])
                        op("dve", lambda e, p=p, tc=tc, cg=cg: e.scalar_tensor_tensor(
                            out=xr[tc][:, cg * 512:(cg + 1) * 512], in0=xr[tc][:, cg * 512:(cg + 1) * 512], scalar=float(alpha), in1=p[:],
                            op0=ALU.mult, op1=ALU.add), reads=[xr[tc].b, p.b], writes=[xr[tc].b])
                for tc in range(2):
                    layer_norm(xr[tc], G1, B1, junk, stat)
                    r0 = t0 + tc * 128
                    dma("pool", lambda e, tc=tc, r0=r0: e.dma_start(out=h1_scr[r0:r0 + 128, :], in_=xr[tc][:]), xr[tc].b, reads=[xr[tc].b])
                    ht = hT[hc_ % 2]
                    hc_ += 1
                    for k0 in range(0, KC, 4):
                        p = PS[4 + (pc % 4)]
                        pc += 1
                        for kk in range(4):
                            kc = k0 + kk
                            op("pe", lambda e, p=p, kk=kk, kc=kc, tc=tc: e.transpose(p[:, kk * 128:(kk + 1) * 128], xr[tc][:, kc * 128:(kc + 1) * 128], ident[:]),
                               reads=[xr[tc].b, ident.b], writes=[p.b])
                        op("act", lambda e, p=p, ht=ht, k0=k0: e.copy(ht[:, k0:k0 + 4, :], p[:].rearrange("p (a t) -> p a t", a=4)),
                           reads=[p.b], writes=[ht.b])
                    dma("pool", lambda e, ht=ht, r0=r0: e.dma_start(out=h1T_scr.rearrange("c p t -> p c t")[:, :, r0:r0 + 128], in_=ht[:]),
                        ht.b, reads=[ht.b])
            S_.barrier()
            S_.release([mT.b, G1.b, B1.b] + [t.b for t in xr + wo + hT])

        with contextlib.ExitStack() as ph:
            sk = T(ph, "sk", [128, HC, NKEYS], F32)
            dma("sp", lambda e: e.dma_start(out=sk[:], in_=skT.rearrange("h p k -> p h k")), sk.b, writes=[sk.b])
            hTt = T(ph, "hTt", [128, KC, 512], BF16)
            wq = [T(ph, f"wq{i}", [128, KC, 512], BF16) for i in range(2)]
            qpT = T(ph, "qpT", [128, HC, 512], F32)
            scst = [T(ph, f"scst{i}", [128, HC, NKEYS], F32) for i in range(2)]
            wc = 0
            pc = 0
            scn = 0
            for ti in range(NQ):
                t0 = ti * 512
                dma("sp", lambda e, t0=t0: e.dma_start(out=hTt[:], in_=h1T_scr.rearrange("c p t -> p c t")[:, :, t0:t0 + 512]),
                    hTt.b, writes=[hTt.b])
                for g in range(QW // 512):
                    w = wq[wc % 2]
                    wc += 1
                    wv = w_q_b.rearrange("(kc p) c -> p kc c", p=128)
                    for k0 in range(0, KC, 8):
                        k1_ = min(KC, k0 + 8)
                        dma("sp", lambda e, w=w, k0=k0, k1_=k1_, g=g, wv=wv: e.dma_start(out=w[:, k0:k1_, :], in_=wv[:, k0:k1_, g * 512:(g + 1) * 512]),
                            w.b, reads=[CB["wq"]], writes=[w.b])
                    for cb in range(4):
                        hcx = g * 4 + cb
                        p = PS[pc % 4]
                        pc += 1
                        for kc in range(KC):
                            op("pe", lambda e, p=p, w=w, kc=kc, cb=cb: e.matmul(p[:], w[:, kc, cb * 128:(cb + 1) * 128], hTt[:, kc, :],
                                                                              start=(kc == 0), stop=(kc == KC - 1)),
                               reads=[w.b, hTt.b], writes=[p.b])
                        op("act", lambda e, p=p, hcx=hcx: e.copy(qpT[:, hcx, :], p[:]), reads=[p.b], writes=[qpT.b])
                for tc in range(4):
                    r0 = t0 + tc * 128
                    sct = scst[scn % 2]
                    scn += 1
                    for g4 in range(0, HC, 4):
                        p = PS[4 + (pc % 4)]
                        pc += 1
                        for kk in range(4):
                            hcx = g4 + kk
                            op("pe", lambda e, p=p, kk=kk, hcx=hcx, tc=tc: e.matmul(p[:, kk * 128:(kk + 1) * 128], qpT[:, hcx, tc * 128:(tc + 1) * 128],
                                                                                  sk[:, hcx, :], start=True, stop=True),
                               reads=[qpT.b, sk.b], writes=[p.b])
                        op("act", lambda e, p=p, g4=g4, sct=sct: e.copy(sct[:, g4:g4 + 4, :], p[:].rearrange("p (a t) -> p a t", a=4)),
                           reads=[p.b], writes=[sct.b])
                    dma("pool", lambda e, sct=sct, r0=r0: e.dma_start(out=sc_scr[r0:r0 + 128, :], in_=sct[:].rearrange("p h k -> p (h k)")),
                        sct.b, reads=[sct.b])
            S_.barrier()
            S_.release([sk.b, hTt.b] + [t.b for t in wq + scst])

        with contextlib.ExitStack() as ph:
            G2 = T(ph, "G2", [128, D], F32)
            B2 = T(ph, "B2", [128, D], F32)
            dma("sp", lambda e: e.dma_start(out=G2[:], in_=ln2g[0:1, :].partition_broadcast(128)), G2.b, writes=[G2.b])
            dma("sp", lambda e: e.dma_start(out=B2[:], in_=ln2b[0:1, :].partition_broadcast(128)), B2.b, writes=[B2.b])
            identb = T(ph, "identb", [128, 128], BF16)
            dma("pool", lambda e: e.dma_start(out=identb[:], in_=c_ident), identb.b, writes=[identb.b])
            sc = T(ph, "sc", [128, HC, NKEYS], F32)
            top = T(ph, "top", [128, PH, 2, TOPK], F32)
            topi = T(ph, "topi", [128, PH, 2, TOPK], U32)
            topf = T(ph, "topf", [128, PH, 2, TOPK], F32)
            cand = T(ph, "cand", [128, PH, TOPK, TOPK], F32)
            cidx = T(ph, "cidx", [128, PH, TOPK, TOPK], F32)
            cwork = T(ph, "cwork", [128, PH, TOPK * TOPK], F32)
            best = T(ph, "best", [128, PH, TOPK], F32)
            junk2 = T(ph, "junk2", [128, TOPK * TOPK], F32)
            idxf = T(ph, "idxf", [128, NSEL], F32)
            dd = T(ph, "dd", [128, PH, TOPK], F32)
            eb = T(ph, "eb", [128, PH, TOPK], F32)
            zs = T(ph, "zs", [128, PH], F32)
            idxi_ = [T(ph, f"idxi{i}", [128, NSEL], I32) for i in range(2)]
            gate_ = [T(ph, f"gate{i}", [128, PH, TOPK], F32) for i in range(2)]
            hb = [T(ph, f"hb{i}", [128, D], BF16) for i in range(2)]
            acc = T(ph, "acc", [128, D], F32)
            junkb = T(ph, "junkb", [128, D], BF16)
            stat = T(ph, "statE", [128, 4], F32)
            NRG = 5
            ring = [T(ph, f"ring{i}", [128, 2 * D], BF16) for i in range(NRG)]
            dgs = [T(ph, f"dgs{i}", [128, 128], BF16) for i in range(4)]
            gel = [T(ph, f"gel{i}", [128, 2], F32) for i in range(4)]
            NCH = SO // 128
            NDG = D // 512
            cnt = dict(r=0, d=0)

            def e2_s1(c):
                par = c % 2
                r0 = c * 128
                idxi, gate = idxi_[par], gate_[par]
                workv = cwork[:].rearrange("p h (c k) -> p (h c) k", c=2)
                dma("pool", lambda e: e.dma_start(out=hb[par][:], in_=h1_scr[r0:r0 + 128, :]), hb[par].b, writes=[hb[par].b])
                dma("sp", lambda e: e.dma_start(out=sc[:].rearrange("p h k -> p (h k)"), in_=sc_scr[r0:r0 + 128, :]), sc.b, writes=[sc.b])
                for hcx in range(HC):
                    hh, c2 = hcx // 2, hcx % 2
                    op("dve", lambda e, hh=hh, c2=c2, hcx=hcx: e.max(out=top[:, hh, c2, 0:8], in_=sc[:, hcx, :]),
                       reads=[sc.b], writes=[top.b])
                    op("dve", lambda e, hh=hh, c2=c2, hcx=hcx: e.max_index(out=topi[:, hh, c2, 0:8], in_max=top[:, hh, c2, 0:8], in_values=sc[:, hcx, :]),
                       reads=[sc.b, top.b], writes=[topi.b])
                    op("dve", lambda e, hh=hh, c2=c2, hcx=hcx: e.match_replace(out=workv[:, hcx, :], in_to_replace=top[:, hh, c2, 0:8],
                                                                               in_values=sc[:, hcx, :], imm_value=-1e30),
                       reads=[sc.b, top.b], writes=[cwork.b])
                    op("dve", lambda e, hh=hh, c2=c2, hcx=hcx: e.max(out=top[:, hh, c2, 8:16], in_=workv[:, hcx, :]),
                       reads=[cwork.b], writes=[top.b])
                    op("dve", lambda e, hh=hh, c2=c2, hcx=hcx: e.max_index(out=topi[:, hh, c2, 8:16], in_max=top[:, hh, c2, 8:16], in_values=workv[:, hcx, :]),
                       reads=[cwork.b, top.b], writes=[topi.b])
                op("dve", lambda e: e.tensor_copy(topf[:], topi[:]), reads=[topi.b], writes=[topf.b])
                bshape = [128, TOPK, TOPK]
                for hh in range(PH):
                    op("dve", lambda e, hh=hh: e.tensor_tensor(out=cand[:, hh, :, :], in0=top[:, hh, 0, :].unsqueeze(2).to_broadcast(bshape),
                                                               in1=top[:, hh, 1, :].unsqueeze(1).to_broadcast(bshape), op=ALU.add),
                       reads=[top.b], writes=[cand.b])
                    op("dve", lambda e, hh=hh: e.scalar_tensor_tensor(out=cidx[:, hh, :, :], in0=topf[:, hh, 0, :].unsqueeze(2).to_broadcast(bshape),
                                                                      scalar=float(NKEYS), in1=topf[:, hh, 1, :].unsqueeze(1).to_broadcast(bshape),
                                                                      op0=ALU.mult, op1=ALU.add),
                       reads=[topf.b], writes=[cidx.b])
                for hh in range(PH):
                    cv = cand[:, hh, :, :].rearrange("p a b -> p (a b)")
                    op("dve", lambda e, hh=hh, cv=cv: e.max(out=best[:, hh, 0:8], in_=cv), reads=[cand.b], writes=[best.b])
                    op("dve", lambda e, hh=hh, cv=cv: e.match_replace(out=cwork[:, hh, :], in_to_replace=best[:, hh, 0:8], in_values=cv, imm_value=-1e30),
                       reads=[cand.b, best.b], writes=[cwork.b])
                    op("dve", lambda e, hh=hh: e.max(out=best[:, hh, 8:16], in_=cwork[:, hh, :]), reads=[cwork.b], writes=[best.b])
                for hh in range(PH):
                    cv = cand[:, hh, :, :].rearrange("p a b -> p (a b)")
                    iv = cidx[:, hh, :, :].rearrange("p a b -> p (a b)")
                    for k in range(TOPK):
                        n = hh * TOPK + k
                        op("dve", lambda e, hh=hh, k=k, n=n, cv=cv, iv=iv: e.scalar_tensor_tensor(
                            out=junk2[:], in0=cv, scalar=best[:, hh, k:k + 1], in1=iv, op0=ALU.is_equal, op1=ALU.mult),
                           reads=[cand.b, cidx.b, best.b], writes=[junk2.b])
                        op("dve", lambda e, n=n: e.tensor_reduce(out=idxf[:, n:n + 1], in_=junk2[:], axis=AX.X, op=ALU.max),
                           reads=[junk2.b, idxf.b], writes=[idxf.b])
                op("dve", lambda e: e.tensor_copy(idxi[:], idxf[:]), reads=[idxf.b], writes=[idxi.b])
                op("dve", lambda e: e.tensor_tensor(out=dd[:], in0=best[:], in1=best[:, :, 0:1].to_broadcast([128, PH, TOPK]), op=ALU.subtract),
                   reads=[best.b], writes=[dd.b])
                op("act", lambda e: e.activation(eb[:], dd[:], AF.Exp), reads=[dd.b], writes=[eb.b])
                op("dve", lambda e: e.tensor_reduce(out=zs[:], in_=eb[:], axis=AX.X, op=ALU.add), reads=[eb.b], writes=[zs.b])
                op("dve", lambda e: e.reciprocal(zs[:], zs[:]), reads=[zs.b], writes=[zs.b])
                op("dve", lambda e: e.tensor_tensor(out=gate[:], in0=eb[:], in1=zs[:].unsqueeze(2).to_broadcast([128, PH, TOPK]), op=ALU.mult),
                   reads=[eb.b, zs.b], writes=[gate.b])

            def e2_n(c, n):
                par = c % 2
                rg = ring[cnt["r"] % NRG]
                cnt["r"] += 1
                dg = dgs[cnt["d"] % 4]
                ge = gel[cnt["d"] % 4]
                cnt["d"] += 1
                idxi = idxi_[par]
                gflat = gate_[par][:].rearrange("p h k -> p (h k)")
                dma("pool", lambda e: e.indirect_dma_start(
                    out=rg[:], out_offset=None, in_=puv_b, in_offset=bass.IndirectOffsetOnAxis(ap=idxi[:, n:n + 1], axis=0)),
                    rg.b, reads=[idxi.b, CB["puv"]], writes=[rg.b])
                op("dve", lambda e: e.memset(ge[:, 0:1], 0.0), writes=[ge.b])
                op("dve", lambda e: e.scalar_tensor_tensor(out=rg[:, 0:D], in0=rg[:, 0:D], scalar=1.0, in1=hb[par][:], op0=ALU.mult, op1=ALU.mult,
                                                           accum_out=ge[:, 0:1]),
                   reads=[rg.b, hb[par].b, ge.b], writes=[rg.b, ge.b])
                op("act", lambda e: e.activation(ge[:, 1:2], ge[:, 0:1], AF.Gelu), reads=[ge.b], writes=[ge.b])
                op("act", lambda e: e.activation(ge[:, 1:2], ge[:, 1:2], AF.Copy, scale=gflat[:, n:n + 1]),
                   reads=[ge.b, gate_[par].b], writes=[ge.b])
                op("act", lambda e: e.activation(dg[:], identb[:], AF.Copy, scale=ge[:, 1:2]),
                   reads=[identb.b, ge.b], writes=[dg.b])
                for g in range(NDG):
                    op("pe", lambda e, g=g: e.matmul(PS[g][:], dg[:], rg[:, D + g * 512:D + (g + 1) * 512], start=(n == 0), stop=(n == NSEL - 1)),
                       reads=[dg.b, rg.b], writes=[PS[g].b])

            def e2_post(c):
                r0 = c * 128
                dma("sp", lambda e: e.dma_start(out=acc[:], in_=h1_scr[r0:r0 + 128, :]), acc.b, writes=[acc.b])
                for g in range(NDG):
                    op("dve", lambda e, g=g: e.scalar_tensor_tensor(out=acc[:, g * 512:(g + 1) * 512], in0=acc[:, g * 512:(g + 1) * 512],
                                                                    scalar=float(alpha), in1=PS[g][:], op0=ALU.mult, op1=ALU.add),
                       reads=[acc.b, PS[g].b], writes=[acc.b])
                layer_norm(acc, G2, B2, junkb, stat)
                dma("sp", lambda e: e.dma_start(out=out_own[r0:r0 + 128, :], in_=acc[:]), acc.b, reads=[acc.b])

            e2_s1(0)
            for c in range(NCH):
                for n in range(NSEL):
                    e2_n(c, n)
                    if n == NSEL // 2 and c + 1 < NCH:
                        e2_s1(c + 1)
                e2_post(c)
            S_.barrier(include_bg=True)

        S_.emit()
    return nc


def make_consts(j):
    ident = np.eye(128, dtype=np.float32)
    jj = np.arange(128)[:, None]
    ss = np.arange(128)[None, :]
    tri = (jj > ss).astype(np.float32)
    ones = np.ones((128, 128), np.float32)
    perm = np.zeros((128, 128), np.float32)
    for m in range(128):
        perm[(m + 64) % 128, m] = 1.0
    inv_freq = (ROPE_THETA ** (-np.arange(0, 128, 2, dtype=np.float32) / np.float32(128))).astype(np.float32)
    rope = np.zeros((128, 3), np.float32)
    rope[:, 0] = np.concatenate([inv_freq, inv_freq])
    rope[:64, 1] = -2.0 * math.pi
    rope[64:, 1] = 2.0 * math.pi
    s = np.arange(128)[:, None, None]
    r = np.arange(8)[None, :, None]
    t = np.arange(512)[None, None, :]
    kpos = r * 128 + s
    qpos = j * 512 + t
    maskS = (kpos < qpos).astype(np.float32).reshape(128, 8 * 512).astype(ml_dtypes.bfloat16)
    maskD = (kpos <= qpos).astype(np.float32).reshape(128, 8 * 512).astype(ml_dtypes.bfloat16)
    return dict(c_ident=ident, c_tri=tri, c_ones=ones, c_perm=perm, c_rope=rope, c_maskS=maskS, c_maskD=maskD)


def prep_inputs(inp, cfg):
    D, S, NB, PH = cfg["D"], cfg["S"], cfg["NB"], cfg["PH"]
    KC = D // 128
    f = lambda a: np.ascontiguousarray(np.asarray(a))
    x = f(inp["x"])
    pos = f(inp["positions"]).astype(np.int32)
    shared = dict(
        w_in=f(inp["w_in"][0]),
        bgT=f(np.asarray(inp["b_gate"][0]).reshape(2 * KC, 128).T),
        lamv=f(np.stack([np.asarray(inp["lambda_q1"][0]), np.asarray(inp["lambda_k1"][0]),
                         np.asarray(inp["lambda_q2"][0]), np.asarray(inp["lambda_k2"][0])])),
        sublnT=f(np.asarray(inp["subln_g"][0]).reshape(2, 128).T),
        w_sbb=f(inp["w_sb_branch"][0]), w_dab=f(inp["w_da_branch"][0]), w_out=f(inp["w_out"][0]),
        ln1g=f(np.asarray(inp["ln1_g"][0])[None, :]), ln1b=f(np.asarray(inp["ln1_b"][0])[None, :]),
        ln2g=f(np.asarray(inp["ln2_g"][0])[None, :]), ln2b=f(np.asarray(inp["ln2_b"][0])[None, :]),
        w_q=f(inp["peer_w_q"][0]),
        skT=f(np.asarray(inp["peer_sub_keys"][0]).reshape(2 * PH, NKEYS, 128).transpose(0, 2, 1)),
        pu=f(inp["peer_u"][0]), pv=f(inp["peer_v"][0]),
    )
    consts = [make_consts(0), make_consts(1)]
    in_maps = []
    for c in range(2 * NB):
        b, j = c // 2, c % 2
        tiles = [2 * i + j for i in range(S // 1024)]
        rows = np.concatenate([np.arange(t * 512, (t + 1) * 512) for t in tiles])
        xb = x[b]
        m = dict(shared)
        m.update(consts[j])
        m["xT_all"] = f(xb.T)
        m["x_own"] = f(xb[rows])
        m["xT_own"] = f(xb[rows].T)
        m["pos_all"] = f(pos[b][None, :])
        m["pos_own"] = f(pos[b][rows][None, :])
        in_maps.append(m)
    return in_maps


def assemble(results, cfg):
    D, S, NB = cfg["D"], cfg["S"], cfg["NB"]
    out = np.zeros((NB, S, D), np.float32)
    for c in range(2 * NB):
        b, j = c // 2, c % 2
        tiles = [2 * i + j for i in range(S // 1024)]
        rows = np.concatenate([np.arange(t * 512, (t + 1) * 512) for t in tiles])
        out[b, rows] = results[c]["out_own"]
    return out


def run(inputs, cfg):
    nc = build(cfg)
    in_maps = prep_inputs(inputs, cfg)
    res = run_bass_kernel_spmd(nc, in_maps, core_ids=list(range(2 * cfg["NB"])))
    return assemble(res.results, cfg)


def kernel(**inputs):
    return run(inputs, FULL_CFG)
```

```python
import contextlib
import math
import numpy as np
import ml_dtypes
import concourse.bass as bass
import concourse.mybir as mybir
from concourse.bass_utils import run_bass_kernel_spmd

F32 = mybir.dt.float32
BF16 = mybir.dt.bfloat16
I32 = mybir.dt.int32
U32 = mybir.dt.uint32
AF = mybir.ActivationFunctionType
ALU = mybir.AluOpType
AX = mybir.AxisListType

SEM_ROT = 30000
LN_EPS = 1e-5
NKEYS = 128
TOPK = 16
NEXP = NKEYS * NKEYS
ROPE_THETA = 10000.0

FULL_CFG = dict(D=4096, S=4096, NB=4, SBH=16, DAH=8, PH=8, DEPTH=1)


class Buf:
    __slots__ = ("name", "w", "r", "ds")

    def __init__(self, name=""):
        self.name = name
        self.w = None
        self.r = {}
        self.ds = None


class DSem:
    def __init__(self, handle):
        self.h = handle
        self.total = 0
        self.bg = False


class Sched:
    ENGS = ("pe", "act", "dve", "pool", "sp")

    def __init__(self, nc, stack):
        self.nc = nc
        self.stack = stack
        self.streams = {e: [] for e in self.ENGS}
        self.esem = {}
        self.ecount = {}
        self.known = {e: {} for e in self.ENGS}
        self.nsem = 0
        self.dsems = []
        self.free_ds = []
        for e in ("pe", "act", "dve", "pool"):
            self._new_esem(e)

    def _alloc(self, name):
        self.nsem += 1
        return self.stack.enter_context(self.nc.semaphore(f"{name}_{self.nsem}"))

    def _new_esem(self, e):
        self.esem[e] = self._alloc(f"es_{e}")
        self.ecount[e] = 0

    def dsem(self, name="d"):
        if self.free_ds:
            return self.free_ds.pop()
        d = DSem(self._alloc(f"ds_{name}"))
        self.dsems.append(d)
        return d

    def release(self, bufs):
        for b in bufs:
            if b.ds is not None:
                self.free_ds.append(b.ds)
                b.ds = None

    def _collect(self, eng, reads, writes, dma=False):
        need = {}

        def add(tok, kind):
            if tok is None:
                return
            semh, val, teng, ds = tok
            if teng == eng and teng is not None and not dma:
                if eng == "pe":
                    return
                if kind == "war":
                    return
            if ds is not None:
                val = ds.total
            k = id(semh)
            if k not in need or need[k][1] < val:
                need[k] = (semh, val)

        for b in reads:
            add(b.w, "raw")
        for b in writes:
            add(b.w, "waw")
            for t in b.r.values():
                add(t, "war")
        return self._filter(eng, need.values())

    def _filter(self, eng, pairs):
        waits = []
        kn = self.known[eng]
        for semh, val in pairs:
            k = id(semh)
            if kn.get(k, 0) >= val:
                continue
            kn[k] = val
            waits.append((semh, val))
        return waits

    def _commit(self, tok, reads, writes):
        k = id(tok[0])
        for b in reads:
            b.r[k] = tok
        for b in writes:
            b.w = tok
            b.r = {}

    def op(self, eng, fn, reads=(), writes=()):
        waits = self._collect(eng, reads, writes)
        if self.ecount[eng] >= SEM_ROT:
            self._new_esem(eng)
        self.ecount[eng] += 1
        semh = self.esem[eng]
        tok = (semh, self.ecount[eng], eng, None)
        self.streams[eng].append((waits, fn, (semh, 1)))
        self._commit(tok, reads, writes)
        return tok

    def dma(self, q, fn, owner, reads=(), writes=()):
        if owner.ds is None:
            owner.ds = self.dsem(owner.name)
        ds = owner.ds
        waits = self._collect(q, reads, writes, dma=True)
        ds.total += 16
        tok = (ds.h, ds.total, None, ds)
        self.streams[q].append((waits, fn, (ds.h, 16)))
        self._commit(tok, reads, writes)
        return tok

    def barrier(self, include_bg=False):
        pairs = [(self.esem[e], self.ecount[e]) for e in ("pe", "act", "dve", "pool") if self.ecount[e] > 0]
        pairs += [(d.h, d.total) for d in self.dsems if d.total > 0 and (include_bg or not d.bg)]
        for e in self.ENGS:
            w = self._filter(e, pairs)
            if w:
                self.streams[e].append((w, None, None))

    def emit(self):
        nc = self.nc
        streams = self.streams

        def run(engh, items):
            for waits, fn, inc in items:
                for semh, val in waits:
                    engh.wait_ge(semh, val)
                if fn is None:
                    continue
                ins = fn(engh)
                if inc is not None:
                    ins.then_inc(inc[0], inc[1])

        with nc.Block() as block:
            @block.tensor
            def _(e):
                run(e, streams["pe"])

            @block.scalar
            def _(e):
                run(e, streams["act"])

            @block.vector
            def _(e):
                run(e, streams["dve"])

            @block.gpsimd
            def _(e):
                run(e, streams["pool"])

            @block.sync
            def _(e):
                run(e, streams["sp"])


def build(cfg):
    D, S, SBH, DAH, PH = cfg["D"], cfg["S"], cfg["SBH"], cfg["DAH"], cfg["PH"]
    DEPTH = cfg["DEPTH"]
    assert DEPTH == 1
    KC = D // 128
    SO = S // 2
    SBW = SBH * 128
    DAW = DAH * 256
    INW = 3 * SBW + 3 * DAW + 2 * D
    c_sbq, c_sbk, c_sbv = 0, SBW, 2 * SBW
    c_daq, c_dak, c_dav = 3 * SBW, 3 * SBW + DAW, 3 * SBW + 2 * DAW
    c_g = 3 * SBW + 3 * DAW
    NQ = SO // 512
    HC = 2 * PH
    NSEL = PH * TOPK
    QW = PH * 256
    alpha = (2 * DEPTH) ** 0.25
    lam_init = 0.8 - 0.6 * math.exp(-0.3 * 0)
    scale = 128 ** -0.5
    PI = math.pi

    nc = bass.Bass("TRN2", target_bir_lowering=False)

    def din(name, shape, dt=F32):
        return nc.dram_tensor(name, list(shape), dt, kind="ExternalInput").ap()

    xT_all = din("xT_all", [D, S])
    xT_own = din("xT_own", [D, SO])
    x_own = din("x_own", [SO, D])
    pos_all = din("pos_all", [1, S], I32)
    pos_own = din("pos_own", [1, SO], I32)
    w_in = din("w_in", [D, INW])
    bgT = din("bgT", [128, 2 * KC])
    lamv = din("lamv", [4, 128])
    sublnT = din("sublnT", [128, 2])
    w_sbb = din("w_sbb", [SBW, D])
    w_dab = din("w_dab", [DAW, D])
    w_out = din("w_out", [D, D])
    ln1g = din("ln1g", [1, D]); ln1b = din("ln1b", [1, D])
    ln2g = din("ln2g", [1, D]); ln2b = din("ln2b", [1, D])
    w_q = din("w_q", [D, QW])
    skT = din("skT", [HC, 128, NKEYS])
    pu = din("pu", [NEXP, D])
    pv = din("pv", [NEXP, D])
    c_ident = din("c_ident", [128, 128])
    c_tri = din("c_tri", [128, 128])
    c_ones = din("c_ones", [128, 128])
    c_perm = din("c_perm", [128, 128])
    c_rope = din("c_rope", [128, 3])
    c_maskS = din("c_maskS", [128, 8 * 512], BF16)
    c_maskD = din("c_maskD", [128, 8 * 512], BF16)
    out_own = nc.dram_tensor("out_own", [SO, D], F32, kind="ExternalOutput").ap()

    def scr(name, shape, dt):
        return nc.dram_tensor(name, list(shape), dt).ap()

    KsbT = scr("KsbT", [SBH, 128, S], BF16)
    Vsb = scr("Vsb", [S, SBW], BF16)
    KdaT = scr("KdaT", [2 * DAH, 128, S], BF16)
    Vda = scr("Vda", [S, DAW], BF16)
    QsbT = scr("QsbT", [SBH, 128, SO], BF16)
    QdaT = scr("QdaT", [2 * DAH, 128, SO], BF16)
    OsbT = scr("OsbT", [SBH, 128, SO], BF16)
    OdaT = scr("OdaT", [2 * DAH, 128, SO], BF16)
    mergedT = scr("mergedT", [KC, 128, SO], BF16)
    h1_scr = scr("h1_scr", [SO, D], F32)
    h1T_scr = scr("h1T_scr", [KC, 128, SO], BF16)
    sc_scr = scr("sc_scr", [SO, 2 * PH * NKEYS], F32)
    w_in_b = scr("w_in_b", [D, INW], BF16)
    w_sbb_b = scr("w_sbb_b", [SBW, D], BF16)
    w_dab_b = scr("w_dab_b", [DAW, D], BF16)
    w_out_b = scr("w_out_b", [D, D], BF16)
    w_q_b = scr("w_q_b", [D, QW], BF16)
    puv_b = scr("puv_b", [NEXP, 2 * D], BF16)

    with contextlib.ExitStack() as st:
        S_ = Sched(nc, st)
        op = S_.op
        dma = S_.dma

        class T:
            def __init__(self, stack, name, shape, dt, psum=False):
                if psum:
                    self.t = stack.enter_context(nc.psum_tensor(name, list(shape), dt))
                else:
                    self.t = stack.enter_context(nc.sbuf_tensor(name, list(shape), dt))
                self.b = Buf(name)

            def __getitem__(self, k):
                return self.t[k]

        PS = [T(st, f"ps{i}", [128, 512], F32, psum=True) for i in range(8)]
        ident = T(st, "ident", [128, 128], F32)
        tri = T(st, "tri", [128, 128], F32)
        ones = T(st, "ones", [128, 128], F32)
        onesb = T(st, "onesb", [128, 128], BF16)
        trib = T(st, "trib", [128, 128], BF16)
        perm = T(st, "perm", [128, 128], F32)
        rope_c = T(st, "rope_c", [128, 3], F32)
        epsc = T(st, "epsc", [128, 1], F32)
        lam = T(st, "lam", [128, 1], F32)

        dma("sp", lambda e: e.dma_start(out=ident[:], in_=c_ident), ident.b, writes=[ident.b])
        dma("sp", lambda e: e.dma_start(out=tri[:], in_=c_tri), tri.b, writes=[tri.b])
        dma("sp", lambda e: e.dma_start(out=ones[:], in_=c_ones), ones.b, writes=[ones.b])
        dma("pool", lambda e: e.dma_start(out=onesb[:], in_=c_ones), onesb.b, writes=[onesb.b])
        dma("pool", lambda e: e.dma_start(out=trib[:], in_=c_tri), trib.b, writes=[trib.b])
        dma("sp", lambda e: e.dma_start(out=perm[:], in_=c_perm), perm.b, writes=[perm.b])
        dma("sp", lambda e: e.dma_start(out=rope_c[:], in_=c_rope), rope_c.b, writes=[rope_c.b])
        op("dve", lambda e: e.memset(epsc[:], LN_EPS), writes=[epsc.b])

        with contextlib.ExitStack() as ph:
            lv = T(ph, "lv", [128, 4, 128], F32)
            lp = T(ph, "lp", [128, 2, 128], F32)
            ls = T(ph, "ls", [128, 2], F32)
            le = T(ph, "le", [128, 2], F32)
            for r in range(4):
                dma("sp", lambda e, r=r: e.dma_start(out=lv[:, r, :], in_=lamv[r:r + 1, :].partition_broadcast(128)),
                    lv.b, writes=[lv.b])
            op("dve", lambda e: e.tensor_tensor(out=lp[:, 0, :], in0=lv[:, 0, :], in1=lv[:, 1, :], op=ALU.mult),
               reads=[lv.b], writes=[lp.b])
            op("dve", lambda e: e.tensor_tensor(out=lp[:, 1, :], in0=lv[:, 2, :], in1=lv[:, 3, :], op=ALU.mult),
               reads=[lv.b], writes=[lp.b])
            op("dve", lambda e: e.tensor_reduce(out=ls[:], in_=lp[:], axis=AX.X, op=ALU.add), reads=[lp.b], writes=[ls.b])
            op("act", lambda e: e.activation(le[:], ls[:], AF.Exp), reads=[ls.b], writes=[le.b])
            op("dve", lambda e: e.tensor_tensor(out=lam[:], in0=le[:, 0:1], in1=le[:, 1:2], op=ALU.subtract),
               reads=[le.b], writes=[lam.b])
            op("dve", lambda e: e.tensor_scalar(out=lam[:], in0=lam[:], scalar1=float(lam_init), scalar2=None, op0=ALU.add),
               reads=[lam.b], writes=[lam.b])
            S_.barrier()
            S_.release([lv.b])

        CB = {}

        pending = []

        def precast(key, dst, src, r_tot, c0, c1, rblk=1024, dc0=None, defer=False):
            if key not in CB:
                CB[key] = Buf("cb_" + key)
                CB[key].ds = S_.dsem("cb_" + key)
                CB[key].ds.bg = True
            b = CB[key]
            if dc0 is None:
                dc0 = c0
            for r in range(0, r_tot, rblk):
                r1 = min(r_tot, r + rblk)
                th = (lambda r=r, r1=r1: dma("pool", lambda e: e.dma_start(out=dst[r:r1, dc0:dc0 + (c1 - c0)], in_=src[r:r1, c0:c1]), b, writes=[b]))
                th.key = key
                if defer:
                    pending.append(th)
                else:
                    th()

        def flush_pending(k=None):
            n = len(pending) if k is None else min(k, len(pending))
            for _ in range(n):
                pending.pop(0)()

        precast("sbkv", w_in_b, w_in, D, c_sbk, c_sbv + SBW, defer=True)
        precast("dakv", w_in_b, w_in, D, c_dak, c_dav + DAW, defer=True)
        precast("q", w_in_b, w_in, D, c_sbq, c_sbq + SBW, defer=True)
        precast("q", w_in_b, w_in, D, c_daq, c_daq + DAW, defer=True)
        pend_e = pending[:]
        del pending[:]
        precast("sbb", w_sbb_b, w_sbb, SBW, 0, D, defer=True)
        precast("dab", w_dab_b, w_dab, DAW, 0, D, defer=True)
        precast("out", w_out_b, w_out, D, 0, D, defer=True)
        precast("wq", w_q_b, w_q, D, 0, QW, defer=True)
        n_w_pending = len(pending)
        precast("puv", puv_b, pu, NEXP, 0, D, dc0=0, defer=True)
        precast("puv", puv_b, pv, NEXP, 0, D, dc0=D, defer=True)
        pend_w = pending[:n_w_pending]
        pend_t = pending[n_w_pending:]
        del pending[:]

        with contextlib.ExitStack() as ph:
            xT = T(ph, "xT", [128, KC, 1024], BF16)
            posi = T(ph, "posi", [128, 1024], I32)
            ang = T(ph, "ang", [128, 1024], F32)
            m1 = T(ph, "m1", [128, 1024], F32)
            ki = T(ph, "ki", [128, 1024], I32)
            kf = T(ph, "kf", [128, 1024], F32)
            CT = T(ph, "CT", [128, 1024], F32)
            ST = T(ph, "ST", [128, 1024], F32)
            wsl = [T(ph, f"wA{i}", [128, KC, 512], BF16) for i in range(2)]
            stg = [T(ph, f"stA{i}", [128, 512], BF16) for i in range(4)]
            qf = [T(ph, f"qf{i}", [128, 512], F32) for i in range(2)]
            t1 = [T(ph, f"t1{i}", [128, 512], F32) for i in range(2)]
            t2 = [T(ph, f"t2{i}", [128, 512], F32) for i in range(2)]
            wcnt = [0]
            scnt = [0]
            pcnt = [0]
            rcnt = [0]

            def phaseA_pass(xsrc, possrc, ntiles, groups, first_pass=False):
                for tile in range(ntiles):
                    t0 = tile * 1024
                    xv = xsrc.rearrange("(kc p) t -> p kc t", p=128)
                    for k0 in range(0, KC, 4):
                        dma("pool", lambda e, k0=k0, t0=t0, xv=xv: e.dma_start(out=xT[:, k0:k0 + 4, :], in_=xv[:, k0:k0 + 4, t0:t0 + 1024]),
                            xT.b, writes=[xT.b])
                    dma("sp", lambda e, t0=t0, possrc=possrc: e.dma_start(out=posi[:], in_=possrc[0:1, t0:t0 + 1024].partition_broadcast(128)),
                        posi.b, writes=[posi.b])
                    op("dve", lambda e: e.tensor_copy(ang[:], posi[:]), reads=[posi.b], writes=[ang.b])
                    op("dve", lambda e: e.tensor_scalar(out=ang[:], in0=ang[:], scalar1=rope_c[:, 0:1], scalar2=None, op0=ALU.mult),
                       reads=[ang.b, rope_c.b], writes=[ang.b])
                    def range_red(add):
                        op("dve", lambda e: e.tensor_scalar(out=m1[:], in0=ang[:], scalar1=float(1.0 / (2 * PI)), scalar2=float(add),
                                                            op0=ALU.mult, op1=ALU.add), reads=[ang.b], writes=[m1.b])
                        op("dve", lambda e: e.tensor_copy(ki[:], m1[:]), reads=[m1.b], writes=[ki.b])
                        op("dve", lambda e: e.tensor_copy(kf[:], ki[:]), reads=[ki.b], writes=[kf.b])
                        op("dve", lambda e: e.tensor_tensor(out=m1[:], in0=m1[:], in1=kf[:], op=ALU.subtract), reads=[m1.b, kf.b], writes=[m1.b])
                        op("dve", lambda e: e.tensor_scalar(out=kf[:], in0=m1[:], scalar1=0.5, scalar2=None, op0=ALU.is_gt),
                           reads=[m1.b], writes=[kf.b])
                        op("dve", lambda e: e.tensor_tensor(out=m1[:], in0=m1[:], in1=kf[:], op=ALU.subtract), reads=[m1.b, kf.b], writes=[m1.b])
                    range_red(0.0)
                    op("act", lambda e: e.activation(ST[:], m1[:], AF.Sin, scale=rope_c[:, 1:2]),
                       reads=[m1.b, rope_c.b], writes=[ST.b])
                    range_red(0.25)
                    op("act", lambda e: e.activation(CT[:], m1[:], AF.Sin, scale=float(2 * PI)), reads=[m1.b], writes=[CT.b])

                    for (c0, kind, dst, hbase, rope, sc, ck) in groups:
                        direct = first_pass and tile == 0
                        if wcnt[0] >= 30 and wcnt[0] % 2 == 0 and pend_w:
                            pend_w.pop(0)()
                        w = wsl[wcnt[0] % 2]
                        wcnt[0] += 1
                        if direct:
                            wv = w_in.rearrange("(kc p) c -> p kc c", p=128)
                            for k0 in range(0, KC, 4):
                                dma("pool", lambda e, w=w, k0=k0, c0=c0, wv=wv: e.dma_start(out=w[:, k0:k0 + 4, :], in_=wv[:, k0:k0 + 4, c0:c0 + 512]),
                                    w.b, writes=[w.b])
                            for _ in range(2):
                                if pend_e:
                                    pend_e.pop(0)()
                        else:
                            while pend_e:
                                pend_e.pop(0)()
                            for th in [t_ for t_ in pend_w if t_.key == ck]:
                                pend_w.remove(th)
                                th()
                            wv = w_in_b.rearrange("(kc p) c -> p kc c", p=128)
                            for k0 in range(0, KC, 8):
                                k1_ = min(KC, k0 + 8)
                                dma("sp", lambda e, w=w, k0=k0, k1_=k1_, c0=c0, wv=wv: e.dma_start(out=w[:, k0:k1_, :], in_=wv[:, k0:k1_, c0:c0 + 512]),
                                    w.b, reads=[CB[ck]], writes=[w.b])
                        if kind == "fm":
                            for cb in range(4):
                                head = hbase + cb
                                for ts in range(2):
                                    p = PS[pcnt[0] % 4]
                                    pcnt[0] += 1
                                    for kc in range(KC):
                                        op("pe", lambda e, p=p, w=w, kc=kc, cb=cb, ts=ts: e.matmul(
                                            p[:], w[:, kc, cb * 128:(cb + 1) * 128], xT[:, kc, ts * 512:(ts + 1) * 512],
                                            start=(kc == 0), stop=(kc == KC - 1)),
                                           reads=[w.b, xT.b], writes=[p.b])
                                    sg = stg[scnt[0] % 4]
                                    scnt[0] += 1
                                    if not rope:
                                        op("act", lambda e, sg=sg, p=p, sc=sc: e.activation(sg[:], p[:], AF.Copy, scale=float(sc)),
                                           reads=[p.b], writes=[sg.b])
                                    else:
                                        r = rcnt[0] % 2
                                        rcnt[0] += 1
                                        p2 = PS[4 + r]
                                        op("act", lambda e, r=r, p=p, sc=sc: e.activation(qf[r][:], p[:], AF.Copy, scale=float(sc)),
                                           reads=[p.b], writes=[qf[r].b])
                                        op("pe", lambda e, r=r, p2=p2: e.matmul(p2[:], perm[:], qf[r][:], start=True, stop=True),
                                           reads=[perm.b, qf[r].b], writes=[p2.b])
                                        op("dve", lambda e, r=r, ts=ts: e.tensor_tensor(out=t1[r][:], in0=qf[r][:], in1=CT[:, ts * 512:(ts + 1) * 512], op=ALU.mult),
                                           reads=[qf[r].b, CT.b], writes=[t1[r].b])
                                        op("dve", lambda e, r=r, ts=ts, p2=p2: e.tensor_tensor(out=t2[r][:], in0=p2[:], in1=ST[:, ts * 512:(ts + 1) * 512], op=ALU.mult),
                                           reads=[p2.b, ST.b], writes=[t2[r].b])
                                        op("dve", lambda e, r=r, sg=sg: e.tensor_tensor(out=sg[:], in0=t1[r][:], in1=t2[r][:], op=ALU.add),
                                           reads=[t1[r].b, t2[r].b], writes=[sg.b])
                                    dma("pool", lambda e, sg=sg, dst=dst, head=head, t0=t0, ts=ts: e.dma_start(
                                        out=dst[head, :, t0 + ts * 512:t0 + (ts + 1) * 512], in_=sg[:]),
                                        sg.b, reads=[sg.b])
                        else:
                            for tc in range(8):
                                p = PS[pcnt[0] % 4]
                                pcnt[0] += 1
                                for kc in range(KC):
                                    op("pe", lambda e, p=p, w=w, kc=kc, tc=tc: e.matmul(
                                        p[:], xT[:, kc, tc * 128:(tc + 1) * 128], w[:, kc, :],
                                        start=(kc == 0), stop=(kc == KC - 1)),
                                       reads=[w.b, xT.b], writes=[p.b])
                                sg = stg[scnt[0] % 4]
                                scnt[0] += 1
                                if tc % 2 == 0:
                                    op("act", lambda e, sg=sg, p=p: e.copy(sg[:], p[:]), reads=[p.b], writes=[sg.b])
                                else:
                                    op("dve", lambda e, sg=sg, p=p: e.tensor_copy(sg[:], p[:]), reads=[p.b], writes=[sg.b])
                                dma("pool", lambda e, sg=sg, dst=dst, hbase=hbase, t0=t0, tc=tc: e.dma_start(
                                    out=dst[t0 + tc * 128:t0 + (tc + 1) * 128, hbase:hbase + 512], in_=sg[:]),
                                    sg.b, reads=[sg.b])

            kv_groups = []
            for g in range(SBW // 512):
                kv_groups.append((c_sbk + g * 512, "fm", KsbT, g * 4, False, 1.0, "sbkv"))
            for g in range(SBW // 512):
                kv_groups.append((c_sbv + g * 512, "tm", Vsb, g * 512, False, 1.0, "sbkv"))
            for g in range(DAW // 512):
                kv_groups.append((c_dak + g * 512, "fm", KdaT, g * 4, True, 1.0, "dakv"))
            for g in range(DAW // 512):
                kv_groups.append((c_dav + g * 512, "tm", Vda, g * 512, False, 1.0, "dakv"))
            q_groups = []
            for g in range(SBW // 512):
                q_groups.append((c_sbq + g * 512, "fm", QsbT, g * 4, False, scale, "q"))
            for g in range(DAW // 512):
                q_groups.append((c_daq + g * 512, "fm", QdaT, g * 4, True, scale, "q"))
            phaseA_pass(xT_all, pos_all, S // 1024, kv_groups, first_pass=True)
            phaseA_pass(xT_own, pos_own, SO // 1024, q_groups)
            while pend_w:
                pend_w.pop(0)()
            S_.barrier()
            S_.release([xT.b, posi.b] + [t.b for t in wsl] + [t.b for t in stg])

        NKB = S // 128
        with contextlib.ExitStack() as ph:
            maskS = T(ph, "maskS", [128, 8 * 512], BF16)
            dma("sp", lambda e: e.dma_start(out=maskS[:], in_=c_maskS), maskS.b, writes=[maskS.b])
            kT = [T(ph, f"kT{i}", [128, S], BF16) for i in range(2)]
            vv = [T(ph, f"vv{i}", [128, NKB, 128], BF16) for i in range(2)]
            qq = [T(ph, f"qq{i}", [128, SO], BF16) for i in range(2)]
            NE, NSP, NL, NLW, NW = 3, 6, 4, 3, 4
            e_sb = [T(ph, f"e_sb{i}", [128, 512], F32) for i in range(NE)]
            sp_sb = [T(ph, f"sp_sb{i}", [128, 512], F32) for i in range(NSP)]
            L_sb = [T(ph, f"L_sb{i}", [128, 512], BF16) for i in range(NL)]
            lw_sb = [T(ph, f"lw_sb{i}", [128, 512], F32) for i in range(NLW)]
            w_sb = [T(ph, f"w_sb{i}", [128, 512], BF16) for i in range(NW)]
            Lsum = [T(ph, f"Lsum{i}", [128, 512], BF16) for i in range(2)]
            ostg = [T(ph, f"ostg{i}", [128, 512], BF16) for i in range(2)]
            chains = []
            gi = 0
            for h in range(SBH):
                for i in range(NQ):
                    nkb = 8 * (i + 1)
                    for n, kb in enumerate(range(nkb - 1, -1, -1)):
                        chains.append(dict(h=h, i=i, kb=kb, first=(n == 0), last=(n == nkb - 1), g=gi,
                                           masked=(kb >= 8 * i), r=kb - 8 * i, newhead=(i == 0 and n == 0)))
                    gi += 1
            NCH_B = len(chains)
            ZB = lambda c: PS[c % 4]
            XB = lambda c: PS[4 + c % 2]
            OB = lambda g: PS[6 + g % 2]

            def load_head(h):
                sl = h % 2
                dma("sp", lambda e: e.dma_start(out=kT[sl][:], in_=KsbT[h, :, :]), kT[sl].b, writes=[kT[sl].b])
                vsrc = Vsb.rearrange("(kb s) c -> s kb c", s=128)
                dma("sp", lambda e: e.dma_start(out=vv[sl][:], in_=vsrc[:, :, h * 128:(h + 1) * 128]), vv[sl].b, writes=[vv[sl].b])
                dma("sp", lambda e: e.dma_start(out=qq[sl][:], in_=QsbT[h, :, :]), qq[sl].b, writes=[qq[sl].b])

            def stage(st_, c):
                ch = chains[c]
                h, i, kb, sl = ch["h"], ch["i"], ch["kb"], ch["h"] % 2
                zp, xp, ob = ZB(c), XB(c), OB(ch["g"])
                E, SPt, Lt, LW, W = e_sb[c % NE], sp_sb[c % NSP], L_sb[c % NL], lw_sb[c % NLW], w_sb[c % NW]
                first, last, masked, r = ch["first"], ch["last"], ch["masked"], ch["r"]
                if st_ == 0:
                    if ch["newhead"] and h == 0:
                        load_head(0)
                    op("pe", lambda e: e.matmul(zp[:], kT[sl][:, kb * 128:(kb + 1) * 128], qq[sl][:, i * 512:(i + 1) * 512], start=True, stop=True),
                       reads=[kT[sl].b, qq[sl].b], writes=[zp.b])
                elif st_ == 1:
                    op("act", lambda e: e.activation(E[:], zp[:], AF.Exp, scale=-1.0), reads=[zp.b], writes=[E.b])
                elif st_ == 2:
                    op("act", lambda e: e.activation(SPt[:], E[:], AF.Ln, bias=1.0), reads=[E.b], writes=[SPt.b])
                elif st_ == 3:
                    op("dve", lambda e: e.scalar_tensor_tensor(out=Lt[:], in0=zp[:], scalar=-1.0, in1=SPt[:], op0=ALU.mult, op1=ALU.subtract),
                       reads=[zp.b, SPt.b], writes=[Lt.b])
                    if masked:
                        op("dve", lambda e: e.tensor_tensor(out=Lt[:], in0=Lt[:], in1=maskS[:, r * 512:(r + 1) * 512], op=ALU.mult),
                           reads=[Lt.b, maskS.b], writes=[Lt.b])
                elif st_ == 4:
                    op("pe", lambda e: e.matmul(xp[:], trib[:], Lt[:], start=True, stop=first), reads=[trib.b, Lt.b], writes=[xp.b])
                    if not first:
                        lsp = Lsum[(c - 1) % 2]
                        op("pe", lambda e: e.matmul(xp[:], onesb[:], lsp[:], start=False, stop=True), reads=[onesb.b, lsp.b], writes=[xp.b])
                    if not last:
                        lsn = Lsum[c % 2]
                        if first:
                            op("pool", lambda e: e.tensor_copy(lsn[:], Lt[:]), reads=[Lt.b], writes=[lsn.b])
                        else:
                            lsp = Lsum[(c - 1) % 2]
                            op("pool", lambda e: e.tensor_tensor(out=lsn[:], in0=lsp[:], in1=Lt[:], op=ALU.add),
                               reads=[Lt.b, lsp.b], writes=[lsn.b])
                elif st_ == 5:
                    op("dve", lambda e: e.tensor_tensor(out=LW[:], in0=xp[:], in1=SPt[:], op=ALU.subtract), reads=[xp.b, SPt.b], writes=[LW.b])
                elif st_ == 6:
                    op("act", lambda e: e.activation(W[:], LW[:], AF.Exp), reads=[LW.b], writes=[W.b])
                    if masked:
                        op("pool", lambda e: e.tensor_tensor(out=W[:], in0=W[:], in1=maskS[:, r * 512:(r + 1) * 512], op=ALU.mult),
                           reads=[W.b, maskS.b], writes=[W.b])
                elif st_ == 7:
                    op("pe", lambda e: e.matmul(ob[:], vv[sl][:, kb, :], W[:], start=first, stop=last), reads=[vv[sl].b, W.b], writes=[ob.b])
                elif st_ == 8:
                    if ch["newhead"] and h + 1 < SBH:
                        load_head(h + 1)
                    if last:
                        og = ostg[ch["g"] % 2]
                        op("act", lambda e: e.copy(og[:], ob[:]), reads=[ob.b], writes=[og.b])
                        dma("sp", lambda e: e.dma_start(out=OsbT[h, :, i * 512:(i + 1) * 512], in_=og[:]), og.b, reads=[og.b])

            NST = 9
            for t in range(NCH_B + NST - 1):
                for st_ in (0, 4, 7, 1, 2, 6, 8, 3, 5):
                    c = t - st_
                    if 0 <= c < NCH_B:
                        stage(st_, c)
            S_.barrier()
            S_.release([maskS.b] + [t.b for t in kT + vv + qq + ostg])

        with contextlib.ExitStack() as ph:
            maskD = T(ph, "maskD", [128, 8 * 512], BF16)
            dma("sp", lambda e: e.dma_start(out=maskD[:], in_=c_maskD), maskD.b, writes=[maskD.b])
            gsc = T(ph, "gsc", [128, 2], F32)
            dma("sp", lambda e: e.dma_start(out=gsc[:], in_=sublnT), gsc.b, writes=[gsc.b])
            op("dve", lambda e: e.tensor_scalar(out=gsc[:], in0=gsc[:], scalar1=float(1.0 - lam_init), scalar2=None, op0=ALU.mult),
               reads=[gsc.b], writes=[gsc.b])
            k1 = [T(ph, f"k1{i}", [128, S], BF16) for i in range(2)]
            k2 = [T(ph, f"k2{i}", [128, S], BF16) for i in range(2)]
            vd = [T(ph, f"vd{i}", [128, NKB, 256], BF16) for i in range(2)]
            q1 = [T(ph, f"q1{i}", [128, SO], BF16) for i in range(2)]
            q2 = [T(ph, f"q2{i}", [128, SO], BF16) for i in range(2)]
            E1 = [T(ph, f"E1{i}", [128, 512], BF16) for i in range(2)]
            E2 = [T(ph, f"E2{i}", [128, 512], BF16) for i in range(2)]
            rz1 = T(ph, "rz1", [128, 512], F32)
            rz2 = T(ph, "rz2", [128, 512], F32)
            ta = T(ph, "ta", [128, 512], F32)
            tb = T(ph, "tb", [128, 512], F32)
            od = [T(ph, f"od{i}", [128, 512], F32) for i in range(2)]
            sq = [T(ph, f"sq{i}", [128, 512], F32) for i in range(2)]
            sd = T(ph, "sd", [128, 512], F32)
            rstd = T(ph, "rstd", [128, 512], F32)
            ystg = [T(ph, f"ystg{i}", [128, 512], BF16) for i in range(2)]
            Z1, Z2 = PS[0], PS[1]
            O1a, O1b, Z1s, O2a, O2b, Z2s = PS[2], PS[3], PS[4], PS[5], PS[6], PS[7]
            blk = 0
            def load_da_head(h):
                sl = h % 2
                dma("sp", lambda e: e.dma_start(out=k1[sl][:], in_=KdaT[2 * h, :, :]), k1[sl].b, writes=[k1[sl].b])
                dma("sp", lambda e: e.dma_start(out=k2[sl][:], in_=KdaT[2 * h + 1, :, :]), k2[sl].b, writes=[k2[sl].b])
                vsrc = Vda.rearrange("(kb s) c -> s kb c", s=128)
                dma("sp", lambda e: e.dma_start(out=vd[sl][:], in_=vsrc[:, :, h * 256:(h + 1) * 256]), vd[sl].b, writes=[vd[sl].b])
                dma("sp", lambda e: e.dma_start(out=q1[sl][:], in_=QdaT[2 * h, :, :]), q1[sl].b, writes=[q1[sl].b])
                dma("sp", lambda e: e.dma_start(out=q2[sl][:], in_=QdaT[2 * h + 1, :, :]), q2[sl].b, writes=[q2[sl].b])

            for h in range(DAH):
                sl = h % 2
                if h == 0:
                    load_da_head(0)
                if h + 1 < DAH:
                    load_da_head(h + 1)
                for i in range(NQ):
                    nkb = 8 * (i + 1)

                    def emit_z(kb, which, sl=sl, i=i):
                        zp = Z1 if which == 1 else Z2
                        kk = k1 if which == 1 else k2
                        qx = q1 if which == 1 else q2
                        op("pe", lambda e, zp=zp, kk=kk, qx=qx, kb=kb: e.matmul(
                            zp[:], kk[sl][:, kb * 128:(kb + 1) * 128], qx[sl][:, i * 512:(i + 1) * 512], start=True, stop=True),
                           reads=[kk[sl].b, qx[sl].b], writes=[zp.b])

                    if pend_t:
                        pend_t.pop(0)()
                    emit_z(0, 1)
                    emit_z(0, 2)
                    for kb in range(nkb):
                        par = (blk + kb) % 2
                        first = kb == 0
                        last = kb == nkb - 1
                        masked = kb >= 8 * i
                        r = kb - 8 * i
                        A1, A2 = E1[par], E2[par]
                        op("act", lambda e, A1=A1: e.activation(A1[:], Z1[:], AF.Exp), reads=[Z1.b], writes=[A1.b])
                        op("act", lambda e, A2=A2: e.activation(A2[:], Z2[:], AF.Exp), reads=[Z2.b], writes=[A2.b])
                        if masked:
                            op("dve", lambda e, A1=A1, r=r: e.tensor_tensor(out=A1[:], in0=A1[:], in1=maskD[:, r * 512:(r + 1) * 512], op=ALU.mult),
                               reads=[A1.b, maskD.b], writes=[A1.b])
                            op("dve", lambda e, A2=A2, r=r: e.tensor_tensor(out=A2[:], in0=A2[:], in1=maskD[:, r * 512:(r + 1) * 512], op=ALU.mult),
                               reads=[A2.b, maskD.b], writes=[A2.b])
                        if not last:
                            emit_z(kb + 1, 1)
                        for (pp, lo) in ((O1a, 0), (O1b, 128)):
                            op("pe", lambda e, pp=pp, lo=lo, kb=kb, A1=A1, first=first, last=last, sl=sl: e.matmul(
                                pp[:], vd[sl][:, kb, lo:lo + 128], A1[:], start=first, stop=last),
                               reads=[vd[sl].b, A1.b], writes=[pp.b])
                        op("pe", lambda e, A1=A1, first=first, last=last: e.matmul(Z1s[:], onesb[:], A1[:], start=first, stop=last),
                           reads=[onesb.b, A1.b], writes=[Z1s.b])
                        if not last:
                            emit_z(kb + 1, 2)
                        for (pp, lo) in ((O2a, 0), (O2b, 128)):
                            op("pe", lambda e, pp=pp, lo=lo, kb=kb, A2=A2, first=first, last=last, sl=sl: e.matmul(
                                pp[:], vd[sl][:, kb, lo:lo + 128], A2[:], start=first, stop=last),
                               reads=[vd[sl].b, A2.b], writes=[pp.b])
                        op("pe", lambda e, A2=A2, first=first, last=last: e.matmul(Z2s[:], onesb[:], A2[:], start=first, stop=last),
                           reads=[onesb.b, A2.b], writes=[Z2s.b])
                    blk += nkb
                    op("dve", lambda e: e.reciprocal(rz1[:], Z1s[:]), reads=[Z1s.b], writes=[rz1.b])
                    op("dve", lambda e: e.reciprocal(rz2[:], Z2s[:]), reads=[Z2s.b], writes=[rz2.b])
                    op("dve", lambda e: e.tensor_scalar(out=rz2[:], in0=rz2[:], scalar1=lam[:, 0:1], scalar2=None, op0=ALU.mult),
                       reads=[rz2.b, lam.b], writes=[rz2.b])
                    for hf, (pa, pb) in enumerate(((O1a, O2a), (O1b, O2b))):
                        op("dve", lambda e, pa=pa: e.tensor_tensor(out=ta[:], in0=pa[:], in1=rz1[:], op=ALU.mult),
                           reads=[pa.b, rz1.b], writes=[ta.b])
                        op("dve", lambda e, pb=pb: e.tensor_tensor(out=tb[:], in0=pb[:], in1=rz2[:], op=ALU.mult),
                           reads=[pb.b, rz2.b], writes=[tb.b])
                        op("dve", lambda e, hf=hf: e.tensor_tensor(out=od[hf][:], in0=ta[:], in1=tb[:], op=ALU.subtract),
                           reads=[ta.b, tb.b], writes=[od[hf].b])
                        op("act", lambda e, hf=hf: e.activation(sq[hf][:], od[hf][:], AF.Square), reads=[od[hf].b], writes=[sq[hf].b])
                    op("pe", lambda e: e.matmul(Z1[:], ones[:], sq[0][:], start=True, stop=False), reads=[ones.b, sq[0].b], writes=[Z1.b])
                    op("pe", lambda e: e.matmul(Z1[:], ones[:], sq[1][:], start=False, stop=True), reads=[ones.b, sq[1].b], writes=[Z1.b])
                    op("act", lambda e: e.activation(sd[:], Z1[:], AF.Sqrt, bias=epsc[:, 0:1], scale=1.0 / 256.0),
                       reads=[Z1.b, epsc.b], writes=[sd.b])
                    op("dve", lambda e: e.reciprocal(rstd[:], sd[:]), reads=[sd.b], writes=[rstd.b])
                    for hf in range(2):
                        yg = ystg[hf]
                        op("dve", lambda e, hf=hf, yg=yg: e.scalar_tensor_tensor(out=yg[:], in0=od[hf][:], scalar=gsc[:, hf:hf + 1], in1=rstd[:],
                                                                                 op0=ALU.mult, op1=ALU.mult),
                           reads=[od[hf].b, gsc.b, rstd.b], writes=[yg.b])
                        dma("sp", lambda e, yg=yg, h=h, hf=hf, i=i: e.dma_start(out=OdaT[2 * h + hf, :, i * 512:(i + 1) * 512], in_=yg[:]),
                            yg.b, reads=[yg.b])
            while pend_t:
                pend_t.pop(0)()
            S_.barrier()
            S_.release([maskD.b, gsc.b] + [t.b for t in k1 + k2 + vd + q1 + q2 + ystg])

        with contextlib.ExitStack() as ph:
            xt = T(ph, "xtD", [128, KC, 512], BF16)
            osb = T(ph, "osb", [128, SBH, 512], BF16)
            oda = T(ph, "oda", [128, 2 * DAH, 512], BF16)
            bg = T(ph, "bg", [128, 2 * KC], F32)
            dma("sp", lambda e: e.dma_start(out=bg[:], in_=bgT), bg.b, writes=[bg.b])
            wsb_ = [T(ph, f"wsb{i}", [128, SBH, 256], BF16) for i in range(2)]
            wda_ = [T(ph, f"wda{i}", [128, 2 * DAH, 256], BF16) for i in range(2)]
            wg1_ = [T(ph, f"wg1{i}", [128, KC, 256], BF16) for i in range(2)]
            wg2_ = [T(ph, f"wg2{i}", [128, KC, 256], BF16) for i in range(2)]
            s1 = [T(ph, f"s1{i}", [128, 512], F32) for i in range(2)]
            s2 = [T(ph, f"s2{i}", [128, 512], F32) for i in range(2)]
            mm1 = [T(ph, f"mm1{i}", [128, 512], F32) for i in range(2)]
            mm2 = [T(ph, f"mm2{i}", [128, 512], F32) for i in range(2)]
            mstg = [T(ph, f"mstg{i}", [128, 512], BF16) for i in range(2)]
            gcnt = 0
            ccnt = 0
            for ti in range(NQ):
                t0 = ti * 512
                xv = xT_own.rearrange("(kc p) t -> p kc t", p=128)
                for k0 in range(0, KC, 4):
                    dma("pool", lambda e, k0=k0, t0=t0, xv=xv: e.dma_start(out=xt[:, k0:k0 + 4, :], in_=xv[:, k0:k0 + 4, t0:t0 + 512]),
                        xt.b, writes=[xt.b])
                dma("pool", lambda e, t0=t0: e.dma_start(out=osb[:], in_=OsbT.rearrange("h p t -> p h t")[:, :, t0:t0 + 512]),
                    osb.b, writes=[osb.b])
                dma("pool", lambda e, t0=t0: e.dma_start(out=oda[:], in_=OdaT.rearrange("h p t -> p h t")[:, :, t0:t0 + 512]),
                    oda.b, writes=[oda.b])
                for cg in range(D // 256):
                    sl = gcnt % 2
                    gcnt += 1
                    c0 = cg * 256
                    dma("pool", lambda e, sl=sl, c0=c0: e.dma_start(out=wsb_[sl][:], in_=w_sbb_b.rearrange("(kc p) c -> p kc c", p=128)[:, :, c0:c0 + 256]),
                        wsb_[sl].b, reads=[CB["sbb"]], writes=[wsb_[sl].b])
                    dma("pool", lambda e, sl=sl, c0=c0: e.dma_start(out=wda_[sl][:], in_=w_dab_b.rearrange("(kc p) c -> p kc c", p=128)[:, :, c0:c0 + 256]),
                        wda_[sl].b, reads=[CB["dab"]], writes=[wda_[sl].b])
                    wv = w_in.rearrange("(kc p) c -> p kc c", p=128)
                    for k0 in range(0, KC, 8):
                        k1_ = min(KC, k0 + 8)
                        dma("pool", lambda e, sl=sl, c0=c0, k0=k0, k1_=k1_, wv=wv: e.dma_start(
                            out=wg1_[sl][:, k0:k1_, :], in_=wv[:, k0:k1_, c_g + c0:c_g + c0 + 256]), wg1_[sl].b, writes=[wg1_[sl].b])
                        dma("pool", lambda e, sl=sl, c0=c0, k0=k0, k1_=k1_, wv=wv: e.dma_start(
                            out=wg2_[sl][:, k0:k1_, :], in_=wv[:, k0:k1_, c_g + D + c0:c_g + D + c0 + 256]), wg2_[sl].b, writes=[wg2_[sl].b])
                    for cc in range(2):
                        c = cg * 2 + cc
                        par = ccnt % 2
                        ccnt += 1
                        pb1, pb2, pg1, pg2 = PS[4 * par], PS[4 * par + 1], PS[4 * par + 2], PS[4 * par + 3]
                        for kc in range(SBH):
                            op("pe", lambda e, pb1=pb1, sl=sl, kc=kc, cc=cc: e.matmul(
                                pb1[:], wsb_[sl][:, kc, cc * 128:(cc + 1) * 128], osb[:, kc, :], start=(kc == 0), stop=(kc == SBH - 1)),
                               reads=[wsb_[sl].b, osb.b], writes=[pb1.b])
                        for kc in range(2 * DAH):
                            op("pe", lambda e, pb2=pb2, sl=sl, kc=kc, cc=cc: e.matmul(
                                pb2[:], wda_[sl][:, kc, cc * 128:(cc + 1) * 128], oda[:, kc, :], start=(kc == 0), stop=(kc == 2 * DAH - 1)),
                               reads=[wda_[sl].b, oda.b], writes=[pb2.b])
                        for kc in range(KC):
                            op("pe", lambda e, pg1=pg1, sl=sl, kc=kc, cc=cc: e.matmul(
                                pg1[:], wg1_[sl][:, kc, cc * 128:(cc + 1) * 128], xt[:, kc, :], start=(kc == 0), stop=(kc == KC - 1)),
                               reads=[wg1_[sl].b, xt.b], writes=[pg1.b])
                        for kc in range(KC):
                            op("pe", lambda e, pg2=pg2, sl=sl, kc=kc, cc=cc: e.matmul(
                                pg2[:], wg2_[sl][:, kc, cc * 128:(cc + 1) * 128], xt[:, kc, :], start=(kc == 0), stop=(kc == KC - 1)),
                               reads=[wg2_[sl].b, xt.b], writes=[pg2.b])
                        op("act", lambda e, par=par, pg1=pg1, c=c: e.activation(s1[par][:], pg1[:], AF.Sigmoid, bias=bg[:, c:c + 1]),
                           reads=[pg1.b, bg.b], writes=[s1[par].b])
                        op("act", lambda e, par=par, pg2=pg2, c=c: e.activation(s2[par][:], pg2[:], AF.Sigmoid, bias=bg[:, KC + c:KC + c + 1]),
                           reads=[pg2.b, bg.b], writes=[s2[par].b])
                        op("dve", lambda e, par=par, pb1=pb1: e.tensor_tensor(out=mm1[par][:], in0=pb1[:], in1=s1[par][:], op=ALU.mult),
                           reads=[pb1.b, s1[par].b], writes=[mm1[par].b])
                        op("dve", lambda e, par=par, pb2=pb2: e.tensor_tensor(out=mm2[par][:], in0=pb2[:], in1=s2[par][:], op=ALU.mult),
                           reads=[pb2.b, s2[par].b], writes=[mm2[par].b])
                        op("dve", lambda e, par=par: e.tensor_tensor(out=mstg[par][:], in0=mm1[par][:], in1=mm2[par][:], op=ALU.add),
                           reads=[mm1[par].b, mm2[par].b], writes=[mstg[par].b])
                        dma("sp", lambda e, par=par, c=c, t0=t0: e.dma_start(out=mergedT[c, :, t0:t0 + 512], in_=mstg[par][:]),
                            mstg[par].b, reads=[mstg[par].b])
            S_.barrier()
            S_.release([xt.b, osb.b, oda.b, bg.b] + [t.b for t in wsb_ + wda_ + wg1_ + wg2_ + mstg])

        def layer_norm(xr, G, Bt, junk, stat):
            op("dve", lambda e: e.tensor_reduce(out=stat[:, 0:1], in_=xr[:], axis=AX.X, op=ALU.add), reads=[xr.b], writes=[stat.b])
            op("dve", lambda e: e.tensor_scalar(out=stat[:, 0:1], in0=stat[:, 0:1], scalar1=float(-1.0 / D), scalar2=None, op0=ALU.mult),
               reads=[stat.b], writes=[stat.b])
            op("dve", lambda e: e.tensor_scalar(out=xr[:], in0=xr[:], scalar1=stat[:, 0:1], scalar2=None, op0=ALU.add),
               reads=[xr.b, stat.b], writes=[xr.b])
            op("dve", lambda e: e.memset(stat[:, 1:2], 0.0), writes=[stat.b])
            op("dve", lambda e: e.scalar_tensor_tensor(out=junk[:], in0=xr[:], scalar=1.0, in1=xr[:], op0=ALU.mult, op1=ALU.mult,
                                                       accum_out=stat[:, 1:2]),
               reads=[xr.b, stat.b], writes=[junk.b, stat.b])
            op("act", lambda e: e.activation(stat[:, 2:3], stat[:, 1:2], AF.Sqrt, bias=epsc[:, 0:1], scale=float(1.0 / D)),
               reads=[stat.b, epsc.b], writes=[stat.b])
            op("dve", lambda e: e.reciprocal(stat[:, 3:4], stat[:, 2:3]), reads=[stat.b], writes=[stat.b])
            op("dve", lambda e: e.scalar_tensor_tensor(out=xr[:], in0=xr[:], scalar=stat[:, 3:4], in1=G[:], op0=ALU.mult, op1=ALU.mult),
               reads=[xr.b, stat.b, G.b], writes=[xr.b])
            op("dve", lambda e: e.tensor_tensor(out=xr[:], in0=xr[:], in1=Bt[:], op=ALU.add), reads=[xr.b, Bt.b], writes=[xr.b])

        with contextlib.ExitStack() as ph:
            mT = T(ph, "mT", [128, KC, 256], BF16)
            xr = [T(ph, f"xr{i}", [128, D], F32) for i in range(2)]
            wo = [T(ph, f"wo{i}", [128, KC, 512], BF16) for i in range(2)]
            G1 = T(ph, "G1", [128, D], F32)
            B1 = T(ph, "B1", [128, D], F32)
            junk = T(ph, "junkD", [128, D], F32)
            stat = T(ph, "statD", [128, 4], F32)
            hT = [T(ph, f"hT{i}", [128, KC, 128], BF16) for i in range(2)]
            dma("sp", lambda e: e.dma_start(out=G1[:], in_=ln1g[0:1, :].partition_broadcast(128)), G1.b, writes=[G1.b])
            dma("sp", lambda e: e.dma_start(out=B1[:], in_=ln1b[0:1, :].partition_broadcast(128)), B1.b, writes=[B1.b])
            wc = 0
            pc = 0
            hc_ = 0
            for ti in range(SO // 256):
                t0 = ti * 256
                dma("sp", lambda e, t0=t0: e.dma_start(out=mT[:], in_=mergedT.rearrange("c p t -> p c t")[:, :, t0:t0 + 256]),
                    mT.b, writes=[mT.b])
                for tc in range(2):
                    dma("sp", lambda e, tc=tc, t0=t0: e.dma_start(out=xr[tc][:], in_=x_own[t0 + tc * 128:t0 + (tc + 1) * 128, :]),
                        xr[tc].b, writes=[xr[tc].b])
                for cg in range(D // 512):
                    w = wo[wc % 2]
                    wc += 1
                    wv = w_out_b.rearrange("(kc p) c -> p kc c", p=128)
                    for k0 in range(0, KC, 8):
                        k1_ = min(KC, k0 + 8)
                        dma("sp", lambda e, w=w, k0=k0, k1_=k1_, cg=cg, wv=wv: e.dma_start(out=w[:, k0:k1_, :], in_=wv[:, k0:k1_, cg * 512:(cg + 1) * 512]),
                            w.b, reads=[CB["out"]], writes=[w.b])
                    for tc in range(2):
                        p = PS[pc % 4]
                        pc += 1
                        for kc in range(KC):
                            op("pe", lambda e, p=p, w=w, kc=kc, tc=tc: e.matmul(p[:], mT[:, kc, tc * 128:(tc + 1) * 128], w[:, kc, :],
                                                                              start=(kc == 0), stop=(kc == KC - 1)),
                               reads=[mT.b, w.b], writes=[p.b])
                        op("dve", lambda e, p=p, tc=tc, cg=cg: e.scalar_tensor_tensor(
                            out=xr[tc][:, cg * 512:(cg + 1) * 512], in0=xr[tc][:, cg * 512:(cg + 1) * 512], scalar=float(alpha), in1=p[:],
                            op0=ALU.mult, op1=ALU.add), reads=[xr[tc].b, p.b], writes=[xr[tc].b])
                for tc in range(2):
                    layer_norm(xr[tc], G1, B1, junk, stat)
                    r0 = t0 + tc * 128
                    dma("pool", lambda e, tc=tc, r0=r0: e.dma_start(out=h1_scr[r0:r0 + 128, :], in_=xr[tc][:]), xr[tc].b, reads=[xr[tc].b])
                    ht = hT[hc_ % 2]
                    hc_ += 1
                    for k0 in range(0, KC, 4):
                        p = PS[4 + (pc % 4)]
                        pc += 1
                        for kk in range(4):
                            kc = k0 + kk
                            op("pe", lambda e, p=p, kk=kk, kc=kc, tc=tc: e.transpose(p[:, kk * 128:(kk + 1) * 128], xr[tc][:, kc * 128:(kc + 1) * 128], ident[:]),
                               reads=[xr[tc].b, ident.b], writes=[p.b])
                        op("act", lambda e, p=p, ht=ht, k0=k0: e.copy(ht[:, k0:k0 + 4, :], p[:].rearrange("p (a t) -> p a t", a=4)),
                           reads=[p.b], writes=[ht.b])
                    dma("pool", lambda e, ht=ht, r0=r0: e.dma_start(out=h1T_scr.rearrange("c p t -> p c t")[:, :, r0:r0 + 128], in_=ht[:]),
                        ht.b, reads=[ht.b])
            S_.barrier()
            S_.release([mT.b, G1.b, B1.b] + [t.b for t in xr + wo + hT])

        with contextlib.ExitStack() as ph:
            sk = T(ph, "sk", [128, HC, NKEYS], F32)
            dma("sp", lambda e: e.dma_start(out=sk[:], in_=skT.rearrange("h p k -> p h k")), sk.b, writes=[sk.b])
            hTt = T(ph, "hTt", [128, KC, 512], BF16)
            wq = [T(ph, f"wq{i}", [128, KC, 512], BF16) for i in range(2)]
            qpT = T(ph, "qpT", [128, HC, 512], F32)
            scst = [T(ph, f"scst{i}", [128, HC, NKEYS], F32) for i in range(2)]
            wc = 0
            pc = 0
            scn = 0
            for ti in range(NQ):
                t0 = ti * 512
                dma("sp", lambda e, t0=t0: e.dma_start(out=hTt[:], in_=h1T_scr.rearrange("c p t -> p c t")[:, :, t0:t0 + 512]),
                    hTt.b, writes=[hTt.b])
                for g in range(QW // 512):
                    w = wq[wc % 2]
                    wc += 1
                    wv = w_q_b.rearrange("(kc p) c -> p kc c", p=128)
                    for k0 in range(0, KC, 8):
                        k1_ = min(KC, k0 + 8)
                        dma("sp", lambda e, w=w, k0=k0, k1_=k1_, g=g, wv=wv: e.dma_start(out=w[:, k0:k1_, :], in_=wv[:, k0:k1_, g * 512:(g + 1) * 512]),
                            w.b, reads=[CB["wq"]], writes=[w.b])
                    for cb in range(4):
                        hcx = g * 4 + cb
                        p = PS[pc % 4]
                        pc += 1
                        for kc in range(KC):
                            op("pe", lambda e, p=p, w=w, kc=kc, cb=cb: e.matmul(p[:], w[:, kc, cb * 128:(cb + 1) * 128], hTt[:, kc, :],
                                                                              start=(kc == 0), stop=(kc == KC - 1)),
                               reads=[w.b, hTt.b], writes=[p.b])
                        op("act", lambda e, p=p, hcx=hcx: e.copy(qpT[:, hcx, :], p[:]), reads=[p.b], writes=[qpT.b])
                for tc in range(4):
                    r0 = t0 + tc * 128
                    sct = scst[scn % 2]
                    scn += 1
                    for g4 in range(0, HC, 4):
                        p = PS[4 + (pc % 4)]
                        pc += 1
                        for kk in range(4):
                            hcx = g4 + kk
                            op("pe", lambda e, p=p, kk=kk, hcx=hcx, tc=tc: e.matmul(p[:, kk * 128:(kk + 1) * 128], qpT[:, hcx, tc * 128:(tc + 1) * 128],
                                                                                  sk[:, hcx, :], start=True, stop=True),
                               reads=[qpT.b, sk.b], writes=[p.b])
                        op("act", lambda e, p=p, g4=g4, sct=sct: e.copy(sct[:, g4:g4 + 4, :], p[:].rearrange("p (a t) -> p a t", a=4)),
                           reads=[p.b], writes=[sct.b])
                    dma("pool", lambda e, sct=sct, r0=r0: e.dma_start(out=sc_scr[r0:r0 + 128, :], in_=sct[:].rearrange("p h k -> p (h k)")),
                        sct.b, reads=[sct.b])
            S_.barrier()
            S_.release([sk.b, hTt.b] + [t.b for t in wq + scst])

        with contextlib.ExitStack() as ph:
            G2 = T(ph, "G2", [128, D], F32)
            B2 = T(ph, "B2", [128, D], F32)
            dma("sp", lambda e: e.dma_start(out=G2[:], in_=ln2g[0:1, :].partition_broadcast(128)), G2.b, writes=[G2.b])
            dma("sp", lambda e: e.dma_start(out=B2[:], in_=ln2b[0:1, :].partition_broadcast(128)), B2.b, writes=[B2.b])
            identb = T(ph, "identb", [128, 128], BF16)
            dma("pool", lambda e: e.dma_start(out=identb[:], in_=c_ident), identb.b, writes=[identb.b])
            sc = T(ph, "sc", [128, HC, NKEYS], F32)
            top = T(ph, "top", [128, PH, 2, TOPK], F32)
            topi = T(ph, "topi", [128, PH, 2, TOPK], U32)
            topf = T(ph, "topf", [128, PH, 2, TOPK], F32)
            cand = T(ph, "cand", [128, PH, TOPK, TOPK], F32)
            cidx = T(ph, "cidx", [128, PH, TOPK, TOPK], F32)
            cwork = T(ph, "cwork", [128, PH, TOPK * TOPK], F32)
            best = T(ph, "best", [128, PH, TOPK], F32)
            junk2 = T(ph, "junk2", [128, TOPK * TOPK], F32)
            idxf = T(ph, "idxf", [128, NSEL], F32)
            dd = T(ph, "dd", [128, PH, TOPK], F32)
            eb = T(ph, "eb", [128, PH, TOPK], F32)
            zs = T(ph, "zs", [128, PH], F32)
            idxi_ = [T(ph, f"idxi{i}", [128, NSEL], I32) for i in range(2)]
            gate_ = [T(ph, f"gate{i}", [128, PH, TOPK], F32) for i in range(2)]
            hb = [T(ph, f"hb{i}", [128, D], BF16) for i in range(2)]
            acc = T(ph, "acc", [128, D], F32)
            junkb = T(ph, "junkb", [128, D], BF16)
            stat = T(ph, "statE", [128, 4], F32)
            NRG = 5
            ring = [T(ph, f"ring{i}", [128, 2 * D], BF16) for i in range(NRG)]
            dgs = [T(ph, f"dgs{i}", [128, 128], BF16) for i in range(4)]
            gel = [T(ph, f"gel{i}", [128, 2], F32) for i in range(4)]
            actp = [T(ph, f"actp{i}", [128, NSEL], F32) for i in range(2)]
            colb = [Buf(f"colb{i}") for i in range(4)]
            NCH = SO // 128
            NDG = D // 512
            cnt = dict(r=0, d=0)

            def e2_s1(c):
                par = c % 2
                r0 = c * 128
                idxi, gate = idxi_[par], gate_[par]
                workv = cwork[:].rearrange("p h (c k) -> p (h c) k", c=2)
                dma("pool", lambda e: e.dma_start(out=hb[par][:], in_=h1_scr[r0:r0 + 128, :]), hb[par].b, writes=[hb[par].b])
                dma("sp", lambda e: e.dma_start(out=sc[:].rearrange("p h k -> p (h k)"), in_=sc_scr[r0:r0 + 128, :]), sc.b, writes=[sc.b])
                for hcx in range(HC):
                    hh, c2 = hcx // 2, hcx % 2
                    op("dve", lambda e, hh=hh, c2=c2, hcx=hcx: e.max(out=top[:, hh, c2, 0:8], in_=sc[:, hcx, :]),
                       reads=[sc.b], writes=[top.b])
                    op("dve", lambda e, hh=hh, c2=c2, hcx=hcx: e.max_index(out=topi[:, hh, c2, 0:8], in_max=top[:, hh, c2, 0:8], in_values=sc[:, hcx, :]),
                       reads=[sc.b, top.b], writes=[topi.b])
                    op("dve", lambda e, hh=hh, c2=c2, hcx=hcx: e.match_replace(out=workv[:, hcx, :], in_to_replace=top[:, hh, c2, 0:8],
                                                                               in_values=sc[:, hcx, :], imm_value=-1e30),
                       reads=[sc.b, top.b], writes=[cwork.b])
                    op("dve", lambda e, hh=hh, c2=c2, hcx=hcx: e.max(out=top[:, hh, c2, 8:16], in_=workv[:, hcx, :]),
                       reads=[cwork.b], writes=[top.b])
                    op("dve", lambda e, hh=hh, c2=c2, hcx=hcx: e.max_index(out=topi[:, hh, c2, 8:16], in_max=top[:, hh, c2, 8:16], in_values=workv[:, hcx, :]),
                       reads=[cwork.b, top.b], writes=[topi.b])
                op("dve", lambda e: e.tensor_copy(topf[:], topi[:]), reads=[topi.b], writes=[topf.b])
                bshape = [128, TOPK, TOPK]
                for hh in range(PH):
                    op("dve", lambda e, hh=hh: e.tensor_tensor(out=cand[:, hh, :, :], in0=top[:, hh, 0, :].unsqueeze(2).to_broadcast(bshape),
                                                               in1=top[:, hh, 1, :].unsqueeze(1).to_broadcast(bshape), op=ALU.add),
                       reads=[top.b], writes=[cand.b])
                    op("dve", lambda e, hh=hh: e.scalar_tensor_tensor(out=cidx[:, hh, :, :], in0=topf[:, hh, 0, :].unsqueeze(2).to_broadcast(bshape),
                                                                      scalar=float(NKEYS), in1=topf[:, hh, 1, :].unsqueeze(1).to_broadcast(bshape),
                                                                      op0=ALU.mult, op1=ALU.add),
                       reads=[topf.b], writes=[cidx.b])
                for hh in range(PH):
                    cv = cand[:, hh, :, :].rearrange("p a b -> p (a b)")
                    op("dve", lambda e, hh=hh, cv=cv: e.max(out=best[:, hh, 0:8], in_=cv), reads=[cand.b], writes=[best.b])
                    op("dve", lambda e, hh=hh, cv=cv: e.match_replace(out=cwork[:, hh, :], in_to_replace=best[:, hh, 0:8], in_values=cv, imm_value=-1e30),
                       reads=[cand.b, best.b], writes=[cwork.b])
                    op("dve", lambda e, hh=hh: e.max(out=best[:, hh, 8:16], in_=cwork[:, hh, :]), reads=[cwork.b], writes=[best.b])
                op("dve", lambda e: e.memset(idxf[:], 0.0), writes=[idxf.b])
                op("dve", lambda e: e.memset(actp[par][:], 0.0), writes=colb)
                for hh in range(PH):
                    cv = cand[:, hh, :, :].rearrange("p a b -> p (a b)")
                    iv = cidx[:, hh, :, :].rearrange("p a b -> p (a b)")
                    for k in range(TOPK):
                        n = hh * TOPK + k
                        op("dve", lambda e, hh=hh, k=k, n=n, cv=cv, iv=iv: e.scalar_tensor_tensor(
                            out=junk2[:], in0=cv, scalar=best[:, hh, k:k + 1], in1=iv, op0=ALU.is_equal, op1=ALU.mult,
                            accum_out=idxf[:, n:n + 1]),
                           reads=[cand.b, cidx.b, best.b, idxf.b], writes=[junk2.b, idxf.b])
                op("dve", lambda e: e.tensor_scalar(out=idxf[:], in0=idxf[:], scalar1=float(NEXP - 1), scalar2=None, op0=ALU.min),
                   reads=[idxf.b], writes=[idxf.b])
                op("dve", lambda e: e.tensor_copy(idxi[:], idxf[:]), reads=[idxf.b], writes=[idxi.b])
                op("dve", lambda e: e.tensor_tensor(out=dd[:], in0=best[:], in1=best[:, :, 0:1].to_broadcast([128, PH, TOPK]), op=ALU.subtract),
                   reads=[best.b], writes=[dd.b])
                op("act", lambda e: e.activation(eb[:], dd[:], AF.Exp), reads=[dd.b], writes=[eb.b])
                op("dve", lambda e: e.tensor_reduce(out=zs[:], in_=eb[:], axis=AX.X, op=ALU.add), reads=[eb.b], writes=[zs.b])
                op("dve", lambda e: e.reciprocal(zs[:], zs[:]), reads=[zs.b], writes=[zs.b])
                op("dve", lambda e: e.tensor_tensor(out=gate[:], in0=eb[:], in1=zs[:].unsqueeze(2).to_broadcast([128, PH, TOPK]), op=ALU.mult),
                   reads=[eb.b, zs.b], writes=[gate.b])

            def e2_n(c, n):
                par = c % 2
                rg = ring[cnt["r"] % NRG]
                cnt["r"] += 1
                dg = dgs[cnt["d"] % 4]
                ge = gel[cnt["d"] % 4]
                cnt["d"] += 1
                idxi = idxi_[par]
                gflat = gate_[par][:].rearrange("p h k -> p (h k)")
                dma("pool", lambda e: e.indirect_dma_start(
                    out=rg[:], out_offset=None, in_=puv_b, in_offset=bass.IndirectOffsetOnAxis(ap=idxi[:, n:n + 1], axis=0)),
                    rg.b, reads=[idxi.b, CB["puv"]], writes=[rg.b])
                cb = colb[n % 4]
                op("dve", lambda e: e.scalar_tensor_tensor(out=rg[:, 0:D], in0=rg[:, 0:D], scalar=1.0, in1=hb[par][:], op0=ALU.mult, op1=ALU.mult,
                                                           accum_out=actp[par][:, n:n + 1]),
                   reads=[rg.b, hb[par].b, cb], writes=[rg.b, cb])
                op("act", lambda e: e.activation(ge[:, 1:2], actp[par][:, n:n + 1], AF.Gelu), reads=[cb], writes=[ge.b])
                op("act", lambda e: e.activation(ge[:, 1:2], ge[:, 1:2], AF.Copy, scale=gflat[:, n:n + 1]),
                   reads=[ge.b, gate_[par].b], writes=[ge.b])
                op("act", lambda e: e.activation(dg[:], identb[:], AF.Copy, scale=ge[:, 1:2]),
                   reads=[identb.b, ge.b], writes=[dg.b])
                for g in range(NDG):
                    op("pe", lambda e, g=g: e.matmul(PS[g][:], dg[:], rg[:, D + g * 512:D + (g + 1) * 512], start=(n == 0), stop=(n == NSEL - 1)),
                       reads=[dg.b, rg.b], writes=[PS[g].b])

            def e2_post(c):
                r0 = c * 128
                dma("sp", lambda e: e.dma_start(out=acc[:], in_=h1_scr[r0:r0 + 128, :]), acc.b, writes=[acc.b])
                for g in range(NDG):
                    op("dve", lambda e, g=g: e.scalar_tensor_tensor(out=acc[:, g * 512:(g + 1) * 512], in0=acc[:, g * 512:(g + 1) * 512],
                                                                    scalar=float(alpha), in1=PS[g][:], op0=ALU.mult, op1=ALU.add),
                       reads=[acc.b, PS[g].b], writes=[acc.b])
                layer_norm(acc, G2, B2, junkb, stat)
                dma("sp", lambda e: e.dma_start(out=out_own[r0:r0 + 128, :], in_=acc[:]), acc.b, reads=[acc.b])

            e2_s1(0)
            for c in range(NCH):
                for n in range(NSEL):
                    e2_n(c, n)
                    if n == NSEL // 2 and c + 1 < NCH:
                        e2_s1(c + 1)
                e2_post(c)
            S_.barrier(include_bg=True)

        S_.emit()
    return nc


def make_consts(j):
    ident = np.eye(128, dtype=np.float32)
    jj = np.arange(128)[:, None]
    ss = np.arange(128)[None, :]
    tri = (jj > ss).astype(np.float32)
    ones = np.ones((128, 128), np.float32)
    perm = np.zeros((128, 128), np.float32)
    for m in range(128):
        perm[(m + 64) % 128, m] = 1.0
    inv_freq = (ROPE_THETA ** (-np.arange(0, 128, 2, dtype=np.float32) / np.float32(128))).astype(np.float32)
    rope = np.zeros((128, 3), np.float32)
    rope[:, 0] = np.concatenate([inv_freq, inv_freq])
    rope[:64, 1] = -2.0 * math.pi
    rope[64:, 1] = 2.0 * math.pi
    s = np.arange(128)[:, None, None]
    r = np.arange(8)[None, :, None]
    t = np.arange(512)[None, None, :]
    kpos = r * 128 + s
    qpos = j * 512 + t
    maskS = (kpos < qpos).astype(np.float32).reshape(128, 8 * 512).astype(ml_dtypes.bfloat16)
    maskD = (kpos <= qpos).astype(np.float32).reshape(128, 8 * 512).astype(ml_dtypes.bfloat16)
    return dict(c_ident=ident, c_tri=tri, c_ones=ones, c_perm=perm, c_rope=rope, c_maskS=maskS, c_maskD=maskD)


def prep_inputs(inp, cfg):
    D, S, NB, PH = cfg["D"], cfg["S"], cfg["NB"], cfg["PH"]
    KC = D // 128
    f = lambda a: np.ascontiguousarray(np.asarray(a))
    x = f(inp["x"])
    pos = f(inp["positions"]).astype(np.int32)
    shared = dict(
        w_in=f(inp["w_in"][0]),
        bgT=f(np.asarray(inp["b_gate"][0]).reshape(2 * KC, 128).T),
        lamv=f(np.stack([np.asarray(inp["lambda_q1"][0]), np.asarray(inp["lambda_k1"][0]),
                         np.asarray(inp["lambda_q2"][0]), np.asarray(inp["lambda_k2"][0])])),
        sublnT=f(np.asarray(inp["subln_g"][0]).reshape(2, 128).T),
        w_sbb=f(inp["w_sb_branch"][0]), w_dab=f(inp["w_da_branch"][0]), w_out=f(inp["w_out"][0]),
        ln1g=f(np.asarray(inp["ln1_g"][0])[None, :]), ln1b=f(np.asarray(inp["ln1_b"][0])[None, :]),
        ln2g=f(np.asarray(inp["ln2_g"][0])[None, :]), ln2b=f(np.asarray(inp["ln2_b"][0])[None, :]),
        w_q=f(inp["peer_w_q"][0]),
        skT=f(np.asarray(inp["peer_sub_keys"][0]).reshape(2 * PH, NKEYS, 128).transpose(0, 2, 1)),
        pu=f(inp["peer_u"][0]), pv=f(inp["peer_v"][0]),
    )
    consts = [make_consts(0), make_consts(1)]
    in_maps = []
    for c in range(2 * NB):
        b, j = c // 2, c % 2
        tiles = [2 * i + j for i in range(S // 1024)]
        rows = np.concatenate([np.arange(t * 512, (t + 1) * 512) for t in tiles])
        xb = x[b]
        m = dict(shared)
        m.update(consts[j])
        m["xT_all"] = f(xb.T)
        m["x_own"] = f(xb[rows])
        m["xT_own"] = f(xb[rows].T)
        m["pos_all"] = f(pos[b][None, :])
        m["pos_own"] = f(pos[b][rows][None, :])
        in_maps.append(m)
    return in_maps


def assemble(results, cfg):
    D, S, NB = cfg["D"], cfg["S"], cfg["NB"]
    out = np.zeros((NB, S, D), np.float32)
    for c in range(2 * NB):
        b, j = c // 2, c % 2
        tiles = [2 * i + j for i in range(S // 1024)]
        rows = np.concatenate([np.arange(t * 512, (t + 1) * 512) for t in tiles])
        out[b, rows] = results[c]["out_own"]
    return out


def run(inputs, cfg):
    nc = build(cfg)
    in_maps = prep_inputs(inputs, cfg)
    res = run_bass_kernel_spmd(nc, in_maps, core_ids=list(range(2 * cfg["NB"])))
    return assemble(res.results, cfg)


def kernel(**inputs):
    return run(inputs, FULL_CFG)
```

```python
import contextlib
import math
import numpy as np
import ml_dtypes
import concourse.bass as bass
import concourse.mybir as mybir
from concourse.bass_utils import run_bass_kernel_spmd

F32 = mybir.dt.float32
BF16 = mybir.dt.bfloat16
I32 = mybir.dt.int32
U32 = mybir.dt.uint32
AF = mybir.ActivationFunctionType
ALU = mybir.AluOpType
AX = mybir.AxisListType

SEM_ROT = 30000
LN_EPS = 1e-5
NKEYS = 128
TOPK = 16
NEXP = NKEYS * NKEYS
ROPE_THETA = 10000.0

FULL_CFG = dict(D=4096, S=4096, NB=4, SBH=16, DAH=8, PH=8, DEPTH=1)


class Buf:
    __slots__ = ("name", "w", "r", "ds")

    def __init__(self, name=""):
        self.name = name
        self.w = None
        self.r = {}
        self.ds = None


class DSem:
    def __init__(self, handle):
        self.h = handle
        self.total = 0
        self.bg = False


class Sched:
    ENGS = ("pe", "act", "dve", "pool", "sp")

    def __init__(self, nc, stack):
        self.nc = nc
        self.stack = stack
        self.streams = {e: [] for e in self.ENGS}
        self.esem = {}
        self.ecount = {}
        self.known = {e: {} for e in self.ENGS}
        self.nsem = 0
        self.dsems = []
        self.free_ds = []
        for e in ("pe", "act", "dve", "pool"):
            self._new_esem(e)

    def _alloc(self, name):
        self.nsem += 1
        return self.stack.enter_context(self.nc.semaphore(f"{name}_{self.nsem}"))

    def _new_esem(self, e):
        self.esem[e] = self._alloc(f"es_{e}")
        self.ecount[e] = 0

    def dsem(self, name="d"):
        if self.free_ds:
            return self.free_ds.pop()
        d = DSem(self._alloc(f"ds_{name}"))
        self.dsems.append(d)
        return d

    def release(self, bufs):
        for b in bufs:
            if b.ds is not None:
                self.free_ds.append(b.ds)
                b.ds = None

    def _collect(self, eng, reads, writes, dma=False):
        need = {}

        def add(tok, kind):
            if tok is None:
                return
            semh, val, teng, ds = tok
            if teng == eng and teng is not None and not dma:
                if eng == "pe":
                    return
                if kind == "war":
                    return
            if ds is not None:
                val = ds.total
            k = id(semh)
            if k not in need or need[k][1] < val:
                need[k] = (semh, val)

        for b in reads:
            add(b.w, "raw")
        for b in writes:
            add(b.w, "waw")
            for t in b.r.values():
                add(t, "war")
        return self._filter(eng, need.values())

    def _filter(self, eng, pairs):
        waits = []
        kn = self.known[eng]
        for semh, val in pairs:
            k = id(semh)
            if kn.get(k, 0) >= val:
                continue
            kn[k] = val
            waits.append((semh, val))
        return waits

    def _commit(self, tok, reads, writes):
        k = id(tok[0])
        for b in reads:
            b.r[k] = tok
        for b in writes:
            b.w = tok
            b.r = {}

    def op(self, eng, fn, reads=(), writes=()):
        waits = self._collect(eng, reads, writes)
        if self.ecount[eng] >= SEM_ROT:
            self._new_esem(eng)
        self.ecount[eng] += 1
        semh = self.esem[eng]
        tok = (semh, self.ecount[eng], eng, None)
        self.streams[eng].append((waits, fn, (semh, 1)))
        self._commit(tok, reads, writes)
        return tok

    def dma(self, q, fn, owner, reads=(), writes=()):
        if owner.ds is None:
            owner.ds = self.dsem(owner.name)
        ds = owner.ds
        waits = self._collect(q, reads, writes, dma=True)
        ds.total += 16
        tok = (ds.h, ds.total, None, ds)
        self.streams[q].append((waits, fn, (ds.h, 16)))
        self._commit(tok, reads, writes)
        return tok

    def barrier(self, include_bg=False):
        pairs = [(self.esem[e], self.ecount[e]) for e in ("pe", "act", "dve", "pool") if self.ecount[e] > 0]
        pairs += [(d.h, d.total) for d in self.dsems if d.total > 0 and (include_bg or not d.bg)]
        for e in self.ENGS:
            w = self._filter(e, pairs)
            if w:
                self.streams[e].append((w, None, None))

    def emit(self):
        nc = self.nc
        streams = self.streams

        def run(engh, items):
            for waits, fn, inc in items:
                for semh, val in waits:
                    engh.wait_ge(semh, val)
                if fn is None:
                    continue
                ins = fn(engh)
                if inc is not None:
                    ins.then_inc(inc[0], inc[1])

        with nc.Block() as block:
            @block.tensor
            def _(e):
                run(e, streams["pe"])

            @block.scalar
            def _(e):
                run(e, streams["act"])

            @block.vector
            def _(e):
                run(e, streams["dve"])

            @block.gpsimd
            def _(e):
                run(e, streams["pool"])

            @block.sync
            def _(e):
                run(e, streams["sp"])


def build(cfg):
    D, S, SBH, DAH, PH = cfg["D"], cfg["S"], cfg["SBH"], cfg["DAH"], cfg["PH"]
    DEPTH = cfg["DEPTH"]
    assert DEPTH == 1
    KC = D // 128
    SO = S // 2
    SBW = SBH * 128
    DAW = DAH * 256
    INW = 3 * SBW + 3 * DAW + 2 * D
    c_sbq, c_sbk, c_sbv = 0, SBW, 2 * SBW
    c_daq, c_dak, c_dav = 3 * SBW, 3 * SBW + DAW, 3 * SBW + 2 * DAW
    c_g = 3 * SBW + 3 * DAW
    NQ = SO // 512
    HC = 2 * PH
    NSEL = PH * TOPK
    QW = PH * 256
    alpha = (2 * DEPTH) ** 0.25
    lam_init = 0.8 - 0.6 * math.exp(-0.3 * 0)
    scale = 128 ** -0.5
    PI = math.pi

    nc = bass.Bass("TRN2", target_bir_lowering=False)

    def din(name, shape, dt=F32):
        return nc.dram_tensor(name, list(shape), dt, kind="ExternalInput").ap()

    xT_all = din("xT_all", [D, S])
    xT_own = din("xT_own", [D, SO])
    x_own = din("x_own", [SO, D])
    pos_all = din("pos_all", [1, S], I32)
    pos_own = din("pos_own", [1, SO], I32)
    w_in = din("w_in", [D, INW])
    bgT = din("bgT", [128, 2 * KC])
    lamv = din("lamv", [4, 128])
    sublnT = din("sublnT", [128, 2])
    w_sbb = din("w_sbb", [SBW, D])
    w_dab = din("w_dab", [DAW, D])
    w_out = din("w_out", [D, D])
    ln1g = din("ln1g", [1, D]); ln1b = din("ln1b", [1, D])
    ln2g = din("ln2g", [1, D]); ln2b = din("ln2b", [1, D])
    w_q = din("w_q", [D, QW])
    skT = din("skT", [HC, 128, NKEYS])
    pu = din("pu", [NEXP, D])
    pv = din("pv", [NEXP, D])
    c_ident = din("c_ident", [128, 128])
    c_tri = din("c_tri", [128, 128])
    c_ones = din("c_ones", [128, 128])
    c_perm = din("c_perm", [128, 128])
    c_rope = din("c_rope", [128, 3])
    c_maskS = din("c_maskS", [128, 8 * 512], BF16)
    c_maskD = din("c_maskD", [128, 8 * 512], BF16)
    out_own = nc.dram_tensor("out_own", [SO, D], F32, kind="ExternalOutput").ap()

    def scr(name, shape, dt):
        return nc.dram_tensor(name, list(shape), dt).ap()

    KsbT = scr("KsbT", [SBH, 128, S], BF16)
    Vsb = scr("Vsb", [S, SBW], BF16)
    KdaT = scr("KdaT", [2 * DAH, 128, S], BF16)
    Vda = scr("Vda", [S, DAW], BF16)
    QsbT = scr("QsbT", [SBH, 128, SO], BF16)
    QdaT = scr("QdaT", [2 * DAH, 128, SO], BF16)
    OsbT = scr("OsbT", [SBH, 128, SO], BF16)
    OdaT = scr("OdaT", [2 * DAH, 128, SO], BF16)
    mergedT = scr("mergedT", [KC, 128, SO], BF16)
    h1_scr = scr("h1_scr", [SO, D], F32)
    h1T_scr = scr("h1T_scr", [KC, 128, SO], BF16)
    sc_scr = scr("sc_scr", [SO, 2 * PH * NKEYS], F32)
    w_in_b = scr("w_in_b", [D, INW], BF16)
    w_sbb_b = scr("w_sbb_b", [SBW, D], BF16)
    w_dab_b = scr("w_dab_b", [DAW, D], BF16)
    w_out_b = scr("w_out_b", [D, D], BF16)
    w_q_b = scr("w_q_b", [D, QW], BF16)
    puv_b = scr("puv_b", [NEXP, 2 * D], BF16)

    with contextlib.ExitStack() as st:
        S_ = Sched(nc, st)
        op = S_.op
        dma = S_.dma

        class T:
            def __init__(self, stack, name, shape, dt, psum=False):
                if psum:
                    self.t = stack.enter_context(nc.psum_tensor(name, list(shape), dt))
                else:
                    self.t = stack.enter_context(nc.sbuf_tensor(name, list(shape), dt))
                self.b = Buf(name)

            def __getitem__(self, k):
                return self.t[k]

        PS = [T(st, f"ps{i}", [128, 512], F32, psum=True) for i in range(8)]
        ident = T(st, "ident", [128, 128], F32)
        tri = T(st, "tri", [128, 128], F32)
        ones = T(st, "ones", [128, 128], F32)
        onesb = T(st, "onesb", [128, 128], BF16)
        trib = T(st, "trib", [128, 128], BF16)
        perm = T(st, "perm", [128, 128], F32)
        rope_c = T(st, "rope_c", [128, 3], F32)
        epsc = T(st, "epsc", [128, 1], F32)
        lam = T(st, "lam", [128, 1], F32)

        dma("sp", lambda e: e.dma_start(out=ident[:], in_=c_ident), ident.b, writes=[ident.b])
        dma("sp", lambda e: e.dma_start(out=tri[:], in_=c_tri), tri.b, writes=[tri.b])
        dma("sp", lambda e: e.dma_start(out=ones[:], in_=c_ones), ones.b, writes=[ones.b])
        dma("pool", lambda e: e.dma_start(out=onesb[:], in_=c_ones), onesb.b, writes=[onesb.b])
        dma("pool", lambda e: e.dma_start(out=trib[:], in_=c_tri), trib.b, writes=[trib.b])
        dma("sp", lambda e: e.dma_start(out=perm[:], in_=c_perm), perm.b, writes=[perm.b])
        dma("sp", lambda e: e.dma_start(out=rope_c[:], in_=c_rope), rope_c.b, writes=[rope_c.b])
        op("dve", lambda e: e.memset(epsc[:], LN_EPS), writes=[epsc.b])

        with contextlib.ExitStack() as ph:
            lv = T(ph, "lv", [128, 4, 128], F32)
            lp = T(ph, "lp", [128, 2, 128], F32)
            ls = T(ph, "ls", [128, 2], F32)
            le = T(ph, "le", [128, 2], F32)
            for r in range(4):
                dma("sp", lambda e, r=r: e.dma_start(out=lv[:, r, :], in_=lamv[r:r + 1, :].partition_broadcast(128)),
                    lv.b, writes=[lv.b])
            op("dve", lambda e: e.tensor_tensor(out=lp[:, 0, :], in0=lv[:, 0, :], in1=lv[:, 1, :], op=ALU.mult),
               reads=[lv.b], writes=[lp.b])
            op("dve", lambda e: e.tensor_tensor(out=lp[:, 1, :], in0=lv[:, 2, :], in1=lv[:, 3, :], op=ALU.mult),
               reads=[lv.b], writes=[lp.b])
            op("dve", lambda e: e.tensor_reduce(out=ls[:], in_=lp[:], axis=AX.X, op=ALU.add), reads=[lp.b], writes=[ls.b])
            op("act", lambda e: e.activation(le[:], ls[:], AF.Exp), reads=[ls.b], writes=[le.b])
            op("dve", lambda e: e.tensor_tensor(out=lam[:], in0=le[:, 0:1], in1=le[:, 1:2], op=ALU.subtract),
               reads=[le.b], writes=[lam.b])
            op("dve", lambda e: e.tensor_scalar(out=lam[:], in0=lam[:], scalar1=float(lam_init), scalar2=None, op0=ALU.add),
               reads=[lam.b], writes=[lam.b])
            S_.barrier()
            S_.release([lv.b])

        CB = {}

        pending = []

        def precast(key, dst, src, r_tot, c0, c1, rblk=1024, dc0=None, defer=False):
            if key not in CB:
                CB[key] = Buf("cb_" + key)
                CB[key].ds = S_.dsem("cb_" + key)
                CB[key].ds.bg = True
            b = CB[key]
            if dc0 is None:
                dc0 = c0
            for r in range(0, r_tot, rblk):
                r1 = min(r_tot, r + rblk)
                th = (lambda r=r, r1=r1: dma("pool", lambda e: e.dma_start(out=dst[r:r1, dc0:dc0 + (c1 - c0)], in_=src[r:r1, c0:c1]), b, writes=[b]))
                th.key = key
                if defer:
                    pending.append(th)
                else:
                    th()

        def flush_pending(k=None):
            n = len(pending) if k is None else min(k, len(pending))
            for _ in range(n):
                pending.pop(0)()

        precast("sbkv", w_in_b, w_in, D, c_sbk, c_sbv + SBW, defer=True)
        precast("dakv", w_in_b, w_in, D, c_dak, c_dav + DAW, defer=True)
        precast("q", w_in_b, w_in, D, c_sbq, c_sbq + SBW, defer=True)
        precast("q", w_in_b, w_in, D, c_daq, c_daq + DAW, defer=True)
        pend_e = pending[:]
        del pending[:]
        precast("sbb", w_sbb_b, w_sbb, SBW, 0, D, defer=True)
        precast("dab", w_dab_b, w_dab, DAW, 0, D, defer=True)
        precast("out", w_out_b, w_out, D, 0, D, defer=True)
        precast("wq", w_q_b, w_q, D, 0, QW, defer=True)
        n_w_pending = len(pending)
        precast("puv", puv_b, pu, NEXP, 0, D, dc0=0, defer=True)
        precast("puv", puv_b, pv, NEXP, 0, D, dc0=D, defer=True)
        pend_w = pending[:n_w_pending]
        pend_t = pending[n_w_pending:]
        del pending[:]

        with contextlib.ExitStack() as ph:
            xT = T(ph, "xT", [128, KC, 1024], BF16)
            posi = T(ph, "posi", [128, 1024], I32)
            ang = T(ph, "ang", [128, 1024], F32)
            m1 = T(ph, "m1", [128, 1024], F32)
            ki = T(ph, "ki", [128, 1024], I32)
            kf = T(ph, "kf", [128, 1024], F32)
            CT = T(ph, "CT", [128, 1024], F32)
            ST = T(ph, "ST", [128, 1024], F32)
            wsl = [T(ph, f"wA{i}", [128, KC, 512], BF16) for i in range(2)]
            stg = [T(ph, f"stA{i}", [128, 512], BF16) for i in range(4)]
            qf = [T(ph, f"qf{i}", [128, 512], F32) for i in range(2)]
            t1 = [T(ph, f"t1{i}", [128, 512], F32) for i in range(2)]
            t2 = [T(ph, f"t2{i}", [128, 512], F32) for i in range(2)]
            wcnt = [0]
            scnt = [0]
            pcnt = [0]
            rcnt = [0]

            puv_in_A = [0]

            def phaseA_pass(xsrc, possrc, ntiles, groups, first_pass=False):
                for tile in range(ntiles):
                    t0 = tile * 1024
                    xv = xsrc.rearrange("(kc p) t -> p kc t", p=128)
                    for k0 in range(0, KC, 4):
                        dma("pool", lambda e, k0=k0, t0=t0, xv=xv: e.dma_start(out=xT[:, k0:k0 + 4, :], in_=xv[:, k0:k0 + 4, t0:t0 + 1024]),
                            xT.b, writes=[xT.b])
                    dma("sp", lambda e, t0=t0, possrc=possrc: e.dma_start(out=posi[:], in_=possrc[0:1, t0:t0 + 1024].partition_broadcast(128)),
                        posi.b, writes=[posi.b])
                    op("dve", lambda e: e.tensor_copy(ang[:], posi[:]), reads=[posi.b], writes=[ang.b])
                    op("dve", lambda e: e.tensor_scalar(out=ang[:], in0=ang[:], scalar1=rope_c[:, 0:1], scalar2=None, op0=ALU.mult),
                       reads=[ang.b, rope_c.b], writes=[ang.b])
                    def range_red(add):
                        op("dve", lambda e: e.tensor_scalar(out=m1[:], in0=ang[:], scalar1=float(1.0 / (2 * PI)), scalar2=float(add),
                                                            op0=ALU.mult, op1=ALU.add), reads=[ang.b], writes=[m1.b])
                        op("dve", lambda e: e.tensor_copy(ki[:], m1[:]), reads=[m1.b], writes=[ki.b])
                        op("dve", lambda e: e.tensor_copy(kf[:], ki[:]), reads=[ki.b], writes=[kf.b])
                        op("dve", lambda e: e.tensor_tensor(out=m1[:], in0=m1[:], in1=kf[:], op=ALU.subtract), reads=[m1.b, kf.b], writes=[m1.b])
                        op("dve", lambda e: e.tensor_scalar(out=kf[:], in0=m1[:], scalar1=0.5, scalar2=None, op0=ALU.is_gt),
                           reads=[m1.b], writes=[kf.b])
                        op("dve", lambda e: e.tensor_tensor(out=m1[:], in0=m1[:], in1=kf[:], op=ALU.subtract), reads=[m1.b, kf.b], writes=[m1.b])
                    range_red(0.0)
                    op("act", lambda e: e.activation(ST[:], m1[:], AF.Sin, scale=rope_c[:, 1:2]),
                       reads=[m1.b, rope_c.b], writes=[ST.b])
                    range_red(0.25)
                    op("act", lambda e: e.activation(CT[:], m1[:], AF.Sin, scale=float(2 * PI)), reads=[m1.b], writes=[CT.b])

                    for (c0, kind, dst, hbase, rope, sc, ck) in groups:
                        direct = first_pass and tile == 0
                        if wcnt[0] >= 30 and wcnt[0] % 2 == 0:
                            if pend_w:
                                pend_w.pop(0)()
                            elif pend_t and puv_in_A[0] < 12:
                                puv_in_A[0] += 1
                                pend_t.pop(0)()
                        w = wsl[wcnt[0] % 2]
                        wcnt[0] += 1
                        if direct:
                            wv = w_in.rearrange("(kc p) c -> p kc c", p=128)
                            for k0 in range(0, KC, 4):
                                dma("pool", lambda e, w=w, k0=k0, c0=c0, wv=wv: e.dma_start(out=w[:, k0:k0 + 4, :], in_=wv[:, k0:k0 + 4, c0:c0 + 512]),
                                    w.b, writes=[w.b])
                            for _ in range(2):
                                if pend_e:
                                    pend_e.pop(0)()
                        else:
                            while pend_e:
                                pend_e.pop(0)()
                            for th in [t_ for t_ in pend_w if t_.key == ck]:
                                pend_w.remove(th)
                                th()
                            wv = w_in_b.rearrange("(kc p) c -> p kc c", p=128)
                            for k0 in range(0, KC, 8):
                                k1_ = min(KC, k0 + 8)
                                dma("sp", lambda e, w=w, k0=k0, k1_=k1_, c0=c0, wv=wv: e.dma_start(out=w[:, k0:k1_, :], in_=wv[:, k0:k1_, c0:c0 + 512]),
                                    w.b, reads=[CB[ck]], writes=[w.b])
                        if kind == "fm":
                            for cb in range(4):
                                head = hbase + cb
                                for ts in range(2):
                                    p = PS[pcnt[0] % 4]
                                    pcnt[0] += 1
                                    for kc in range(KC):
                                        op("pe", lambda e, p=p, w=w, kc=kc, cb=cb, ts=ts: e.matmul(
                                            p[:], w[:, kc, cb * 128:(cb + 1) * 128], xT[:, kc, ts * 512:(ts + 1) * 512],
                                            start=(kc == 0), stop=(kc == KC - 1)),
                                           reads=[w.b, xT.b], writes=[p.b])
                                    sg = stg[scnt[0] % 4]
                                    scnt[0] += 1
                                    if not rope:
                                        op("act", lambda e, sg=sg, p=p, sc=sc: e.activation(sg[:], p[:], AF.Copy, scale=float(sc)),
                                           reads=[p.b], writes=[sg.b])
                                    else:
                                        r = rcnt[0] % 2
                                        rcnt[0] += 1
                                        p2 = PS[4 + r]
                                        op("act", lambda e, r=r, p=p, sc=sc: e.activation(qf[r][:], p[:], AF.Copy, scale=float(sc)),
                                           reads=[p.b], writes=[qf[r].b])
                                        op("pe", lambda e, r=r, p2=p2: e.matmul(p2[:], perm[:], qf[r][:], start=True, stop=True),
                                           reads=[perm.b, qf[r].b], writes=[p2.b])
                                        op("dve", lambda e, r=r, ts=ts: e.tensor_tensor(out=t1[r][:], in0=qf[r][:], in1=CT[:, ts * 512:(ts + 1) * 512], op=ALU.mult),
                                           reads=[qf[r].b, CT.b], writes=[t1[r].b])
                                        op("dve", lambda e, r=r, ts=ts, p2=p2: e.tensor_tensor(out=t2[r][:], in0=p2[:], in1=ST[:, ts * 512:(ts + 1) * 512], op=ALU.mult),
                                           reads=[p2.b, ST.b], writes=[t2[r].b])
                                        op("dve", lambda e, r=r, sg=sg: e.tensor_tensor(out=sg[:], in0=t1[r][:], in1=t2[r][:], op=ALU.add),
                                           reads=[t1[r].b, t2[r].b], writes=[sg.b])
                                    dma("pool", lambda e, sg=sg, dst=dst, head=head, t0=t0, ts=ts: e.dma_start(
                                        out=dst[head, :, t0 + ts * 512:t0 + (ts + 1) * 512], in_=sg[:]),
                                        sg.b, reads=[sg.b])
                        else:
                            for tc in range(8):
                                p = PS[pcnt[0] % 4]
                                pcnt[0] += 1
                                for kc in range(KC):
                                    op("pe", lambda e, p=p, w=w, kc=kc, tc=tc: e.matmul(
                                        p[:], xT[:, kc, tc * 128:(tc + 1) * 128], w[:, kc, :],
                                        start=(kc == 0), stop=(kc == KC - 1)),
                                       reads=[w.b, xT.b], writes=[p.b])
                                sg = stg[scnt[0] % 4]
                                scnt[0] += 1
                                if tc % 2 == 0:
                                    op("act", lambda e, sg=sg, p=p: e.copy(sg[:], p[:]), reads=[p.b], writes=[sg.b])
                                else:
                                    op("dve", lambda e, sg=sg, p=p: e.tensor_copy(sg[:], p[:]), reads=[p.b], writes=[sg.b])
                                dma("pool", lambda e, sg=sg, dst=dst, hbase=hbase, t0=t0, tc=tc: e.dma_start(
                                    out=dst[t0 + tc * 128:t0 + (tc + 1) * 128, hbase:hbase + 512], in_=sg[:]),
                                    sg.b, reads=[sg.b])

            kv_groups = []
            for g in range(SBW // 512):
                kv_groups.append((c_sbk + g * 512, "fm", KsbT, g * 4, False, 1.0, "sbkv"))
            for g in range(SBW // 512):
                kv_groups.append((c_sbv + g * 512, "tm", Vsb, g * 512, False, 1.0, "sbkv"))
            for g in range(DAW // 512):
                kv_groups.append((c_dak + g * 512, "fm", KdaT, g * 4, True, 1.0, "dakv"))
            for g in range(DAW // 512):
                kv_groups.append((c_dav + g * 512, "tm", Vda, g * 512, False, 1.0, "dakv"))
            q_groups = []
            for g in range(SBW // 512):
                q_groups.append((c_sbq + g * 512, "fm", QsbT, g * 4, False, scale, "q"))
            for g in range(DAW // 512):
                q_groups.append((c_daq + g * 512, "fm", QdaT, g * 4, True, scale, "q"))
            phaseA_pass(xT_all, pos_all, S // 1024, kv_groups, first_pass=True)
            phaseA_pass(xT_own, pos_own, SO // 1024, q_groups)
            while pend_w:
                pend_w.pop(0)()
            S_.barrier()
            S_.release([xT.b, posi.b] + [t.b for t in wsl] + [t.b for t in stg])

        NKB = S // 128
        with contextlib.ExitStack() as ph:
            maskS = T(ph, "maskS", [128, 8 * 512], BF16)
            dma("sp", lambda e: e.dma_start(out=maskS[:], in_=c_maskS), maskS.b, writes=[maskS.b])
            kT = [T(ph, f"kT{i}", [128, S], BF16) for i in range(2)]
            vv = [T(ph, f"vv{i}", [128, NKB, 128], BF16) for i in range(2)]
            qq = [T(ph, f"qq{i}", [128, SO], BF16) for i in range(2)]
            NE, NSP, NL, NLW, NW = 3, 6, 4, 3, 4
            e_sb = [T(ph, f"e_sb{i}", [128, 512], F32) for i in range(NE)]
            sp_sb = [T(ph, f"sp_sb{i}", [128, 512], F32) for i in range(NSP)]
            L_sb = [T(ph, f"L_sb{i}", [128, 512], BF16) for i in range(NL)]
            lw_sb = [T(ph, f"lw_sb{i}", [128, 512], F32) for i in range(NLW)]
            w_sb = [T(ph, f"w_sb{i}", [128, 512], BF16) for i in range(NW)]
            Lsum = [T(ph, f"Lsum{i}", [128, 512], BF16) for i in range(2)]
            ostg = [T(ph, f"ostg{i}", [128, 512], BF16) for i in range(2)]
            chains = []
            gi = 0
            for h in range(SBH):
                for i in range(NQ):
                    nkb = 8 * (i + 1)
                    for n, kb in enumerate(range(nkb - 1, -1, -1)):
                        chains.append(dict(h=h, i=i, kb=kb, first=(n == 0), last=(n == nkb - 1), g=gi,
                                           masked=(kb >= 8 * i), r=kb - 8 * i, newhead=(i == 0 and n == 0)))
                    gi += 1
            NCH_B = len(chains)
            ZB = lambda c: PS[c % 4]
            XB = lambda c: PS[4 + c % 2]
            OB = lambda g: PS[6 + g % 2]

            def load_head(h):
                sl = h % 2
                dma("sp", lambda e: e.dma_start(out=kT[sl][:], in_=KsbT[h, :, :]), kT[sl].b, writes=[kT[sl].b])
                vsrc = Vsb.rearrange("(kb s) c -> s kb c", s=128)
                dma("sp", lambda e: e.dma_start(out=vv[sl][:], in_=vsrc[:, :, h * 128:(h + 1) * 128]), vv[sl].b, writes=[vv[sl].b])
                dma("sp", lambda e: e.dma_start(out=qq[sl][:], in_=QsbT[h, :, :]), qq[sl].b, writes=[qq[sl].b])

            def stage(st_, c):
                ch = chains[c]
                h, i, kb, sl = ch["h"], ch["i"], ch["kb"], ch["h"] % 2
                zp, xp, ob = ZB(c), XB(c), OB(ch["g"])
                E, SPt, Lt, LW, W = e_sb[c % NE], sp_sb[c % NSP], L_sb[c % NL], lw_sb[c % NLW], w_sb[c % NW]
                first, last, masked, r = ch["first"], ch["last"], ch["masked"], ch["r"]
                if st_ == 0:
                    if ch["newhead"] and h == 0:
                        load_head(0)
                    op("pe", lambda e: e.matmul(zp[:], kT[sl][:, kb * 128:(kb + 1) * 128], qq[sl][:, i * 512:(i + 1) * 512], start=True, stop=True),
                       reads=[kT[sl].b, qq[sl].b], writes=[zp.b])
                elif st_ == 1:
                    op("act", lambda e: e.activation(E[:], zp[:], AF.Exp, scale=-1.0), reads=[zp.b], writes=[E.b])
                elif st_ == 2:
                    op("act", lambda e: e.activation(SPt[:], E[:], AF.Ln, bias=1.0), reads=[E.b], writes=[SPt.b])
                elif st_ == 3:
                    op("dve", lambda e: e.scalar_tensor_tensor(out=Lt[:], in0=zp[:], scalar=-1.0, in1=SPt[:], op0=ALU.mult, op1=ALU.subtract),
                       reads=[zp.b, SPt.b], writes=[Lt.b])
                    if masked:
                        op("dve", lambda e: e.tensor_tensor(out=Lt[:], in0=Lt[:], in1=maskS[:, r * 512:(r + 1) * 512], op=ALU.mult),
                           reads=[Lt.b, maskS.b], writes=[Lt.b])
                elif st_ == 4:
                    op("pe", lambda e: e.matmul(xp[:], trib[:], Lt[:], start=True, stop=first), reads=[trib.b, Lt.b], writes=[xp.b])
                    if not first:
                        lsp = Lsum[(c - 1) % 2]
                        op("pe", lambda e: e.matmul(xp[:], onesb[:], lsp[:], start=False, stop=True), reads=[onesb.b, lsp.b], writes=[xp.b])
                    if not last:
                        lsn = Lsum[c % 2]
                        if first:
                            op("pool", lambda e: e.tensor_copy(lsn[:], Lt[:]), reads=[Lt.b], writes=[lsn.b])
                        else:
                            lsp = Lsum[(c - 1) % 2]
                            op("pool", lambda e: e.tensor_tensor(out=lsn[:], in0=lsp[:], in1=Lt[:], op=ALU.add),
                               reads=[Lt.b, lsp.b], writes=[lsn.b])
                elif st_ == 5:
                    op("dve", lambda e: e.tensor_tensor(out=LW[:], in0=xp[:], in1=SPt[:], op=ALU.subtract), reads=[xp.b, SPt.b], writes=[LW.b])
                elif st_ == 6:
                    op("act", lambda e: e.activation(W[:], LW[:], AF.Exp), reads=[LW.b], writes=[W.b])
                    if masked:
                        op("pool", lambda e: e.tensor_tensor(out=W[:], in0=W[:], in1=maskS[:, r * 512:(r + 1) * 512], op=ALU.mult),
                           reads=[W.b, maskS.b], writes=[W.b])
                elif st_ == 7:
                    op("pe", lambda e: e.matmul(ob[:], vv[sl][:, kb, :], W[:], start=first, stop=last), reads=[vv[sl].b, W.b], writes=[ob.b])
                elif st_ == 8:
                    if ch["newhead"] and h + 1 < SBH:
                        load_head(h + 1)
                    if last:
                        og = ostg[ch["g"] % 2]
                        op("act", lambda e: e.copy(og[:], ob[:]), reads=[ob.b], writes=[og.b])
                        dma("sp", lambda e: e.dma_start(out=OsbT[h, :, i * 512:(i + 1) * 512], in_=og[:]), og.b, reads=[og.b])

            NST = 9
            for t in range(NCH_B + NST - 1):
                for st_ in (0, 4, 7, 1, 2, 6, 8, 3, 5):
                    c = t - st_
                    if 0 <= c < NCH_B:
                        stage(st_, c)
            S_.barrier()
            S_.release([maskS.b] + [t.b for t in kT + vv + qq + ostg])

        with contextlib.ExitStack() as ph:
            maskD = T(ph, "maskD", [128, 8 * 512], BF16)
            dma("sp", lambda e: e.dma_start(out=maskD[:], in_=c_maskD), maskD.b, writes=[maskD.b])
            gsc = T(ph, "gsc", [128, 2], F32)
            dma("sp", lambda e: e.dma_start(out=gsc[:], in_=sublnT), gsc.b, writes=[gsc.b])
            op("dve", lambda e: e.tensor_scalar(out=gsc[:], in0=gsc[:], scalar1=float(1.0 - lam_init), scalar2=None, op0=ALU.mult),
               reads=[gsc.b], writes=[gsc.b])
            k1 = [T(ph, f"k1{i}", [128, S], BF16) for i in range(2)]
            k2 = [T(ph, f"k2{i}", [128, S], BF16) for i in range(2)]
            vd = [T(ph, f"vd{i}", [128, NKB, 256], BF16) for i in range(2)]
            q1 = [T(ph, f"q1{i}", [128, SO], BF16) for i in range(2)]
            q2 = [T(ph, f"q2{i}", [128, SO], BF16) for i in range(2)]
            E1 = [T(ph, f"E1{i}", [128, 512], BF16) for i in range(2)]
            E2 = [T(ph, f"E2{i}", [128, 512], BF16) for i in range(2)]
            rz1 = T(ph, "rz1", [128, 512], F32)
            rz2 = T(ph, "rz2", [128, 512], F32)
            ta = T(ph, "ta", [128, 512], F32)
            tb = T(ph, "tb", [128, 512], F32)
            od = [T(ph, f"od{i}", [128, 512], F32) for i in range(2)]
            sq = [T(ph, f"sq{i}", [128, 512], F32) for i in range(2)]
            sd = T(ph, "sd", [128, 512], F32)
            rstd = T(ph, "rstd", [128, 512], F32)
            ystg = [T(ph, f"ystg{i}", [128, 512], BF16) for i in range(2)]
            Z1, Z2 = PS[0], PS[1]
            O1a, O1b, Z1s, O2a, O2b, Z2s = PS[2], PS[3], PS[4], PS[5], PS[6], PS[7]
            blk = 0
            def load_da_head(h):
                sl = h % 2
                dma("sp", lambda e: e.dma_start(out=k1[sl][:], in_=KdaT[2 * h, :, :]), k1[sl].b, writes=[k1[sl].b])
                dma("sp", lambda e: e.dma_start(out=k2[sl][:], in_=KdaT[2 * h + 1, :, :]), k2[sl].b, writes=[k2[sl].b])
                vsrc = Vda.rearrange("(kb s) c -> s kb c", s=128)
                dma("sp", lambda e: e.dma_start(out=vd[sl][:], in_=vsrc[:, :, h * 256:(h + 1) * 256]), vd[sl].b, writes=[vd[sl].b])
                dma("sp", lambda e: e.dma_start(out=q1[sl][:], in_=QdaT[2 * h, :, :]), q1[sl].b, writes=[q1[sl].b])
                dma("sp", lambda e: e.dma_start(out=q2[sl][:], in_=QdaT[2 * h + 1, :, :]), q2[sl].b, writes=[q2[sl].b])

            for h in range(DAH):
                sl = h % 2
                if h == 0:
                    load_da_head(0)
                if h + 1 < DAH:
                    load_da_head(h + 1)
                for i in range(NQ):
                    nkb = 8 * (i + 1)

                    def emit_z(kb, which, sl=sl, i=i):
                        zp = Z1 if which == 1 else Z2
                        kk = k1 if which == 1 else k2
                        qx = q1 if which == 1 else q2
                        op("pe", lambda e, zp=zp, kk=kk, qx=qx, kb=kb: e.matmul(
                            zp[:], kk[sl][:, kb * 128:(kb + 1) * 128], qx[sl][:, i * 512:(i + 1) * 512], start=True, stop=True),
                           reads=[kk[sl].b, qx[sl].b], writes=[zp.b])

                    if pend_t:
                        pend_t.pop(0)()
                    emit_z(0, 1)
                    emit_z(0, 2)
                    for kb in range(nkb):
                        par = (blk + kb) % 2
                        first = kb == 0
                        last = kb == nkb - 1
                        masked = kb >= 8 * i
                        r = kb - 8 * i
                        A1, A2 = E1[par], E2[par]
                        op("act", lambda e, A1=A1: e.activation(A1[:], Z1[:], AF.Exp), reads=[Z1.b], writes=[A1.b])
                        op("act", lambda e, A2=A2: e.activation(A2[:], Z2[:], AF.Exp), reads=[Z2.b], writes=[A2.b])
                        if masked:
                            op("dve", lambda e, A1=A1, r=r: e.tensor_tensor(out=A1[:], in0=A1[:], in1=maskD[:, r * 512:(r + 1) * 512], op=ALU.mult),
                               reads=[A1.b, maskD.b], writes=[A1.b])
                            op("dve", lambda e, A2=A2, r=r: e.tensor_tensor(out=A2[:], in0=A2[:], in1=maskD[:, r * 512:(r + 1) * 512], op=ALU.mult),
                               reads=[A2.b, maskD.b], writes=[A2.b])
                        if not last:
                            emit_z(kb + 1, 1)
                        for (pp, lo) in ((O1a, 0), (O1b, 128)):
                            op("pe", lambda e, pp=pp, lo=lo, kb=kb, A1=A1, first=first, last=last, sl=sl: e.matmul(
                                pp[:], vd[sl][:, kb, lo:lo + 128], A1[:], start=first, stop=last),
                               reads=[vd[sl].b, A1.b], writes=[pp.b])
                        op("pe", lambda e, A1=A1, first=first, last=last: e.matmul(Z1s[:], onesb[:], A1[:], start=first, stop=last),
                           reads=[onesb.b, A1.b], writes=[Z1s.b])
                        if not last:
                            emit_z(kb + 1, 2)
                        for (pp, lo) in ((O2a, 0), (O2b, 128)):
                            op("pe", lambda e, pp=pp, lo=lo, kb=kb, A2=A2, first=first, last=last, sl=sl: e.matmul(
                                pp[:], vd[sl][:, kb, lo:lo + 128], A2[:], start=first, stop=last),
                               reads=[vd[sl].b, A2.b], writes=[pp.b])
                        op("pe", lambda e, A2=A2, first=first, last=last: e.matmul(Z2s[:], onesb[:], A2[:], start=first, stop=last),
                           reads=[onesb.b, A2.b], writes=[Z2s.b])
                    blk += nkb
                    op("dve", lambda e: e.reciprocal(rz1[:], Z1s[:]), reads=[Z1s.b], writes=[rz1.b])
                    op("dve", lambda e: e.reciprocal(rz2[:], Z2s[:]), reads=[Z2s.b], writes=[rz2.b])
                    op("dve", lambda e: e.tensor_scalar(out=rz2[:], in0=rz2[:], scalar1=lam[:, 0:1], scalar2=None, op0=ALU.mult),
                       reads=[rz2.b, lam.b], writes=[rz2.b])
                    for hf, (pa, pb) in enumerate(((O1a, O2a), (O1b, O2b))):
                        op("dve", lambda e, pa=pa: e.tensor_tensor(out=ta[:], in0=pa[:], in1=rz1[:], op=ALU.mult),
                           reads=[pa.b, rz1.b], writes=[ta.b])
                        op("dve", lambda e, pb=pb: e.tensor_tensor(out=tb[:], in0=pb[:], in1=rz2[:], op=ALU.mult),
                           reads=[pb.b, rz2.b], writes=[tb.b])
                        op("dve", lambda e, hf=hf: e.tensor_tensor(out=od[hf][:], in0=ta[:], in1=tb[:], op=ALU.subtract),
                           reads=[ta.b, tb.b], writes=[od[hf].b])
                        op("act", lambda e, hf=hf: e.activation(sq[hf][:], od[hf][:], AF.Square), reads=[od[hf].b], writes=[sq[hf].b])
                    op("pe", lambda e: e.matmul(Z1[:], ones[:], sq[0][:], start=True, stop=False), reads=[ones.b, sq[0].b], writes=[Z1.b])
                    op("pe", lambda e: e.matmul(Z1[:], ones[:], sq[1][:], start=False, stop=True), reads=[ones.b, sq[1].b], writes=[Z1.b])
                    op("act", lambda e: e.activation(sd[:], Z1[:], AF.Sqrt, bias=epsc[:, 0:1], scale=1.0 / 256.0),
                       reads=[Z1.b, epsc.b], writes=[sd.b])
                    op("dve", lambda e: e.reciprocal(rstd[:], sd[:]), reads=[sd.b], writes=[rstd.b])
                    for hf in range(2):
                        yg = ystg[hf]
                        op("dve", lambda e, hf=hf, yg=yg: e.scalar_tensor_tensor(out=yg[:], in0=od[hf][:], scalar=gsc[:, hf:hf + 1], in1=rstd[:],
                                                                                 op0=ALU.mult, op1=ALU.mult),
                           reads=[od[hf].b, gsc.b, rstd.b], writes=[yg.b])
                        dma("sp", lambda e, yg=yg, h=h, hf=hf, i=i: e.dma_start(out=OdaT[2 * h + hf, :, i * 512:(i + 1) * 512], in_=yg[:]),
                            yg.b, reads=[yg.b])
            while pend_t:
                pend_t.pop(0)()
            S_.barrier()
            S_.release([maskD.b, gsc.b] + [t.b for t in k1 + k2 + vd + q1 + q2 + ystg])

        with contextlib.ExitStack() as ph:
            xt = T(ph, "xtD", [128, KC, 512], BF16)
            osb = T(ph, "osb", [128, SBH, 512], BF16)
            oda = T(ph, "oda", [128, 2 * DAH, 512], BF16)
            bg = T(ph, "bg", [128, 2 * KC], F32)
            dma("sp", lambda e: e.dma_start(out=bg[:], in_=bgT), bg.b, writes=[bg.b])
            wsb_ = [T(ph, f"wsb{i}", [128, SBH, 256], BF16) for i in range(2)]
            wda_ = [T(ph, f"wda{i}", [128, 2 * DAH, 256], BF16) for i in range(2)]
            wg1_ = [T(ph, f"wg1{i}", [128, KC, 256], BF16) for i in range(2)]
            wg2_ = [T(ph, f"wg2{i}", [128, KC, 256], BF16) for i in range(2)]
            s1 = [T(ph, f"s1{i}", [128, 512], F32) for i in range(2)]
            s2 = [T(ph, f"s2{i}", [128, 512], F32) for i in range(2)]
            mm1 = [T(ph, f"mm1{i}", [128, 512], F32) for i in range(2)]
            mm2 = [T(ph, f"mm2{i}", [128, 512], F32) for i in range(2)]
            mstg = [T(ph, f"mstg{i}", [128, 512], BF16) for i in range(2)]
            gcnt = 0
            ccnt = 0
            for ti in range(NQ):
                t0 = ti * 512
                xv = xT_own.rearrange("(kc p) t -> p kc t", p=128)
                for k0 in range(0, KC, 4):
                    dma("pool", lambda e, k0=k0, t0=t0, xv=xv: e.dma_start(out=xt[:, k0:k0 + 4, :], in_=xv[:, k0:k0 + 4, t0:t0 + 512]),
                        xt.b, writes=[xt.b])
                dma("pool", lambda e, t0=t0: e.dma_start(out=osb[:], in_=OsbT.rearrange("h p t -> p h t")[:, :, t0:t0 + 512]),
                    osb.b, writes=[osb.b])
                dma("pool", lambda e, t0=t0: e.dma_start(out=oda[:], in_=OdaT.rearrange("h p t -> p h t")[:, :, t0:t0 + 512]),
                    oda.b, writes=[oda.b])
                for cg in range(D // 256):
                    sl = gcnt % 2
                    gcnt += 1
                    c0 = cg * 256
                    dma("pool", lambda e, sl=sl, c0=c0: e.dma_start(out=wsb_[sl][:], in_=w_sbb_b.rearrange("(kc p) c -> p kc c", p=128)[:, :, c0:c0 + 256]),
                        wsb_[sl].b, reads=[CB["sbb"]], writes=[wsb_[sl].b])
                    dma("pool", lambda e, sl=sl, c0=c0: e.dma_start(out=wda_[sl][:], in_=w_dab_b.rearrange("(kc p) c -> p kc c", p=128)[:, :, c0:c0 + 256]),
                        wda_[sl].b, reads=[CB["dab"]], writes=[wda_[sl].b])
                    wv = w_in.rearrange("(kc p) c -> p kc c", p=128)
                    for k0 in range(0, KC, 8):
                        k1_ = min(KC, k0 + 8)
                        dma("pool", lambda e, sl=sl, c0=c0, k0=k0, k1_=k1_, wv=wv: e.dma_start(
                            out=wg1_[sl][:, k0:k1_, :], in_=wv[:, k0:k1_, c_g + c0:c_g + c0 + 256]), wg1_[sl].b, writes=[wg1_[sl].b])
                        dma("pool", lambda e, sl=sl, c0=c0, k0=k0, k1_=k1_, wv=wv: e.dma_start(
                            out=wg2_[sl][:, k0:k1_, :], in_=wv[:, k0:k1_, c_g + D + c0:c_g + D + c0 + 256]), wg2_[sl].b, writes=[wg2_[sl].b])
                    for cc in range(2):
                        c = cg * 2 + cc
                        par = ccnt % 2
                        ccnt += 1
                        pb1, pb2, pg1, pg2 = PS[4 * par], PS[4 * par + 1], PS[4 * par + 2], PS[4 * par + 3]
                        for kc in range(SBH):
                            op("pe", lambda e, pb1=pb1, sl=sl, kc=kc, cc=cc: e.matmul(
                                pb1[:], wsb_[sl][:, kc, cc * 128:(cc + 1) * 128], osb[:, kc, :], start=(kc == 0), stop=(kc == SBH - 1)),
                               reads=[wsb_[sl].b, osb.b], writes=[pb1.b])
                        for kc in range(2 * DAH):
                            op("pe", lambda e, pb2=pb2, sl=sl, kc=kc, cc=cc: e.matmul(
                                pb2[:], wda_[sl][:, kc, cc * 128:(cc + 1) * 128], oda[:, kc, :], start=(kc == 0), stop=(kc == 2 * DAH - 1)),
                               reads=[wda_[sl].b, oda.b], writes=[pb2.b])
                        for kc in range(KC):
                            op("pe", lambda e, pg1=pg1, sl=sl, kc=kc, cc=cc: e.matmul(
                                pg1[:], wg1_[sl][:, kc, cc * 128:(cc + 1) * 128], xt[:, kc, :], start=(kc == 0), stop=(kc == KC - 1)),
                               reads=[wg1_[sl].b, xt.b], writes=[pg1.b])
                        for kc in range(KC):
                            op("pe", lambda e, pg2=pg2, sl=sl, kc=kc, cc=cc: e.matmul(
                                pg2[:], wg2_[sl][:, kc, cc * 128:(cc + 1) * 128], xt[:, kc, :], start=(kc == 0), stop=(kc == KC - 1)),
                               reads=[wg2_[sl].b, xt.b], writes=[pg2.b])
                        op("act", lambda e, par=par, pg1=pg1, c=c: e.activation(s1[par][:], pg1[:], AF.Sigmoid, bias=bg[:, c:c + 1]),
                           reads=[pg1.b, bg.b], writes=[s1[par].b])
                        op("act", lambda e, par=par, pg2=pg2, c=c: e.activation(s2[par][:], pg2[:], AF.Sigmoid, bias=bg[:, KC + c:KC + c + 1]),
                           reads=[pg2.b, bg.b], writes=[s2[par].b])
                        op("dve", lambda e, par=par, pb1=pb1: e.tensor_tensor(out=mm1[par][:], in0=pb1[:], in1=s1[par][:], op=ALU.mult),
                           reads=[pb1.b, s1[par].b], writes=[mm1[par].b])
                        op("dve", lambda e, par=par, pb2=pb2: e.tensor_tensor(out=mm2[par][:], in0=pb2[:], in1=s2[par][:], op=ALU.mult),
                           reads=[pb2.b, s2[par].b], writes=[mm2[par].b])
                        op("dve", lambda e, par=par: e.tensor_tensor(out=mstg[par][:], in0=mm1[par][:], in1=mm2[par][:], op=ALU.add),
                           reads=[mm1[par].b, mm2[par].b], writes=[mstg[par].b])
                        dma("sp", lambda e, par=par, c=c, t0=t0: e.dma_start(out=mergedT[c, :, t0:t0 + 512], in_=mstg[par][:]),
                            mstg[par].b, reads=[mstg[par].b])
            S_.barrier()
            S_.release([xt.b, osb.b, oda.b, bg.b] + [t.b for t in wsb_ + wda_ + wg1_ + wg2_ + mstg])

        def layer_norm(xr, G, Bt, junk, stat):
            op("dve", lambda e: e.tensor_reduce(out=stat[:, 0:1], in_=xr[:], axis=AX.X, op=ALU.add), reads=[xr.b], writes=[stat.b])
            op("dve", lambda e: e.tensor_scalar(out=stat[:, 0:1], in0=stat[:, 0:1], scalar1=float(-1.0 / D), scalar2=None, op0=ALU.mult),
               reads=[stat.b], writes=[stat.b])
            op("dve", lambda e: e.tensor_scalar(out=xr[:], in0=xr[:], scalar1=stat[:, 0:1], scalar2=None, op0=ALU.add),
               reads=[xr.b, stat.b], writes=[xr.b])
            op("dve", lambda e: e.memset(stat[:, 1:2], 0.0), writes=[stat.b])
            op("dve", lambda e: e.scalar_tensor_tensor(out=junk[:], in0=xr[:], scalar=1.0, in1=xr[:], op0=ALU.mult, op1=ALU.mult,
                                                       accum_out=stat[:, 1:2]),
               reads=[xr.b, stat.b], writes=[junk.b, stat.b])
            op("act", lambda e: e.activation(stat[:, 2:3], stat[:, 1:2], AF.Sqrt, bias=epsc[:, 0:1], scale=float(1.0 / D)),
               reads=[stat.b, epsc.b], writes=[stat.b])
            op("dve", lambda e: e.reciprocal(stat[:, 3:4], stat[:, 2:3]), reads=[stat.b], writes=[stat.b])
            op("dve", lambda e: e.scalar_tensor_tensor(out=xr[:], in0=xr[:], scalar=stat[:, 3:4], in1=G[:], op0=ALU.mult, op1=ALU.mult),
               reads=[xr.b, stat.b, G.b], writes=[xr.b])
            op("dve", lambda e: e.tensor_tensor(out=xr[:], in0=xr[:], in1=Bt[:], op=ALU.add), reads=[xr.b, Bt.b], writes=[xr.b])

        with contextlib.ExitStack() as ph:
            mT = T(ph, "mT", [128, KC, 256], BF16)
            xr = [T(ph, f"xr{i}", [128, D], F32) for i in range(2)]
            wo = [T(ph, f"wo{i}", [128, KC, 512], BF16) for i in range(2)]
            G1 = T(ph, "G1", [128, D], F32)
            B1 = T(ph, "B1", [128, D], F32)
            junk = T(ph, "junkD", [128, D], F32)
            stat = T(ph, "statD", [128, 4], F32)
            hT = [T(ph, f"hT{i}", [128, KC, 128], BF16) for i in range(2)]
            dma("sp", lambda e: e.dma_start(out=G1[:], in_=ln1g[0:1, :].partition_broadcast(128)), G1.b, writes=[G1.b])
            dma("sp", lambda e: e.dma_start(out=B1[:], in_=ln1b[0:1, :].partition_broadcast(128)), B1.b, writes=[B1.b])
            wc = 0
            pc = 0
            hc_ = 0
            for ti in range(SO // 256):
                t0 = ti * 256
                dma("sp", lambda e, t0=t0: e.dma_start(out=mT[:], in_=mergedT.rearrange("c p t -> p c t")[:, :, t0:t0 + 256]),
                    mT.b, writes=[mT.b])
                for tc in range(2):
                    dma("sp", lambda e, tc=tc, t0=t0: e.dma_start(out=xr[tc][:], in_=x_own[t0 + tc * 128:t0 + (tc + 1) * 128, :]),
                        xr[tc].b, writes=[xr[tc].b])
                for cg in range(D // 512):
                    w = wo[wc % 2]
                    wc += 1
                    wv = w_out_b.rearrange("(kc p) c -> p kc c", p=128)
                    for k0 in range(0, KC, 8):
                        k1_ = min(KC, k0 + 8)
                        dma("sp", lambda e, w=w, k0=k0, k1_=k1_, cg=cg, wv=wv: e.dma_start(out=w[:, k0:k1_, :], in_=wv[:, k0:k1_, cg * 512:(cg + 1) * 512]),
                            w.b, reads=[CB["out"]], writes=[w.b])
                    for tc in range(2):
                        p = PS[pc % 4]
                        pc += 1
                        for kc in range(KC):
                            op("pe", lambda e, p=p, w=w, kc=kc, tc=tc: e.matmul(p[:], mT[:, kc, tc * 128:(tc + 1) * 128], w[:, kc, :],
                                                                              start=(kc == 0), stop=(kc == KC - 1)),
                               reads=[mT.b, w.b], writes=[p.b])
                        op("dve", lambda e, p=p, tc=tc, cg=cg: e.scalar_tensor_tensor(
                            out=xr[tc][:, cg * 512:(cg + 1) * 512], in0=xr[tc][:, cg * 512:(cg + 1) * 512], scalar=float(alpha), in1=p[:],
                            op0=ALU.mult, op1=ALU.add), reads=[xr[tc].b, p.b], writes=[xr[tc].b])
                for tc in range(2):
                    layer_norm(xr[tc], G1, B1, junk, stat)
                    r0 = t0 + tc * 128
                    dma("pool", lambda e, tc=tc, r0=r0: e.dma_start(out=h1_scr[r0:r0 + 128, :], in_=xr[tc][:]), xr[tc].b, reads=[xr[tc].b])
                    ht = hT[hc_ % 2]
                    hc_ += 1
                    for k0 in range(0, KC, 4):
                        p = PS[4 + (pc % 4)]
                        pc += 1
                        for kk in range(4):
                            kc = k0 + kk
                            op("pe", lambda e, p=p, kk=kk, kc=kc, tc=tc: e.transpose(p[:, kk * 128:(kk + 1) * 128], xr[tc][:, kc * 128:(kc + 1) * 128], ident[:]),
                               reads=[xr[tc].b, ident.b], writes=[p.b])
                        op("act", lambda e, p=p, ht=ht, k0=k0: e.copy(ht[:, k0:k0 + 4, :], p[:].rearrange("p (a t) -> p a t", a=4)),
                           reads=[p.b], writes=[ht.b])
                    dma("pool", lambda e, ht=ht, r0=r0: e.dma_start(out=h1T_scr.rearrange("c p t -> p c t")[:, :, r0:r0 + 128], in_=ht[:]),
                        ht.b, reads=[ht.b])
            S_.barrier()
            S_.release([mT.b, G1.b, B1.b] + [t.b for t in xr + wo + hT])

        with contextlib.ExitStack() as ph:
            sk = T(ph, "sk", [128, HC, NKEYS], F32)
            dma("sp", lambda e: e.dma_start(out=sk[:], in_=skT.rearrange("h p k -> p h k")), sk.b, writes=[sk.b])
            hTt = T(ph, "hTt", [128, KC, 512], BF16)
            wq = [T(ph, f"wq{i}", [128, KC, 512], BF16) for i in range(2)]
            qpT = T(ph, "qpT", [128, HC, 512], F32)
            scst = [T(ph, f"scst{i}", [128, HC, NKEYS], F32) for i in range(2)]
            wc = 0
            pc = 0
            scn = 0
            for ti in range(NQ):
                t0 = ti * 512
                dma("sp", lambda e, t0=t0: e.dma_start(out=hTt[:], in_=h1T_scr.rearrange("c p t -> p c t")[:, :, t0:t0 + 512]),
                    hTt.b, writes=[hTt.b])
                for g in range(QW // 512):
                    w = wq[wc % 2]
                    wc += 1
                    wv = w_q_b.rearrange("(kc p) c -> p kc c", p=128)
                    for k0 in range(0, KC, 8):
                        k1_ = min(KC, k0 + 8)
                        dma("sp", lambda e, w=w, k0=k0, k1_=k1_, g=g, wv=wv: e.dma_start(out=w[:, k0:k1_, :], in_=wv[:, k0:k1_, g * 512:(g + 1) * 512]),
                            w.b, reads=[CB["wq"]], writes=[w.b])
                    for cb in range(4):
                        hcx = g * 4 + cb
                        p = PS[pc % 4]
                        pc += 1
                        for kc in range(KC):
                            op("pe", lambda e, p=p, w=w, kc=kc, cb=cb: e.matmul(p[:], w[:, kc, cb * 128:(cb + 1) * 128], hTt[:, kc, :],
                                                                              start=(kc == 0), stop=(kc == KC - 1)),
                               reads=[w.b, hTt.b], writes=[p.b])
                        op("act", lambda e, p=p, hcx=hcx: e.copy(qpT[:, hcx, :], p[:]), reads=[p.b], writes=[qpT.b])
                for tc in range(4):
                    r0 = t0 + tc * 128
                    sct = scst[scn % 2]
                    scn += 1
                    for g4 in range(0, HC, 4):
                        p = PS[4 + (pc % 4)]
                        pc += 1
                        for kk in range(4):
                            hcx = g4 + kk
                            op("pe", lambda e, p=p, kk=kk, hcx=hcx, tc=tc: e.matmul(p[:, kk * 128:(kk + 1) * 128], qpT[:, hcx, tc * 128:(tc + 1) * 128],
                                                                                  sk[:, hcx, :], start=True, stop=True),
                               reads=[qpT.b, sk.b], writes=[p.b])
                        op("act", lambda e, p=p, g4=g4, sct=sct: e.copy(sct[:, g4:g4 + 4, :], p[:].rearrange("p (a t) -> p a t", a=4)),
                           reads=[p.b], writes=[sct.b])
                    dma("pool", lambda e, sct=sct, r0=r0: e.dma_start(out=sc_scr[r0:r0 + 128, :], in_=sct[:].rearrange("p h k -> p (h k)")),
                        sct.b, reads=[sct.b])
            S_.barrier()
            S_.release([sk.b, hTt.b] + [t.b for t in wq + scst])

        with contextlib.ExitStack() as ph:
            G2 = T(ph, "G2", [128, D], F32)
            B2 = T(ph, "B2", [128, D], F32)
            dma("sp", lambda e: e.dma_start(out=G2[:], in_=ln2g[0:1, :].partition_broadcast(128)), G2.b, writes=[G2.b])
            dma("sp", lambda e: e.dma_start(out=B2[:], in_=ln2b[0:1, :].partition_broadcast(128)), B2.b, writes=[B2.b])
            identb = T(ph, "identb", [128, 128], BF16)
            dma("pool", lambda e: e.dma_start(out=identb[:], in_=c_ident), identb.b, writes=[identb.b])
            sc = T(ph, "sc", [128, HC, NKEYS], F32)
            top = T(ph, "top", [128, PH, 2, TOPK], F32)
            topi = T(ph, "topi", [128, PH, 2, TOPK], U32)
            topf = T(ph, "topf", [128, PH, 2, TOPK], F32)
            cand = T(ph, "cand", [128, PH, TOPK, TOPK], F32)
            cidx = T(ph, "cidx", [128, PH, TOPK, TOPK], F32)
            cwork = T(ph, "cwork", [128, PH, TOPK * TOPK], F32)
            best = T(ph, "best", [128, PH, TOPK], F32)
            junk2 = T(ph, "junk2", [128, TOPK * TOPK], F32)
            idxf = T(ph, "idxf", [128, NSEL], F32)
            dd = T(ph, "dd", [128, PH, TOPK], F32)
            eb = T(ph, "eb", [128, PH, TOPK], F32)
            zs = T(ph, "zs", [128, PH], F32)
            idxi_ = [T(ph, f"idxi{i}", [128, NSEL], I32) for i in range(2)]
            gate_ = [T(ph, f"gate{i}", [128, PH, TOPK], F32) for i in range(2)]
            hb = [T(ph, f"hb{i}", [128, D], BF16) for i in range(2)]
            acc = T(ph, "acc", [128, D], F32)
            junkb = T(ph, "junkb", [128, D], BF16)
            stat = T(ph, "statE", [128, 4], F32)
            NRG = 5
            ring = [T(ph, f"ring{i}", [128, 2 * D], BF16) for i in range(NRG)]
            dgs = [T(ph, f"dgs{i}", [128, 128], BF16) for i in range(4)]
            gel = [T(ph, f"gel{i}", [128, 2], F32) for i in range(4)]
            NCH = SO // 128
            NDG = D // 512
            cnt = dict(r=0, d=0)

            def e2_s1(c):
                par = c % 2
                r0 = c * 128
                idxi, gate = idxi_[par], gate_[par]
                workv = cwork[:].rearrange("p h (c k) -> p (h c) k", c=2)
                dma("pool", lambda e: e.dma_start(out=hb[par][:], in_=h1_scr[r0:r0 + 128, :]), hb[par].b, writes=[hb[par].b])
                dma("sp", lambda e: e.dma_start(out=sc[:].rearrange("p h k -> p (h k)"), in_=sc_scr[r0:r0 + 128, :]), sc.b, writes=[sc.b])
                for hcx in range(HC):
                    hh, c2 = hcx // 2, hcx % 2
                    op("dve", lambda e, hh=hh, c2=c2, hcx=hcx: e.max(out=top[:, hh, c2, 0:8], in_=sc[:, hcx, :]),
                       reads=[sc.b], writes=[top.b])
                    op("dve", lambda e, hh=hh, c2=c2, hcx=hcx: e.max_index(out=topi[:, hh, c2, 0:8], in_max=top[:, hh, c2, 0:8], in_values=sc[:, hcx, :]),
                       reads=[sc.b, top.b], writes=[topi.b])
                    op("dve", lambda e, hh=hh, c2=c2, hcx=hcx: e.match_replace(out=workv[:, hcx, :], in_to_replace=top[:, hh, c2, 0:8],
                                                                               in_values=sc[:, hcx, :], imm_value=-1e30),
                       reads=[sc.b, top.b], writes=[cwork.b])
                    op("dve", lambda e, hh=hh, c2=c2, hcx=hcx: e.max(out=top[:, hh, c2, 8:16], in_=workv[:, hcx, :]),
                       reads=[cwork.b], writes=[top.b])
                    op("dve", lambda e, hh=hh, c2=c2, hcx=hcx: e.max_index(out=topi[:, hh, c2, 8:16], in_max=top[:, hh, c2, 8:16], in_values=workv[:, hcx, :]),
                       reads=[cwork.b, top.b], writes=[topi.b])
                op("dve", lambda e: e.tensor_copy(topf[:], topi[:]), reads=[topi.b], writes=[topf.b])
                bshape = [128, TOPK, TOPK]
                for hh in range(PH):
                    op("dve", lambda e, hh=hh: e.tensor_tensor(out=cand[:, hh, :, :], in0=top[:, hh, 0, :].unsqueeze(2).to_broadcast(bshape),
                                                               in1=top[:, hh, 1, :].unsqueeze(1).to_broadcast(bshape), op=ALU.add),
                       reads=[top.b], writes=[cand.b])
                    op("dve", lambda e, hh=hh: e.scalar_tensor_tensor(out=cidx[:, hh, :, :], in0=topf[:, hh, 0, :].unsqueeze(2).to_broadcast(bshape),
                                                                      scalar=float(NKEYS), in1=topf[:, hh, 1, :].unsqueeze(1).to_broadcast(bshape),
                                                                      op0=ALU.mult, op1=ALU.add),
                       reads=[topf.b], writes=[cidx.b])
                for hh in range(PH):
                    cv = cand[:, hh, :, :].rearrange("p a b -> p (a b)")
                    op("dve", lambda e, hh=hh, cv=cv: e.max(out=best[:, hh, 0:8], in_=cv), reads=[cand.b], writes=[best.b])
                    op("dve", lambda e, hh=hh, cv=cv: e.match_replace(out=cwork[:, hh, :], in_to_replace=best[:, hh, 0:8], in_values=cv, imm_value=-1e30),
                       reads=[cand.b, best.b], writes=[cwork.b])
                    op("dve", lambda e, hh=hh: e.max(out=best[:, hh, 8:16], in_=cwork[:, hh, :]), reads=[cwork.b], writes=[best.b])
                for hh in range(PH):
                    cv = cand[:, hh, :, :].rearrange("p a b -> p (a b)")
                    iv = cidx[:, hh, :, :].rearrange("p a b -> p (a b)")
                    for k in range(TOPK):
                        n = hh * TOPK + k
                        op("dve", lambda e, hh=hh, k=k, n=n, cv=cv, iv=iv: e.scalar_tensor_tensor(
                            out=junk2[:], in0=cv, scalar=best[:, hh, k:k + 1], in1=iv, op0=ALU.is_equal, op1=ALU.mult),
                           reads=[cand.b, cidx.b, best.b], writes=[junk2.b])
                        op("dve", lambda e, n=n: e.tensor_reduce(out=idxf[:, n:n + 1], in_=junk2[:], axis=AX.X, op=ALU.max),
                           reads=[junk2.b, idxf.b], writes=[idxf.b])
                op("dve", lambda e: e.tensor_copy(idxi[:], idxf[:]), reads=[idxf.b], writes=[idxi.b])
                op("dve", lambda e: e.tensor_tensor(out=dd[:], in0=best[:], in1=best[:, :, 0:1].to_broadcast([128, PH, TOPK]), op=ALU.subtract),
                   reads=[best.b], writes=[dd.b])
                op("act", lambda e: e.activation(eb[:], dd[:], AF.Exp), reads=[dd.b], writes=[eb.b])
                op("dve", lambda e: e.tensor_reduce(out=zs[:], in_=eb[:], axis=AX.X, op=ALU.add), reads=[eb.b], writes=[zs.b])
                op("dve", lambda e: e.reciprocal(zs[:], zs[:]), reads=[zs.b], writes=[zs.b])
                op("dve", lambda e: e.tensor_tensor(out=gate[:], in0=eb[:], in1=zs[:].unsqueeze(2).to_broadcast([128, PH, TOPK]), op=ALU.mult),
                   reads=[eb.b, zs.b], writes=[gate.b])

            def e2_n(c, n):
                par = c % 2
                rg = ring[cnt["r"] % NRG]
                cnt["r"] += 1
                dg = dgs[cnt["d"] % 4]
                ge = gel[cnt["d"] % 4]
                cnt["d"] += 1
                idxi = idxi_[par]
                gflat = gate_[par][:].rearrange("p h k -> p (h k)")
                dma("pool", lambda e: e.indirect_dma_start(
                    out=rg[:], out_offset=None, in_=puv_b, in_offset=bass.IndirectOffsetOnAxis(ap=idxi[:, n:n + 1], axis=0)),
                    rg.b, reads=[idxi.b, CB["puv"]], writes=[rg.b])
                op("dve", lambda e: e.memset(ge[:, 0:1], 0.0), writes=[ge.b])
                op("dve", lambda e: e.scalar_tensor_tensor(out=rg[:, 0:D], in0=rg[:, 0:D], scalar=1.0, in1=hb[par][:], op0=ALU.mult, op1=ALU.mult,
                                                           accum_out=ge[:, 0:1]),
                   reads=[rg.b, hb[par].b, ge.b], writes=[rg.b, ge.b])
                op("act", lambda e: e.activation(ge[:, 1:2], ge[:, 0:1], AF.Gelu), reads=[ge.b], writes=[ge.b])
                op("act", lambda e: e.activation(ge[:, 1:2], ge[:, 1:2], AF.Copy, scale=gflat[:, n:n + 1]),
                   reads=[ge.b, gate_[par].b], writes=[ge.b])
                op("act", lambda e: e.activation(dg[:], identb[:], AF.Copy, scale=ge[:, 1:2]),
                   reads=[identb.b, ge.b], writes=[dg.b])
                for g in range(NDG):
                    op("pe", lambda e, g=g: e.matmul(PS[g][:], dg[:], rg[:, D + g * 512:D + (g + 1) * 512], start=(n == 0), stop=(n == NSEL - 1)),
                       reads=[dg.b, rg.b], writes=[PS[g].b])

            def e2_post(c):
                r0 = c * 128
                dma("sp", lambda e: e.dma_start(out=acc[:], in_=h1_scr[r0:r0 + 128, :]), acc.b, writes=[acc.b])
                for g in range(NDG):
                    op("dve", lambda e, g=g: e.scalar_tensor_tensor(out=acc[:, g * 512:(g + 1) * 512], in0=acc[:, g * 512:(g + 1) * 512],
                                                                    scalar=float(alpha), in1=PS[g][:], op0=ALU.mult, op1=ALU.add),
                       reads=[acc.b, PS[g].b], writes=[acc.b])
                layer_norm(acc, G2, B2, junkb, stat)
                dma("sp", lambda e: e.dma_start(out=out_own[r0:r0 + 128, :], in_=acc[:]), acc.b, reads=[acc.b])

            e2_s1(0)
            for c in range(NCH):
                for n in range(NSEL):
                    e2_n(c, n)
                    if n == NSEL // 2 and c + 1 < NCH:
                        e2_s1(c + 1)
                e2_post(c)
            S_.barrier(include_bg=True)

        S_.emit()
    return nc


def make_consts(j):
    ident = np.eye(128, dtype=np.float32)
    jj = np.arange(128)[:, None]
    ss = np.arange(128)[None, :]
    tri = (jj > ss).astype(np.float32)
    ones = np.ones((128, 128), np.float32)
    perm = np.zeros((128, 128), np.float32)
    for m in range(128):
        perm[(m + 64) % 128, m] = 1.0
    inv_freq = (ROPE_THETA ** (-np.arange(0, 128, 2, dtype=np.float32) / np.float32(128))).astype(np.float32)
    rope = np.zeros((128, 3), np.float32)
    rope[:, 0] = np.concatenate([inv_freq, inv_freq])
    rope[:64, 1] = -2.0 * math.pi
    rope[64:, 1] = 2.0 * math.pi
    s = np.arange(128)[:, None, None]
    r = np.arange(8)[None, :, None]
    t = np.arange(512)[None, None, :]
    kpos = r * 128 + s
    qpos = j * 512 + t
    maskS = (kpos < qpos).astype(np.float32).reshape(128, 8 * 512).astype(ml_dtypes.bfloat16)
    maskD = (kpos <= qpos).astype(np.float32).reshape(128, 8 * 512).astype(ml_dtypes.bfloat16)
    return dict(c_ident=ident, c_tri=tri, c_ones=ones, c_perm=perm, c_rope=rope, c_maskS=maskS, c_maskD=maskD)


def prep_inputs(inp, cfg):
    D, S, NB, PH = cfg["D"], cfg["S"], cfg["NB"], cfg["PH"]
    KC = D // 128
    f = lambda a: np.ascontiguousarray(np.asarray(a))
    x = f(inp["x"])
    pos = f(inp["positions"]).astype(np.int32)
    shared = dict(
        w_in=f(inp["w_in"][0]),
        bgT=f(np.asarray(inp["b_gate"][0]).reshape(2 * KC, 128).T),
        lamv=f(np.stack([np.asarray(inp["lambda_q1"][0]), np.asarray(inp["lambda_k1"][0]),
                         np.asarray(inp["lambda_q2"][0]), np.asarray(inp["lambda_k2"][0])])),
        sublnT=f(np.asarray(inp["subln_g"][0]).reshape(2, 128).T),
        w_sbb=f(inp["w_sb_branch"][0]), w_dab=f(inp["w_da_branch"][0]), w_out=f(inp["w_out"][0]),
        ln1g=f(np.asarray(inp["ln1_g"][0])[None, :]), ln1b=f(np.asarray(inp["ln1_b"][0])[None, :]),
        ln2g=f(np.asarray(inp["ln2_g"][0])[None, :]), ln2b=f(np.asarray(inp["ln2_b"][0])[None, :]),
        w_q=f(inp["peer_w_q"][0]),
        skT=f(np.asarray(inp["peer_sub_keys"][0]).reshape(2 * PH, NKEYS, 128).transpose(0, 2, 1)),
        pu=f(inp["peer_u"][0]), pv=f(inp["peer_v"][0]),
    )
    consts = [make_consts(0), make_consts(1)]
    in_maps = []
    for c in range(2 * NB):
        b, j = c // 2, c % 2
        tiles = [2 * i + j for i in range(S // 1024)]
        rows = np.concatenate([np.arange(t * 512, (t + 1) * 512) for t in tiles])
        xb = x[b]
        m = dict(shared)
        m.update(consts[j])
        m["xT_all"] = f(xb.T)
        m["x_own"] = f(xb[rows])
        m["xT_own"] = f(xb[rows].T)
        m["pos_all"] = f(pos[b][None, :])
        m["pos_own"] = f(pos[b][rows][None, :])
        in_maps.append(m)
    return in_maps


def assemble(results, cfg):
    D, S, NB = cfg["D"], cfg["S"], cfg["NB"]
    out = np.zeros((NB, S, D), np.float32)
    for c in range(2 * NB):
        b, j = c // 2, c % 2
        tiles = [2 * i + j for i in range(S // 1024)]
        rows = np.concatenate([np.arange(t * 512, (t + 1) * 512) for t in tiles])
        out[b, rows] = results[c]["out_own"]
    return out


def run(inputs, cfg):
    nc = build(cfg)
    in_maps = prep_inputs(inputs, cfg)
    res = run_bass_kernel_spmd(nc, in_maps, core_ids=list(range(2 * cfg["NB"])))
    return assemble(res.results, cfg)


def kernel(**inputs):
    return run(inputs, FULL_CFG)
```
